# Optimizing a Trainium2 kernel written in Bass

```python
import math
import jax, jax.numpy as jnp
from jax import lax
import numpy as np

D_MODEL = 1024
BATCH = 4
SEQ = 8192
DEPTH = 1

CHUNK = 64
Q_BLOCK = 128
FOX_HEADS = 8
FOX_HEAD_DIM = 64
FOX_WIDTH = FOX_HEADS * FOX_HEAD_DIM
RWKV_HEADS = 8
RWKV_HEAD_DIM = 64
RWKV_WIDTH = RWKV_HEADS * RWKV_HEAD_DIM
W_LORA = 64
A_LORA = 64
G_LORA = 128
GN_EPS = 64e-5
PEER_HEADS = 8
PEER_N_KEYS = 128
PEER_N_EXPERTS = PEER_N_KEYS * PEER_N_KEYS
PEER_HALF = 128
PEER_TOPK = 16
PEER_TOKEN_BLOCK = 128
DN_ALPHA = (2 * DEPTH) ** 0.25
DN_BETA = (8 * DEPTH) ** -0.25
LN_EPS = 1e-5
FOX_SPLITS = (FOX_WIDTH, FOX_WIDTH, FOX_WIDTH, FOX_HEADS)
RWKV_SPLITS = (RWKV_WIDTH, RWKV_WIDTH, RWKV_WIDTH, W_LORA, A_LORA, G_LORA)
FOX_COLS = 3 * FOX_WIDTH + FOX_HEADS
RWKV_COLS = 3 * RWKV_WIDTH + W_LORA + A_LORA + G_LORA
IN_WIDTH = FOX_COLS + RWKV_COLS + 2 * D_MODEL

kernel_name = "fox_rwkv7_peer_deepnorm_hybrid"


def _split(t, sizes):
    out = []
    start = 0
    for n in sizes:
        out.append(t[..., start:start + n])
        start += n
    return out


def _layer_norm(x, g, b):
    xf = x.astype(jnp.float32)
    mu = jnp.mean(xf, axis=-1, keepdims=True)
    var = jnp.mean(jnp.square(xf - mu), axis=-1, keepdims=True)
    return ((xf - mu) * lax.rsqrt(var + LN_EPS) * g + b).astype(x.dtype)


def _token_shift(p, mu):
    prev = jnp.pad(p, ((0, 0), (1, 0), (0, 0)))[:, :-1]
    return p + mu * (prev - p)


def _forgetting_attention(q, k, v, f_logit):
    B, S = q.shape[0], q.shape[1]
    heads = lambda t: t.reshape(B, S, FOX_HEADS, FOX_HEAD_DIM).transpose(0, 2, 1, 3)
    q, k, v = heads(q), heads(k), heads(v)
    log_f = jax.nn.log_sigmoid(f_logit.astype(jnp.float32))
    c = jnp.cumsum(log_f, axis=1).transpose(0, 2, 1)
    scale = FOX_HEAD_DIM ** -0.5
    outs = []
    for blk in range(S // Q_BLOCK):
        q0 = blk * Q_BLOCK
        q1 = q0 + Q_BLOCK
        s = jnp.einsum('bhqd,bhkd->bhqk', q[:, :, q0:q1], k[:, :, :q1]).astype(jnp.float32) * scale
        s = s + c[:, :, q0:q1, None] - c[:, :, None, :q1]
        causal = jnp.arange(q0, q1)[:, None] >= jnp.arange(q1)[None, :]
        s = jnp.where(causal, s, -jnp.inf)
        p = jax.nn.softmax(s, axis=-1).astype(v.dtype)
        outs.append(jnp.einsum('bhqk,bhkd->bhqd', p, v[:, :, :q1]))
    o = jnp.concatenate(outs, axis=2)
    return o.transpose(0, 2, 1, 3).reshape(B, S, FOX_WIDTH)


def _rwkv7_time_mix(r, k, v, w_lora, a_lora, g_lora, w0, w2, a0, a2, g2, k_k, k_a, r_k, ln_g, ln_b):
    B, S = r.shape[0], r.shape[1]
    f32 = jnp.float32
    H, N = RWKV_HEADS, RWKV_HEAD_DIM
    w_pre = (w0 + jnp.tanh(w_lora) @ w2).astype(f32)
    decay = jnp.exp(-jnp.exp(-jax.nn.softplus(-w_pre) - 0.5))
    a = jax.nn.sigmoid((a0 + a_lora @ a2).astype(f32))
    g = jax.nn.sigmoid(g_lora) @ g2
    heads = lambda t: t.astype(f32).reshape(B, S, H, N)
    kk = heads(k * k_k)
    kk = kk / jnp.maximum(jnp.sqrt(jnp.sum(kk * kk, axis=-1, keepdims=True)), 1e-12)
    k_mod = k.astype(f32) * (1.0 + (a - 1.0) * k_a.astype(f32))
    r_h, k_h, v_h, w_h, a_h = heads(r), heads(k_mod), heads(v), heads(decay), heads(a)

    def to_chunks(t):
        return t.transpose(1, 0, 2, 3).reshape(S // CHUNK, CHUNK, B, H, N)

    def frame_step(state, inp):
        r_t, w_t, k_t, v_t, kk_t, a_t = inp
        sa = jnp.einsum('bhij,bhj->bhi', state, -kk_t)
        state = (state * w_t[:, :, None, :] + sa[..., None] * (kk_t * a_t)[:, :, None, :]
                 + v_t[..., None] * k_t[:, :, None, :])
        return state, jnp.einsum('bhij,bhj->bhi', state, r_t)

    def chunk_step(state, chunk_inp):
        return lax.scan(frame_step, state, chunk_inp)

    state0 = jnp.zeros((B, H, N, N), f32)
    xs = (to_chunks(r_h), to_chunks(w_h), to_chunks(k_h), to_chunks(v_h), to_chunks(kk), to_chunks(a_h))
    _, y = lax.scan(chunk_step, state0, xs)
    y = y.reshape(S, B, H, N).transpose(1, 0, 2, 3)
    mu = jnp.mean(y, axis=-1, keepdims=True)
    var = jnp.mean(jnp.square(y - mu), axis=-1, keepdims=True)
    y = ((y - mu) * lax.rsqrt(var + GN_EPS)).reshape(B, S, RWKV_WIDTH) * ln_g + ln_b
    bonus = jnp.sum(r_h * k_h * r_k.astype(f32).reshape(H, N), axis=-1, keepdims=True) * v_h
    y = (y + bonus.reshape(B, S, RWKV_WIDTH)) * g
    return y.astype(r.dtype)


def _peer(x, w_q, sub_keys, u_table, v_table):
    B, S, D = x.shape
    xt = x.reshape((B * S) // PEER_TOKEN_BLOCK, PEER_TOKEN_BLOCK, D)
    K = PEER_TOPK

    def block(xb):
        t = xb.shape[0]
        q = (xb @ w_q).reshape(t, PEER_HEADS, 2, PEER_HALF)
        s = jnp.einsum('thpd,hpkd->thpk', q, sub_keys).astype(jnp.float32)
        top_s, top_i = lax.top_k(s, K)
        cand_s = top_s[:, :, 0, :, None] + top_s[:, :, 1, None, :]
        cand_i = top_i[:, :, 0, :, None] * PEER_N_KEYS + top_i[:, :, 1, None, :]
        best_s, best_pos = lax.top_k(cand_s.reshape(t, PEER_HEADS, K * K), K)
        idx = jnp.take_along_axis(cand_i.reshape(t, PEER_HEADS, K * K), best_pos, axis=-1)
        gate = jax.nn.softmax(best_s, axis=-1).astype(xb.dtype)
        h = jax.nn.gelu(jnp.einsum('thkd,td->thk', u_table[idx], xb), approximate=False)
        return jnp.einsum('thk,thkd->td', gate * h, v_table[idx])

    return lax.map(block, xt).reshape(B, S, D)


def setup_inputs(seed: int = 0) -> dict:
    key = jax.random.key(seed)
    ks = jax.random.split(key, 32)
    L = DEPTH
    nrm = lambda k, shape, scale: jax.random.normal(k, shape, jnp.float32) * scale
    return {
        "x": nrm(ks[0], (BATCH, SEQ, D_MODEL), 1.0),
        "w_in": nrm(ks[1], (L, D_MODEL, IN_WIDTH), D_MODEL ** -0.5),
        "fox_f_bias": 3.0 + nrm(ks[2], (L, FOX_HEADS), 0.5),
        "rwkv_mu": jax.random.uniform(ks[3], (L, RWKV_COLS), jnp.float32),
        "rwkv_w0": nrm(ks[4], (L, RWKV_WIDTH), 1.0),
        "rwkv_w2": nrm(ks[5], (L, W_LORA, RWKV_WIDTH), 0.1 * W_LORA ** -0.5),
        "rwkv_a0": nrm(ks[6], (L, RWKV_WIDTH), 0.5),
        "rwkv_a2": nrm(ks[7], (L, A_LORA, RWKV_WIDTH), A_LORA ** -0.5),
        "rwkv_g2": nrm(ks[8], (L, G_LORA, RWKV_WIDTH), G_LORA ** -0.5),
        "rwkv_k_k": 0.85 + nrm(ks[9], (L, RWKV_WIDTH), 0.05),
        "rwkv_k_a": 1.0 + nrm(ks[10], (L, RWKV_WIDTH), 0.05),
        "rwkv_r_k": nrm(ks[11], (L, RWKV_WIDTH), 0.1),
        "rwkv_ln_g": 1.0 + nrm(ks[12], (L, RWKV_WIDTH), 0.05),
        "rwkv_ln_b": nrm(ks[13], (L, RWKV_WIDTH), 0.01),
        "p_fox": nrm(ks[14], (L, FOX_WIDTH, D_MODEL), FOX_WIDTH ** -0.5),
        "p_rwkv": nrm(ks[15], (L, RWKV_WIDTH, D_MODEL), RWKV_WIDTH ** -0.5),
        "w_o": nrm(ks[16], (L, D_MODEL, D_MODEL), DN_BETA * D_MODEL ** -0.5),
        "ln1_g": 1.0 + nrm(ks[17], (L, D_MODEL), 0.05),
        "ln1_b": nrm(ks[18], (L, D_MODEL), 0.01),
        "peer_w_q": nrm(ks[19], (L, D_MODEL, PEER_HEADS * 2 * PEER_HALF), D_MODEL ** -0.5),
        "peer_sub_keys": nrm(ks[20], (L, PEER_HEADS, 2, PEER_N_KEYS, PEER_HALF), PEER_HALF ** -0.5),
        "peer_u": nrm(ks[21], (L, PEER_N_EXPERTS, D_MODEL), D_MODEL ** -0.5),
        "peer_v": nrm(ks[22], (L, PEER_N_EXPERTS, D_MODEL), DN_BETA * PEER_HEADS ** -0.5),
        "ln2_g": 1.0 + nrm(ks[23], (L, D_MODEL), 0.05),
        "ln2_b": nrm(ks[24], (L, D_MODEL), 0.01),
    }


def reference(x, w_in, fox_f_bias, rwkv_mu, rwkv_w0, rwkv_w2, rwkv_a0, rwkv_a2, rwkv_g2,
              rwkv_k_k, rwkv_k_a, rwkv_r_k, rwkv_ln_g, rwkv_ln_b, p_fox, p_rwkv, w_o,
              ln1_g, ln1_b, peer_w_q, peer_sub_keys, peer_u, peer_v, ln2_g, ln2_b):
    for l in range(DEPTH):
        p = x @ w_in[l]
        fox_cols = p[..., :FOX_COLS]
        rwkv_cols = _token_shift(p[..., FOX_COLS:FOX_COLS + RWKV_COLS], rwkv_mu[l])
        gate_cols = p[..., FOX_COLS + RWKV_COLS:]
        q, k, v, f_logit = _split(fox_cols, FOX_SPLITS)
        r_r, r_k, r_v, r_wl, r_al, r_gl = _split(rwkv_cols, RWKV_SPLITS)
        gate_fox, gate_rwkv = _split(gate_cols, (D_MODEL, D_MODEL))

        y_fox = _forgetting_attention(q, k, v, f_logit + fox_f_bias[l])
        y_rwkv = _rwkv7_time_mix(r_r, r_k, r_v, r_wl, r_al, r_gl,
                                 rwkv_w0[l], rwkv_w2[l], rwkv_a0[l], rwkv_a2[l], rwkv_g2[l],
                                 rwkv_k_k[l], rwkv_k_a[l], rwkv_r_k[l], rwkv_ln_g[l], rwkv_ln_b[l])
        merged = (jax.nn.sigmoid(gate_fox) * (y_fox @ p_fox[l])
                  + jax.nn.sigmoid(gate_rwkv) * (y_rwkv @ p_rwkv[l]))
        x = _layer_norm(DN_ALPHA * x + merged @ w_o[l], ln1_g[l], ln1_b[l])
        x = _layer_norm(DN_ALPHA * x + _peer(x, peer_w_q[l], peer_sub_keys[l], peer_u[l], peer_v[l]),
                        ln2_g[l], ln2_b[l])
    return x
```

```python
import numpy as np
import concourse.bass as bass
import concourse.mybir as mybir
from concourse.bass_utils import run_bass_kernel_spmd
from contextlib import ExitStack

F32 = mybir.dt.float32
U32 = mybir.dt.uint32
AF = mybir.ActivationFunctionType
ALU = mybir.AluOpType
AX = mybir.AxisListType

EPOCH = 8000
DMA_MAX = 1900


class Buf:
    __slots__ = ("name", "w", "r")

    def __init__(self, name):
        self.name = name
        self.w = None
        self.r = {}


class DmaSem:
    __slots__ = ("handle", "count", "uid")
    _n = 0

    def __init__(self, handle):
        self.handle = handle
        self.count = 0
        DmaSem._n += 1
        self.uid = DmaSem._n


class Sched:
    ENG = ("sp", "act", "dve", "pool", "pe")

    def __init__(self, nc):
        self.nc = nc
        self.streams = {e: [] for e in self.ENG}
        self.seq = {e: 0 for e in self.ENG}
        self.known = {e: {} for e in self.ENG}
        self.esem = {}
        self.stack = ExitStack()
        self.nsem = 0
        self.ninstr = 0
        self.out_toks = []
        self.pi = 0

    def sbuf(self, name, shape, dtype):
        return self.stack.enter_context(self.nc.sbuf_tensor(name, list(shape), dtype))

    def psum(self, name, shape, dtype):
        return self.stack.enter_context(self.nc.psum_tensor(name, list(shape), dtype))

    def newsem(self, name):
        self.nsem += 1
        return self.nc.alloc_semaphore(name=f"{name}_{self.nsem}")

    def dmasem(self, name="d"):
        return DmaSem(self.newsem(name))

    def _esem(self, e, epoch):
        k = (e, epoch)
        if k not in self.esem:
            self.esem[k] = self.newsem(f"e_{e}{epoch}")
        return self.esem[k]

    def _wait(self, e, tok):
        if tok is None:
            return
        kind, ident, val = tok
        key = ident if kind == "eng" else ("dma", ident.uid)
        if self.known[e].get(key, 0) >= val:
            return
        self.known[e][key] = val
        if kind == "eng":
            sem = self._esem(ident, (val - 1) // EPOCH)
            v = (val - 1) % EPOCH + 1
        else:
            sem = ident.handle
            v = val
        self.streams[e].append(lambda eng, sem=sem, v=v: eng.wait_ge(sem, v))
        self.ninstr += 1

    def _deps(self, e, reads, writes, nowaw=False):
        for b in reads:
            if b.w is not None:
                if not (e == "pe" and b.w[0] == "eng" and b.w[1] == "pe"):
                    self._wait(e, b.w)
        for b in writes:
            if b.w is not None and not nowaw:
                if not (b.w[0] == "eng" and b.w[1] == e):
                    self._wait(e, b.w)
            for tok in b.r.values():
                if not (tok[0] == "eng" and tok[1] == e):
                    self._wait(e, tok)

    def _record(self, tok, reads, writes):
        for b in writes:
            b.w = tok
            b.r = {}
        key = tok[1] if tok[0] == "eng" else ("dma", tok[1].uid)
        for b in reads:
            b.r[key] = tok

    def op(self, e, fn, reads=(), writes=()):
        self._deps(e, reads, writes)
        self.seq[e] += 1
        s = self.seq[e]
        sem = self._esem(e, (s - 1) // EPOCH)
        self.streams[e].append(lambda eng, fn=fn, sem=sem: fn(eng).then_inc(sem, 1))
        self.ninstr += 1
        tok = ("eng", e, s)
        self._record(tok, reads, writes)
        return tok

    def dma(self, q, out, in_, sem, reads=(), writes=(), nowaw=False, fn=None, **kw):
        self._deps(q, reads, writes, nowaw=nowaw)
        sem.count += 1
        assert sem.count < DMA_MAX, "dma sem overflow"
        h = sem.handle
        if fn is None:
            self.streams[q].append(
                lambda eng, out=out, in_=in_, h=h, kw=kw: eng.dma_start(out=out, in_=in_, **kw).then_inc(h, 16))
        else:
            self.streams[q].append(lambda eng, fn=fn, h=h: fn(eng).then_inc(h, 16))
        self.ninstr += 1
        tok = ("dma", sem, 16 * sem.count)
        self._record(tok, reads, writes)
        return tok

    def emit(self):
        for tok in self.out_toks:
            self._wait("sp", tok)
        nc = self.nc
        with nc.Block() as block:
            for e, deco in (("sp", block.sync), ("act", block.scalar), ("dve", block.vector),
                            ("pool", block.gpsimd), ("pe", block.tensor)):
                stream = self.streams[e]

                def body(eng, stream=stream):
                    for th in stream:
                        th(eng)
                deco(body)
        self.stack.close()


class T:
    def __init__(self, S, name, shape, dtype=F32, psum=False):
        self.S = S
        self.t = (S.psum if psum else S.sbuf)("t_" + name, shape, dtype)
        self.b = Buf(name)
        self._sem = None
        self.name = name

    @property
    def sem(self):
        if self._sem is None or self._sem.count > DMA_MAX - 10:
            self._sem = self.S.dmasem(self.name)
        return self._sem


def ld(S, tl, dst, src, q="sp", nowaw=False):
    return S.dma(q, dst, src, tl.sem, writes=[tl.b], nowaw=nowaw)


def st(S, tl, dst, src, q="sp"):
    tok = S.dma(q, dst, src, tl.sem, reads=[tl.b])
    S.out_toks.append(tok)
    return tok


def mk_psum(S, n=8):
    return [T(S, f"ps{i}", [128, 512], F32, psum=True) for i in range(n)]


def nxt(S, PS):
    p = PS[S.pi % len(PS)]
    S.pi += 1
    return p


def din(nc, name, shape, dt=F32):
    return nc.dram_tensor(name, list(shape), dt, kind="ExternalInput").ap()


def dout(nc, name, shape, dt=F32):
    return nc.dram_tensor(name, list(shape), dt, kind="ExternalOutput").ap()


SEQ = 8192
NSC = SEQ // 512
NEG_E = -0.6065306597126334


def build_l1():
    nc = bass.Bass("TRN2", target_bir_lowering=False)
    xT = din(nc, "xT", [1024, SEQ])
    Wg = din(nc, "Wg", [1024, 1796])
    pcol = din(nc, "pcol", [128, 19])
    w2a2 = din(nc, "w2a2", [128, 256])
    g2 = din(nc, "g2", [128, 256])
    bd = din(nc, "bd", [128, 128])
    cm = din(nc, "cm", [128, 512])
    names = ["fq", "fk", "fv", "al", "be", "nbe", "ka", "rh", "rv", "rg", "bo"]
    O = {n: dout(nc, n, [256, SEQ]) for n in names}
    fc = dout(nc, "fc", [4, SEQ])
    fnc = dout(nc, "fnc", [4, SEQ])
    gc = dout(nc, "gc", [256, SEQ // 64])

    S = Sched(nc)
    PS = mk_psum(S)
    W = T(S, "W", [128, 8, 1796])
    PC = T(S, "PC", [128, 19])
    W2 = T(S, "W2", [128, 256])
    G2 = T(S, "G2", [128, 256])
    BD = T(S, "BD", [128, 128])
    CM = T(S, "CM", [128, 512])
    ONE = T(S, "ONE", [4, 512])
    CAR = T(S, "CAR", [4, 1])
    X = [T(S, f"X{i}", [128, 8, 512]) for i in range(2)]
    Wv = Wg.rearrange("(kc p) c -> p kc c", p=128)
    for kc in range(8):
        ld(S, W, W.t[:, kc, :], Wv[:, kc, :], nowaw=True)
    ld(S, PC, PC.t[:], pcol)
    ld(S, W2, W2.t[:], w2a2)
    ld(S, G2, G2.t[:], g2)
    ld(S, BD, BD.t[:], bd)
    ld(S, CM, CM.t[:], cm)
    S.op("dve", lambda e: e.memset(ONE.t[:], 1.0), writes=[ONE.b])
    S.op("dve", lambda e: e.memset(CAR.t[:], 0.0), writes=[CAR.b])
    xv = xT.rearrange("(kc p) t -> p kc t", p=128)

    Pn = ["r0", "r1", "k0", "k1", "v0", "v1", "l", "g"]
    P = {n: T(S, "P" + n, [128, 513]) for n in Pn}
    for n in Pn:
        S.op("dve", lambda e, n=n: e.memset(P[n].t[:, 0:1], 0.0), writes=[P[n].b])
    SH = {n: T(S, "SH" + n, [128, 512]) for n in Pn}
    tmp = T(S, "tmp", [128, 512])
    FO = [T(S, f"FO{i}", [128, 512]) for i in range(2)]
    FL = T(S, "FL", [4, 512]); FC = T(S, "FCt", [4, 512]); FN = T(S, "FNt", [4, 512])
    TH = T(S, "TH", [64, 512]); SGL = T(S, "SGL", [128, 512])
    nm = ["lw", "a", "g", "t1", "sq", "nr", "kk", "t2", "kmod", "cs", "en", "ep", "d2", "epv",
          "alpha", "t3", "beta", "nbeta", "kappa", "rho", "pr", "bo"]
    R = {n: T(S, "R" + n, [128, 512]) for n in nm}
    GC = T(S, "GC", [128, 8])
    pcmap = {"r0": 0, "r1": 1, "k0": 2, "k1": 3, "v0": 4, "v1": 5, "l": 6, "g": 7}
    blkmap = {"r0": 6, "r1": 7, "k0": 8, "k1": 9, "v0": 10, "v1": 11, "l": 12, "g": 13}

    def mm_block(xt, col0, ncol):
        ps = nxt(S, PS)
        for kc in range(8):
            S.op("pe", lambda e, ps=ps, kc=kc: e.matmul(ps.t[0:ncol, :], W.t[:, kc, col0:col0 + ncol], xt.t[:, kc, :],
                                                         start=(kc == 0), stop=(kc == 7)),
                 reads=[W.b, xt.b], writes=[ps.b])
        return ps

    for sc in range(NSC):
        xt = X[sc % 2]
        tsl = slice(sc * 512, (sc + 1) * 512)
        ld(S, xt, xt.t[:], xv[:, :, tsl])
        for blk in range(6):
            ps = mm_block(xt, blk * 128, 128)
            fo = FO[blk % 2]
            scale = 0.125 if blk < 2 else 1.0
            S.op("act", lambda e, fo=fo, ps=ps, scale=scale: e.activation(fo.t[:], ps.t[:], AF.Copy, scale=scale),
                 reads=[ps.b], writes=[fo.b])
            dst = O[("fq", "fk", "fv")[blk // 2]][(blk % 2) * 128:(blk % 2) * 128 + 128, tsl]
            st(S, fo, dst, fo.t[:])
        ps = mm_block(xt, 1792, 4)
        S.op("act", lambda e, ps=ps: e.activation(FL.t[:], ps.t[0:4, :], AF.Sigmoid, bias=PC.t[0:4, 18:19]),
             reads=[ps.b, PC.b], writes=[FL.b])
        S.op("act", lambda e: e.activation(FL.t[:], FL.t[:], AF.Ln), reads=[FL.b], writes=[FL.b])
        S.op("dve", lambda e: e.tensor_tensor_scan(FC.t[:], ONE.t[:], FL.t[:], CAR.t[:, 0:1], ALU.mult, ALU.add),
             reads=[ONE.b, FL.b, CAR.b], writes=[FC.b])
        S.op("dve", lambda e: e.tensor_copy(CAR.t[:], FC.t[:, 511:512]), reads=[FC.b], writes=[CAR.b])
        S.op("dve", lambda e: e.tensor_scalar(FN.t[:], FC.t[:], -1.0, None, ALU.mult), reads=[FC.b], writes=[FN.b])
        st(S, FC, fc[:, tsl], FC.t[:])
        st(S, FN, fnc[:, tsl], FN.t[:])
        for n in Pn:
            ps = mm_block(xt, blkmap[n] * 128, 128)
            p = P[n]; sh = SH[n]; mu = PC.t[:, pcmap[n]:pcmap[n] + 1]
            S.op("act", lambda e, p=p, ps=ps: e.activation(p.t[:, 1:513], ps.t[:], AF.Copy), reads=[ps.b], writes=[p.b])
            S.op("dve", lambda e, p=p: e.tensor_tensor(tmp.t[:], p.t[:, 0:512], p.t[:, 1:513], ALU.subtract),
                 reads=[p.b], writes=[tmp.b])
            S.op("dve", lambda e, p=p, sh=sh, mu=mu: e.scalar_tensor_tensor(sh.t[:], tmp.t[:], mu, p.t[:, 1:513], ALU.mult, ALU.add),
                 reads=[p.b, tmp.b, PC.b], writes=[sh.b])
            S.op("dve", lambda e, p=p: e.tensor_copy(p.t[:, 0:1], p.t[:, 512:513]), reads=[p.b], writes=[p.b])
        S.op("act", lambda e: e.activation(TH.t[:], SH["l"].t[0:64, :], AF.Tanh), reads=[SH["l"].b], writes=[TH.b])
        S.op("act", lambda e: e.activation(SGL.t[:], SH["g"].t[:], AF.Sigmoid), reads=[SH["g"].b], writes=[SGL.b])
        def rwkv_block(b, sc=sc, tsl=tsl):
            cs_ = slice(b * 128, b * 128 + 128)
            pc = lambda j: PC.t[:, j:j + 1]
            shr, shk, shv = SH[f"r{b}"], SH[f"k{b}"], SH[f"v{b}"]
            ps = nxt(S, PS)
            S.op("pe", lambda e, ps=ps: e.matmul(ps.t[:], W2.t[0:64, cs_], TH.t[:], start=True, stop=True),
                 reads=[W2.b, TH.b], writes=[ps.b])
            S.op("act", lambda e, ps=ps: e.activation(R["lw"].t[:], ps.t[:], AF.Sigmoid, bias=pc(8 + b)),
                 reads=[ps.b, PC.b], writes=[R["lw"].b])
            S.op("dve", lambda e: e.tensor_scalar(R["lw"].t[:], R["lw"].t[:], NEG_E, None, ALU.mult),
                 reads=[R["lw"].b], writes=[R["lw"].b])
            ps = nxt(S, PS)
            S.op("pe", lambda e, ps=ps: e.matmul(ps.t[:], W2.t[64:128, cs_], SH["l"].t[64:128, :], start=True, stop=True),
                 reads=[W2.b, SH["l"].b], writes=[ps.b])
            S.op("act", lambda e, ps=ps: e.activation(R["a"].t[:], ps.t[:], AF.Sigmoid, bias=pc(10 + b)),
                 reads=[ps.b, PC.b], writes=[R["a"].b])
            ps = nxt(S, PS)
            S.op("pe", lambda e, ps=ps: e.matmul(ps.t[:], G2.t[:, cs_], SGL.t[:], start=True, stop=True),
                 reads=[G2.b, SGL.b], writes=[ps.b])
            S.op("act", lambda e, ps=ps: e.activation(R["g"].t[:], ps.t[:], AF.Copy), reads=[ps.b], writes=[R["g"].b])
            S.op("dve", lambda e: e.tensor_scalar(R["t1"].t[:], shk.t[:], pc(12 + b), None, ALU.mult),
                 reads=[shk.b, PC.b], writes=[R["t1"].b])
            S.op("dve", lambda e: e.tensor_tensor(R["sq"].t[:], R["t1"].t[:], R["t1"].t[:], ALU.mult),
                 reads=[R["t1"].b], writes=[R["sq"].b])
            ps = nxt(S, PS)
            S.op("pe", lambda e, ps=ps: e.matmul(ps.t[:], BD.t[:], R["sq"].t[:], start=True, stop=True),
                 reads=[BD.b, R["sq"].b], writes=[ps.b])
            S.op("act", lambda e, ps=ps: e.activation(R["nr"].t[:], ps.t[:], AF.Sqrt), reads=[ps.b], writes=[R["nr"].b])
            S.op("dve", lambda e: e.tensor_scalar(R["nr"].t[:], R["nr"].t[:], 1e-12, None, ALU.max),
                 reads=[R["nr"].b], writes=[R["nr"].b])
            S.op("dve", lambda e: e.reciprocal(R["nr"].t[:], R["nr"].t[:]), reads=[R["nr"].b], writes=[R["nr"].b])
            S.op("dve", lambda e: e.tensor_tensor(R["kk"].t[:], R["t1"].t[:], R["nr"].t[:], ALU.mult),
                 reads=[R["t1"].b, R["nr"].b], writes=[R["kk"].b])
            S.op("dve", lambda e: e.tensor_scalar(R["t2"].t[:], R["a"].t[:], -1.0, pc(14 + b), ALU.add, ALU.mult),
                 reads=[R["a"].b, PC.b], writes=[R["t2"].b])
            S.op("dve", lambda e: e.scalar_tensor_tensor(R["kmod"].t[:], R["t2"].t[:], 1.0, shk.t[:], ALU.add, ALU.mult),
                 reads=[R["t2"].b, shk.b], writes=[R["kmod"].b])
            S.op("dve", lambda e: e.tensor_tensor_scan(R["cs"].t[:], CM.t[:], R["lw"].t[:], 0.0, ALU.mult, ALU.add),
                 reads=[CM.b, R["lw"].b], writes=[R["cs"].b])
            S.op("act", lambda e: e.activation(R["en"].t[:], R["cs"].t[:], AF.Exp, scale=-1.0), reads=[R["cs"].b], writes=[R["en"].b])
            S.op("act", lambda e: e.activation(R["ep"].t[:], R["cs"].t[:], AF.Exp), reads=[R["cs"].b], writes=[R["ep"].b])
            S.op("dve", lambda e: e.tensor_tensor(R["d2"].t[:], R["cs"].t[:], R["lw"].t[:], ALU.subtract),
                 reads=[R["cs"].b, R["lw"].b], writes=[R["d2"].b])
            S.op("act", lambda e: e.activation(R["epv"].t[:], R["d2"].t[:], AF.Exp), reads=[R["d2"].b], writes=[R["epv"].b])
            S.op("dve", lambda e: e.tensor_tensor(R["alpha"].t[:], R["kk"].t[:], R["epv"].t[:], ALU.mult),
                 reads=[R["kk"].b, R["epv"].b], writes=[R["alpha"].b])
            S.op("dve", lambda e: e.tensor_tensor(R["t3"].t[:], R["kk"].t[:], R["a"].t[:], ALU.mult),
                 reads=[R["kk"].b, R["a"].b], writes=[R["t3"].b])
            S.op("dve", lambda e: e.tensor_tensor(R["beta"].t[:], R["t3"].t[:], R["en"].t[:], ALU.mult),
                 reads=[R["t3"].b, R["en"].b], writes=[R["beta"].b])
            S.op("dve", lambda e: e.tensor_scalar(R["nbeta"].t[:], R["beta"].t[:], -1.0, None, ALU.mult),
                 reads=[R["beta"].b], writes=[R["nbeta"].b])
            S.op("dve", lambda e: e.tensor_tensor(R["kappa"].t[:], R["kmod"].t[:], R["en"].t[:], ALU.mult),
                 reads=[R["kmod"].b, R["en"].b], writes=[R["kappa"].b])
            S.op("dve", lambda e: e.tensor_tensor(R["rho"].t[:], shr.t[:], R["ep"].t[:], ALU.mult),
                 reads=[shr.b, R["ep"].b], writes=[R["rho"].b])
            S.op("dve", lambda e: e.tensor_copy(GC.t[:], R["ep"].t[:, 63::64]), reads=[R["ep"].b], writes=[GC.b])
            S.op("dve", lambda e: e.scalar_tensor_tensor(R["pr"].t[:], shr.t[:], pc(16 + b), R["kmod"].t[:], ALU.mult, ALU.mult),
                 reads=[shr.b, PC.b, R["kmod"].b], writes=[R["pr"].b])
            ps = nxt(S, PS)
            S.op("pe", lambda e, ps=ps: e.matmul(ps.t[:], BD.t[:], R["pr"].t[:], start=True, stop=True),
                 reads=[BD.b, R["pr"].b], writes=[ps.b])
            S.op("act", lambda e, ps=ps: e.activation(R["bo"].t[:], ps.t[:], AF.Copy), reads=[ps.b], writes=[R["bo"].b])
            for on, tl in (("al", R["alpha"]), ("be", R["beta"]), ("nbe", R["nbeta"]), ("ka", R["kappa"]),
                           ("rh", R["rho"]), ("rv", shv), ("rg", R["g"]), ("bo", R["bo"])):
                st(S, tl, O[on][cs_, tsl], tl.t[:])
            st(S, GC, gc[cs_, sc * 8:(sc + 1) * 8], GC.t[:])
        for b in range(2):
            rwkv_block(b)
    S.emit()
    return nc, S


def l1_inputs(x_b, w_in, g, P):
    c = lambda lo, n: np.arange(lo + 256 * g, lo + 256 * g + 256) if n == 256 else None
    RB = 1544
    cols = np.concatenate([
        np.arange(0 + 256 * g, 256 * g + 256), np.arange(512 + 256 * g, 512 + 256 * g + 256),
        np.arange(1024 + 256 * g, 1024 + 256 * g + 256),
        RB + np.arange(256 * g, 256 * g + 256), RB + 512 + np.arange(256 * g, 256 * g + 256),
        RB + 1024 + np.arange(256 * g, 256 * g + 256), RB + np.arange(1536, 1792),
        1536 + np.arange(4 * g, 4 * g + 4)])
    Wg = np.ascontiguousarray(w_in[:, cols])
    mu = P["rwkv_mu"]
    ch = slice(256 * g, 256 * g + 256)
    pcol = np.zeros((128, 19), np.float32)
    two = lambda v: v.reshape(2, 128).T
    pcol[:, 0:2] = two(mu[0:512][ch]); pcol[:, 2:4] = two(mu[512:1024][ch]); pcol[:, 4:6] = two(mu[1024:1536][ch])
    pcol[:, 6] = mu[1536:1664]; pcol[:, 7] = mu[1664:1792]
    pcol[:, 8:10] = two(P["rwkv_w0"][ch]); pcol[:, 10:12] = two(P["rwkv_a0"][ch])
    pcol[:, 12:14] = two(P["rwkv_k_k"][ch]); pcol[:, 14:16] = two(P["rwkv_k_a"][ch]); pcol[:, 16:18] = two(P["rwkv_r_k"][ch])
    pcol[0:4, 18] = P["fox_f_bias"][4 * g:4 * g + 4]
    w2a2 = np.concatenate([P["rwkv_w2"][:, ch], P["rwkv_a2"][:, ch]], 0)
    bd = np.kron(np.eye(2, dtype=np.float32), np.ones((64, 64), np.float32))
    cm = np.ones((128, 512), np.float32); cm[:, ::64] = 0.0
    return {"xT": np.ascontiguousarray(x_b.T), "Wg": Wg, "pcol": pcol, "w2a2": np.ascontiguousarray(w2a2),
            "g2": np.ascontiguousarray(P["rwkv_g2"][:, ch]), "bd": bd, "cm": cm}


def build_l2(nheads=4, nqc=NSC):
    nc = bass.Bass("TRN2", target_bir_lowering=False)
    QAd = din(nc, "QA", [nheads, 66, SEQ])
    KAd = din(nc, "KA", [nheads, 66, SEQ])
    VOd = din(nc, "VO", [nheads, 128, 64 * 128])
    M01d = din(nc, "M01", [128, 128])
    yf = dout(nc, "yf", [nheads * 64, SEQ])
    S = Sched(nc)
    PS = mk_psum(S)
    SC = PS[0:6]
    ACC = PS[6:8]
    QA = T(S, "QA", [66, SEQ]); KA = T(S, "KA", [66, SEQ]); VO = T(S, "VO", [128, 64, 128])
    M01 = T(S, "M01", [128, 128])
    PT = [T(S, f"PT{i}", [128, 512]) for i in range(3)]
    RD = T(S, "RD", [128, 512]); Y = T(S, "Y", [64, 512])
    ld(S, M01, M01.t[:], M01d)
    st_ = {"pi": 0, "pt": 0}

    def head(h):
        for i in range(4):
            sl = slice(i * 2048, (i + 1) * 2048)
            ld(S, QA, QA.t[:, sl], QAd[h, :, sl], nowaw=(i > 0))
            ld(S, KA, KA.t[:, sl], KAd[h, :, sl], nowaw=(i > 0))
        ld(S, VO, VO.t[:], VOd[h].rearrange("p (n d) -> p n d", d=128))

        def qchunk(qc):
            acc = ACC[qc % 2]
            nkb = 4 * qc + 4

            def kblock(kb):
                j = kb - 4 * qc
                q0 = max(0, 128 * j)
                ps = SC[st_["pi"] % 6]; st_["pi"] += 1
                pt = PT[st_["pt"] % 3]; st_["pt"] += 1
                S.op("pe", lambda e: e.matmul(ps.t[:, q0:512], KA.t[:, kb * 128:(kb + 1) * 128],
                                              QA.t[:, qc * 512 + q0:qc * 512 + 512], start=True, stop=True),
                     reads=[KA.b, QA.b], writes=[ps.b])
                S.op("act", lambda e: e.activation(pt.t[:, q0:512], ps.t[:, q0:512], AF.Exp), reads=[ps.b], writes=[pt.b])
                if j >= 0:
                    S.op("dve", lambda e: e.tensor_tensor(pt.t[:, q0:q0 + 128], pt.t[:, q0:q0 + 128], M01.t[:], ALU.mult),
                         reads=[pt.b, M01.b], writes=[pt.b])
                S.op("pe", lambda e: e.matmul(acc.t[:, q0:512], VO.t[:, kb, :], pt.t[:, q0:512],
                                              start=(kb == 0), stop=(kb == nkb - 1)),
                     reads=[VO.b, pt.b], writes=[acc.b])
            for kb in range(nkb):
                kblock(kb)
            S.op("dve", lambda e: e.reciprocal(RD.t[64:128, :], acc.t[64:128, :]), reads=[acc.b], writes=[RD.b])
            S.op("dve", lambda e: e.tensor_tensor(Y.t[:], acc.t[0:64, :], RD.t[64:128, :], ALU.mult),
                 reads=[acc.b, RD.b], writes=[Y.b])
            st(S, Y, yf[h * 64:(h + 1) * 64, qc * 512:(qc + 1) * 512], Y.t[:])
        for qc in range(nqc):
            qchunk(qc)
    for h in range(nheads):
        head(h)
    S.emit()
    return nc, S


def l2_inputs(r1):
    S_ = SEQ
    one = np.ones((1, S_), np.float32)
    QA = np.stack([np.concatenate([r1["fq"][h * 64:(h + 1) * 64], r1["fc"][h:h + 1], one], 0) for h in range(4)])
    KA = np.stack([np.concatenate([r1["fk"][h * 64:(h + 1) * 64], one, r1["fnc"][h:h + 1]], 0) for h in range(4)])
    VO = np.ones((4, 128, 64, 128), np.float32)
    for h in range(4):
        v = r1["fv"][h * 64:(h + 1) * 64].T.reshape(64, 128, 64)
        VO[h, :, :, 0:64] = v.transpose(1, 0, 2)
    M01 = np.triu(np.ones((128, 128), np.float32))
    return {"QA": np.ascontiguousarray(QA), "KA": np.ascontiguousarray(KA),
            "VO": np.ascontiguousarray(VO.reshape(4, 128, 64 * 128)), "M01": M01}


GN_EPS = 64e-5


def build_l3(nheads=4, nsc=NSC, dbg_nchunk=8, dbg_post=True):
    nc = bass.Bass("TRN2", target_bir_lowering=False)
    FMd = din(nc, "FM", [nheads, 4, 64, SEQ])
    TMd = din(nc, "TM", [nheads, 4, 64, SEQ])
    BOd = din(nc, "BO", [nheads, 64, 128])
    GCd = din(nc, "GC", [nheads, 64, 128])
    LGd = din(nc, "LG", [nheads, 64, 64])
    LBd = din(nc, "LB", [nheads, 64, 64])
    MKd = din(nc, "MK", [5, 64, 512])
    yr = dout(nc, "yr", [SEQ, nheads * 64])
    S = Sched(nc)
    PS = mk_psum(S)
    MK = [T(S, f"MK{i}", [64, 512]) for i in range(5)]
    for i in range(5):
        ld(S, MK[i], MK[i].t[:], MKd[i])
    MSL, MSU, MIU, NMIU, ID8 = MK
    FM = [[T(S, f"FM{p}_{i}", [64, 512]) for i in range(4)] for p in range(2)]
    TM = [[T(S, f"TM{p}_{i}", [64, 8, 64]) for i in range(4)] for p in range(2)]
    BO = T(S, "BO", [64, 128]); GC = T(S, "GC", [64, 128]); LG = T(S, "LG", [64, 64]); LB = T(S, "LB", [64, 64])
    mk3 = lambda n: T(S, n, [64, 8, 64])
    A = mk3("A"); AT = mk3("AT"); Wt = [mk3("W0"), mk3("W1")]; Pt = [mk3("P0"), mk3("P1")]; PTt = [mk3("PT0"), mk3("PT1")]
    AakT = mk3("AakT"); nArbT = mk3("nArbT"); ArkT = mk3("ArkT")
    ST = [T(S, "ST0", [64, 64]), T(S, "ST1", [64, 64])]
    STg = T(S, "STg", [64, 64]); RHS = T(S, "RHS", [64, 64]); US = T(S, "US", [64, 64])
    YO = [mk3("YO0"), mk3("YO1")]
    stat = T(S, "stat", [64, 6]); mv = T(S, "mv", [64, 2]); rstd = T(S, "rstd", [64, 1]); yn = T(S, "yn", [64, 64])
    cnt = {"st": 0}

    def batch_mm(lhs_of, rhs_of, reads):
        ps = nxt(S, PS)
        for j in range(8):
            S.op("pe", lambda e, j=j: e.matmul(ps.t[0:64, j * 64:(j + 1) * 64], lhs_of(j), rhs_of(j), start=True, stop=True),
                 reads=reads, writes=[ps.b])
        return ps

    def head(h):
        ld(S, BO, BO.t[:], BOd[h]); ld(S, GC, GC.t[:], GCd[h]); ld(S, LG, LG.t[:], LGd[h]); ld(S, LB, LB.t[:], LBd[h])
        S.op("dve", lambda e: e.memset(ST[cnt["st"] % 2].t[:], 0.0), writes=[ST[cnt["st"] % 2].b])

        def superchunk(sc):
            fm = FM[sc % 2]; tm = TM[sc % 2]
            tsl = slice(sc * 512, (sc + 1) * 512)
            for i in range(4):
                ld(S, fm[i], fm[i].t[:], FMd[h, i, :, tsl])
                ld(S, tm[i], tm[i].t[:], TMd[h, i, :, tsl].rearrange("p (c k) -> p c k", k=64))
            alT, beT, kaT, rhT = fm
            nbe, ka, vm, gt = tm
            fsl = lambda t_, j: t_.t[:, j * 64:(j + 1) * 64]
            f3 = lambda t_, j: t_.t[:, j, :]
            flat = lambda t_: t_.t[:].rearrange("p a b -> p (a b)")
            def evac_mask(dst, ps, mk):
                S.op("dve", lambda e: e.tensor_tensor(flat(dst), ps.t[0:64, :], mk.t[:], ALU.mult), reads=[ps.b, mk.b], writes=[dst.b])

            def evac_copy(dst, ps):
                S.op("act", lambda e: e.activation(flat(dst), ps.t[0:64, :], AF.Copy), reads=[ps.b], writes=[dst.b])

            def evac_add(dst, ps, src):
                S.op("dve", lambda e: e.tensor_tensor(flat(dst), ps.t[0:64, :], flat(src), ALU.add), reads=[ps.b, src.b], writes=[dst.b])

            def bmm3(l, r):
                return batch_mm(lambda j: f3(l, j), lambda j: f3(r, j), [l.b, r.b])

            def bmmf(l, r):
                return batch_mm(lambda j: fsl(l, j), lambda j: fsl(r, j), [l.b, r.b])

            evac_mask(A, bmmf(alT, beT), MSL)
            evac_mask(AT, bmmf(beT, alT), MSU)
            W0 = Wt[0]
            S.op("dve", lambda e: e.tensor_tensor(flat(W0), ID8.t[:], flat(AT), ALU.subtract), reads=[ID8.b, AT.b], writes=[W0.b])
            evac_copy(Pt[0], bmm3(AT, A))
            evac_copy(PTt[0], bmm3(A, AT))
            for i in range(5):
                Wc, Pc, PTc = Wt[i % 2], Pt[i % 2], PTt[i % 2]
                Wn, Pn, PTn = Wt[(i + 1) % 2], Pt[(i + 1) % 2], PTt[(i + 1) % 2]
                evac_add(Wn, bmm3(Pc, Wc), Wc)
                if i < 4:
                    evac_copy(Pn, bmm3(PTc, Pc))
                    evac_copy(PTn, bmm3(Pc, PTc))
            Wf = Wt[5 % 2]
            evac_mask(AakT, bmmf(kaT, alT), MSU)
            evac_mask(nArbT, bmmf(beT, rhT), NMIU)
            evac_mask(ArkT, bmmf(kaT, rhT), MIU)
            yo = YO[sc % 2]

            def chunk(j):
                c = sc * 8 + j
                st0 = ST[cnt["st"] % 2]; st1 = ST[(cnt["st"] + 1) % 2]; cnt["st"] += 1
                gcol = GC.t[:, c:c + 1]
                psr = nxt(S, PS)
                S.op("pe", lambda e: e.matmul(psr.t[0:64, 0:64], fsl(alT, j), st0.t[:], start=True, stop=False),
                     reads=[alT.b, st0.b], writes=[psr.b])
                S.op("pe", lambda e: e.matmul(psr.t[0:64, 0:64], f3(AakT, j), f3(vm, j), start=False, stop=True),
                     reads=[AakT.b, vm.b], writes=[psr.b])
                S.op("act", lambda e: e.activation(RHS.t[:], psr.t[0:64, 0:64], AF.Copy), reads=[psr.b], writes=[RHS.b])
                S.op("dve", lambda e: e.tensor_scalar(STg.t[:], st0.t[:], gcol, None, ALU.mult), reads=[st0.b, GC.b], writes=[STg.b])
                psu = nxt(S, PS)
                S.op("pe", lambda e: e.matmul(psu.t[0:64, 0:64], f3(Wf, j), RHS.t[:], start=True, stop=True),
                     reads=[Wf.b, RHS.b], writes=[psu.b])
                S.op("act", lambda e: e.activation(US.t[:], psu.t[0:64, 0:64], AF.Copy), reads=[psu.b], writes=[US.b])
                psy = nxt(S, PS)
                S.op("pe", lambda e: e.matmul(psy.t[0:64, 0:64], fsl(rhT, j), st0.t[:], start=True, stop=False),
                     reads=[rhT.b, st0.b], writes=[psy.b])
                S.op("pe", lambda e: e.matmul(psy.t[0:64, 0:64], f3(nArbT, j), US.t[:], start=False, stop=False),
                     reads=[nArbT.b, US.b], writes=[psy.b])
                S.op("pe", lambda e: e.matmul(psy.t[0:64, 0:64], f3(ArkT, j), f3(vm, j), start=False, stop=True),
                     reads=[ArkT.b, vm.b], writes=[psy.b])
                psd = nxt(S, PS)
                S.op("pe", lambda e: e.matmul(psd.t[0:64, 0:64], f3(nbe, j), US.t[:], start=True, stop=False),
                     reads=[nbe.b, US.b], writes=[psd.b])
                S.op("pe", lambda e: e.matmul(psd.t[0:64, 0:64], f3(ka, j), f3(vm, j), start=False, stop=True),
                     reads=[ka.b, vm.b], writes=[psd.b])
                S.op("dve", lambda e: e.scalar_tensor_tensor(st1.t[:], psd.t[0:64, 0:64], gcol, STg.t[:], ALU.mult, ALU.add),
                     reads=[psd.b, GC.b, STg.b], writes=[st1.b])
                if not dbg_post:
                    S.op('act', lambda e: e.activation(f3(yo, j), psy.t[0:64, 0:64], AF.Copy), reads=[psy.b], writes=[yo.b])
                    return
                S.op("dve", lambda e: e.bn_stats(stat.t[:], psy.t[0:64, 0:64]), reads=[psy.b], writes=[stat.b])
                S.op("dve", lambda e: e.bn_aggr(mv.t[:], stat.t[:]), reads=[stat.b], writes=[mv.b])
                S.op("dve", lambda e: e.tensor_scalar(rstd.t[:], mv.t[:, 1:2], GN_EPS, None, ALU.add), reads=[mv.b], writes=[rstd.b])
                S.op("act", lambda e: e.activation(rstd.t[:], rstd.t[:], AF.Sqrt), reads=[rstd.b], writes=[rstd.b])
                S.op("dve", lambda e: e.reciprocal(rstd.t[:], rstd.t[:]), reads=[rstd.b], writes=[rstd.b])
                S.op("dve", lambda e: e.tensor_scalar(yn.t[:], psy.t[0:64, 0:64], mv.t[:, 0:1], rstd.t[:, 0:1], ALU.subtract, ALU.mult),
                     reads=[psy.b, mv.b, rstd.b], writes=[yn.b])
                S.op("dve", lambda e: e.tensor_tensor(yn.t[:], yn.t[:], LG.t[:], ALU.mult), reads=[yn.b, LG.b], writes=[yn.b])
                S.op("dve", lambda e: e.tensor_tensor(yn.t[:], yn.t[:], LB.t[:], ALU.add), reads=[yn.b, LB.b], writes=[yn.b])
                S.op("dve", lambda e: e.scalar_tensor_tensor(yn.t[:], f3(vm, j), BO.t[:, c:c + 1], yn.t[:], ALU.mult, ALU.add),
                     reads=[vm.b, BO.b, yn.b], writes=[yn.b])
                S.op("dve", lambda e: e.tensor_tensor(f3(yo, j), yn.t[:], f3(gt, j), ALU.mult), reads=[yn.b, gt.b], writes=[yo.b])
            for j in range(dbg_nchunk):
                chunk(j)
            st(S, yo, yr[tsl, h * 64:(h + 1) * 64].rearrange("(c t) v -> t c v", t=64), yo.t[:])
        for sc in range(nsc):
            superchunk(sc)
    for h in range(nheads):
        head(h)
    S.emit()
    return nc, S


def l3_inputs(r1, ln_g, ln_b, g):
    hs = lambda a, h: a[h * 64:(h + 1) * 64]
    tmaj = lambda a: np.ascontiguousarray(a.T.reshape(SEQ // 64, 64, 64).transpose(1, 0, 2).reshape(64, SEQ))
    FM = np.stack([np.stack([hs(r1[n], h) for n in ("al", "be", "ka", "rh")]) for h in range(4)])
    TM = np.stack([np.stack([tmaj(hs(r1[n], h)) for n in ("nbe", "ka", "rv", "rg")]) for h in range(4)])
    BO = np.stack([np.ascontiguousarray(r1["bo"][h * 64].reshape(SEQ // 64, 64).T) for h in range(4)])
    GC = np.stack([hs(r1["gc"], h) for h in range(4)])
    ch = slice(256 * g, 256 * g + 256)
    LG = np.stack([np.broadcast_to(hs(ln_g[ch], h)[None, :], (64, 64)) for h in range(4)])
    LB = np.stack([np.broadcast_to(hs(ln_b[ch], h)[None, :], (64, 64)) for h in range(4)])
    one = np.ones((64, 64), np.float32)
    msl = np.tril(one, -1); msu = np.triu(one, 1); miu = np.triu(one)
    rep = lambda m: np.tile(m, (1, 8))
    MK = np.stack([rep(msl), rep(msu), rep(miu), rep(-miu), rep(np.eye(64, dtype=np.float32))])
    c = np.ascontiguousarray
    return {"FM": c(FM), "TM": c(TM), "BO": c(BO), "GC": c(GC), "LG": c(LG), "LB": c(LB), "MK": c(MK)}


NTOK = 4096
DN_ALPHA = 2.0 ** 0.25
LN_EPS = 1e-5


def layer_norm_tile(S, Z, OUT, LNG, LNB, stat, mv, rstd):
    for hf in range(2):
        S.op("dve", lambda e, hf=hf: e.bn_stats(stat.t[:, hf * 6:(hf + 1) * 6], Z.t[:, hf * 512:(hf + 1) * 512]),
             reads=[Z.b], writes=[stat.b])
    S.op("dve", lambda e: e.bn_aggr(mv.t[:], stat.t[:]), reads=[stat.b], writes=[mv.b])
    S.op("dve", lambda e: e.tensor_scalar(rstd.t[:], mv.t[:, 1:2], LN_EPS, None, ALU.add), reads=[mv.b], writes=[rstd.b])
    S.op("act", lambda e: e.activation(rstd.t[:], rstd.t[:], AF.Sqrt), reads=[rstd.b], writes=[rstd.b])
    S.op("dve", lambda e: e.reciprocal(rstd.t[:], rstd.t[:]), reads=[rstd.b], writes=[rstd.b])
    S.op("dve", lambda e: e.tensor_scalar(Z.t[:], Z.t[:], mv.t[:, 0:1], rstd.t[:, 0:1], ALU.subtract, ALU.mult),
         reads=[Z.b, mv.b, rstd.b], writes=[Z.b])
    S.op("dve", lambda e: e.tensor_tensor(Z.t[:], Z.t[:], LNG.t[:], ALU.mult), reads=[Z.b, LNG.b], writes=[Z.b])
    S.op("dve", lambda e: e.tensor_tensor(OUT.t[:], Z.t[:], LNB.t[:], ALU.add), reads=[Z.b, LNB.b], writes=[OUT.b])


def build_l4(nsc=NTOK // 512):
    nc = bass.Bass("TRN2", target_bir_lowering=False)
    xTd = din(nc, "xT", [1024, NTOK]); xd = din(nc, "x", [NTOK, 1024])
    yfd = din(nc, "yfT", [512, NTOK]); yrd = din(nc, "yrT", [512, NTOK])
    wgd = din(nc, "wg", [1024, 2048]); pad = din(nc, "pa", [512, 1024]); pbd = din(nc, "pb", [512, 1024])
    wod = din(nc, "wo", [1024, 1024]); lgd = din(nc, "lg", [128, 1024]); lbd = din(nc, "lb", [128, 1024])
    x1d = dout(nc, "x1", [NTOK, 1024])
    S = Sched(nc)
    PS = mk_psum(S)
    WG = T(S, "WG", [128, 8, 2048]); PA = T(S, "PA", [128, 4, 1024]); PB = T(S, "PB", [128, 4, 1024]); WO = T(S, "WO", [128, 8, 1024])
    LNG = T(S, "LNG", [128, 1024]); LNB = T(S, "LNB", [128, 1024])
    for kc in range(8):
        ld(S, WG, WG.t[:, kc, :], wgd.rearrange("(k p) c -> p k c", p=128)[:, kc, :], nowaw=True)
        ld(S, WO, WO.t[:, kc, :], wod.rearrange("(k p) c -> p k c", p=128)[:, kc, :], nowaw=True)
    ld(S, PA, PA.t[:], pad.rearrange("(k p) c -> p k c", p=128))
    ld(S, PB, PB.t[:], pbd.rearrange("(k p) c -> p k c", p=128))
    ld(S, LNG, LNG.t[:], lgd); ld(S, LNB, LNB.t[:], lbd)
    XT = T(S, "XT", [128, 8, 512]); YF = T(S, "YF", [128, 4, 512]); YR = T(S, "YR", [128, 4, 512])
    MT = T(S, "MT", [128, 8, 512]); SG = T(S, "SG", [128, 512]); MA = T(S, "MA", [128, 512])
    XR = T(S, "XR", [128, 1024]); Z = T(S, "Z", [128, 1024]); X1 = T(S, "X1", [128, 1024])
    stat = T(S, "stat", [128, 12]); mv = T(S, "mv", [128, 2]); rstd = T(S, "rstd", [128, 1])

    def superchunk(sc):
        tsl = slice(sc * 512, (sc + 1) * 512)
        ld(S, XT, XT.t[:], xTd.rearrange("(k p) t -> p k t", p=128)[:, :, tsl])
        ld(S, YF, YF.t[:], yfd.rearrange("(k p) t -> p k t", p=128)[:, :, tsl])
        ld(S, YR, YR.t[:], yrd.rearrange("(k p) t -> p k t", p=128)[:, :, tsl])

        def nblock(nb):
            def branch(goff, PW, Y, dst_first):
                psg = nxt(S, PS)
                for kc in range(8):
                    S.op("pe", lambda e, kc=kc: e.matmul(psg.t[:], WG.t[:, kc, goff + nb * 128:goff + nb * 128 + 128], XT.t[:, kc, :],
                                                         start=(kc == 0), stop=(kc == 7)), reads=[WG.b, XT.b], writes=[psg.b])
                S.op("act", lambda e: e.activation(SG.t[:], psg.t[:], AF.Sigmoid), reads=[psg.b], writes=[SG.b])
                psz = nxt(S, PS)
                for kc in range(4):
                    S.op("pe", lambda e, kc=kc: e.matmul(psz.t[:], PW.t[:, kc, nb * 128:nb * 128 + 128], Y.t[:, kc, :],
                                                         start=(kc == 0), stop=(kc == 3)), reads=[PW.b, Y.b], writes=[psz.b])
                if dst_first:
                    S.op("dve", lambda e: e.tensor_tensor(MA.t[:], psz.t[:], SG.t[:], ALU.mult), reads=[psz.b, SG.b], writes=[MA.b])
                else:
                    S.op("dve", lambda e: e.tensor_tensor(SG.t[:], psz.t[:], SG.t[:], ALU.mult), reads=[psz.b, SG.b], writes=[SG.b])
                    S.op("dve", lambda e: e.tensor_tensor(MT.t[:, nb, :], MA.t[:], SG.t[:], ALU.add), reads=[MA.b, SG.b], writes=[MT.b])
            branch(0, PA, YF, True)
            branch(1024, PB, YR, False)
        for nb in range(8):
            nblock(nb)

        def ttile(tt):
            r0 = sc * 512 + tt * 128
            ld(S, XR, XR.t[:], xd[r0:r0 + 128, :])
            for hf in range(2):
                def half(hf=hf):
                    ps = nxt(S, PS)
                    for nb in range(8):
                        S.op("pe", lambda e, nb=nb: e.matmul(ps.t[:], MT.t[:, nb, tt * 128:(tt + 1) * 128], WO.t[:, nb, hf * 512:(hf + 1) * 512],
                                                             start=(nb == 0), stop=(nb == 7)), reads=[MT.b, WO.b], writes=[ps.b])
                    S.op("dve", lambda e: e.scalar_tensor_tensor(Z.t[:, hf * 512:(hf + 1) * 512], XR.t[:, hf * 512:(hf + 1) * 512], DN_ALPHA,
                                                                 ps.t[:], ALU.mult, ALU.add), reads=[XR.b, ps.b], writes=[Z.b])
                half()
            layer_norm_tile(S, Z, X1, LNG, LNB, stat, mv, rstd)
            st(S, X1, x1d[r0:r0 + 128, :], X1.t[:])
        for tt in range(4):
            ttile(tt)
    for sc in range(nsc):
        superchunk(sc)
    S.emit()
    return nc, S


def build_l5(ntile=NTOK // 128):
    nc = bass.Bass("TRN2", target_bir_lowering=False)
    x1d = din(nc, "x1", [NTOK, 1024]); x1Td = din(nc, "x1T", [1024, NTOK])
    wqd = din(nc, "wq", [1024, 2048]); skd = din(nc, "sk", [128, 16 * 128])
    ud = din(nc, "u", [16384, 1024]); vd = din(nc, "v", [16384, 1024])
    lgd = din(nc, "lg", [128, 1024]); lbd = din(nc, "lb", [128, 1024]); iod = din(nc, "iota", [128, 256])
    od = dout(nc, "out", [NTOK, 1024])
    S = Sched(nc)
    PS = mk_psum(S)
    WQ = T(S, "WQ", [128, 8, 2048]); SK = T(S, "SK", [128, 16, 128])
    LNG = T(S, "LNG", [128, 1024]); LNB = T(S, "LNB", [128, 1024])
    for kc in range(8):
        ld(S, WQ, WQ.t[:, kc, :], wqd.rearrange("(k p) c -> p k c", p=128)[:, kc, :], nowaw=True)
    ld(S, SK, SK.t[:], skd.rearrange("p (b k) -> p b k", k=128))
    ld(S, LNG, LNG.t[:], lgd); ld(S, LNB, LNB.t[:], lbd)
    IOT = T(S, "IOT", [128, 256]); ld(S, IOT, IOT.t[:], iod)
    BPU = T(S, "BPU", [128, 16], U32); BPF = T(S, "BPF", [128, 16])
    X1 = T(S, "X1", [128, 1024]); X1T = T(S, "X1T", [128, 8, 128])
    QT = T(S, "QT", [128, 16, 128]); SC = T(S, "SC", [128, 16, 128]); SC2 = T(S, "SC2", [128, 128])
    TS = T(S, "TS", [128, 16, 16]); TI = T(S, "TI", [128, 16, 16], U32); TIF = T(S, "TIF", [128, 16, 16]); TI128 = T(S, "TI128", [128, 16, 16])
    CS = T(S, "CS", [128, 256]); CI = T(S, "CI", [128, 256]); CS2 = T(S, "CS2", [128, 256]); JK = T(S, "JK", [128, 256])
    BS = T(S, "BS", [128, 8, 16]); BP = T(S, "BP", [128, 8], U32); IDF = T(S, "IDF", [128, 128]); IDX = T(S, "IDX", [128, 128], U32)
    NM = T(S, "NM", [128, 8]); EX = T(S, "EX", [128, 8, 16]); SM = T(S, "SM", [128, 8]); GW = T(S, "GW", [128, 128])
    UB = [T(S, f"UB{i}", [128, 1024]) for i in range(4)]
    JK2 = T(S, "JK2", [128, 1024]); H = T(S, "H", [128, 128]); HG = T(S, "HG", [128, 128])
    ACC = T(S, "ACC", [128, 1024]); Z = T(S, "Z", [128, 1024]); OUT = T(S, "OUT", [128, 1024])
    stat = T(S, "stat", [128, 12]); mv = T(S, "mv", [128, 2]); rstd = T(S, "rstd", [128, 1])
    cnt = {"ub": 0}

    def tile(ti):
        r0 = ti * 128
        ld(S, X1, X1.t[:], x1d[r0:r0 + 128, :])
        ld(S, X1T, X1T.t[:], x1Td.rearrange("(k p) t -> p k t", p=128)[:, :, r0:r0 + 128])

        def qgroup(gq):
            ps = nxt(S, PS)
            for bi in range(4):
                blk = gq * 4 + bi
                for kc in range(8):
                    S.op("pe", lambda e, bi=bi, blk=blk, kc=kc: e.matmul(ps.t[:, bi * 128:(bi + 1) * 128], WQ.t[:, kc, blk * 128:(blk + 1) * 128],
                                                                         X1T.t[:, kc, :], start=(kc == 0), stop=(kc == 7)),
                         reads=[WQ.b, X1T.b], writes=[ps.b])
            S.op("act", lambda e: e.activation(QT.t[:, gq * 4:(gq + 1) * 4, :].rearrange("p a b -> p (a b)"), ps.t[:], AF.Copy),
                 reads=[ps.b], writes=[QT.b])
        for gq in range(4):
            qgroup(gq)

        def sgroup(gq):
            ps = nxt(S, PS)
            for bi in range(4):
                blk = gq * 4 + bi
                S.op("pe", lambda e, bi=bi, blk=blk: e.matmul(ps.t[:, bi * 128:(bi + 1) * 128], QT.t[:, blk, :], SK.t[:, blk, :],
                                                              start=True, stop=True), reads=[QT.b, SK.b], writes=[ps.b])
            S.op("act", lambda e: e.activation(SC.t[:, gq * 4:(gq + 1) * 4, :].rearrange("p a b -> p (a b)"), ps.t[:], AF.Copy),
                 reads=[ps.b], writes=[SC.b])
        for gq in range(4):
            sgroup(gq)

        def top16(blk):
            S.op("dve", lambda e: e.max(TS.t[:, blk, 0:8], SC.t[:, blk, :]), reads=[SC.b], writes=[TS.b])
            S.op("dve", lambda e: e.max_index(TI.t[:, blk, 0:8], TS.t[:, blk, 0:8], SC.t[:, blk, :]), reads=[SC.b, TS.b], writes=[TI.b])
            S.op("dve", lambda e: e.match_replace(SC2.t[:], TS.t[:, blk, 0:8], SC.t[:, blk, :], -1e30), reads=[SC.b, TS.b], writes=[SC2.b])
            S.op("dve", lambda e: e.max(TS.t[:, blk, 8:16], SC2.t[:]), reads=[SC2.b], writes=[TS.b])
            S.op("dve", lambda e: e.max_index(TI.t[:, blk, 8:16], TS.t[:, blk, 8:16], SC2.t[:]), reads=[SC2.b, TS.b], writes=[TI.b])
        for blk in range(16):
            top16(blk)
        S.op("dve", lambda e: e.tensor_copy(TIF.t[:], TI.t[:]), reads=[TI.b], writes=[TIF.b])
        S.op("dve", lambda e: e.tensor_scalar(TI128.t[:], TIF.t[:], 128.0, None, ALU.mult), reads=[TIF.b], writes=[TI128.b])

        def head(h):
            for a in range(16):
                S.op("dve", lambda e, a=a: e.tensor_scalar(CS.t[:, a * 16:(a + 1) * 16], TS.t[:, 2 * h + 1, :], TS.t[:, 2 * h, a:a + 1], None, ALU.add),
                     reads=[TS.b], writes=[CS.b])
                S.op("dve", lambda e, a=a: e.tensor_scalar(CI.t[:, a * 16:(a + 1) * 16], TIF.t[:, 2 * h + 1, :], TI128.t[:, 2 * h, a:a + 1], None, ALU.add),
                     reads=[TIF.b, TI128.b], writes=[CI.b])
            S.op("dve", lambda e: e.max(BS.t[:, h, 0:8], CS.t[:]), reads=[CS.b], writes=[BS.b])
            S.op("dve", lambda e: e.max_index(BPU.t[:, 0:8], BS.t[:, h, 0:8], CS.t[:]), reads=[CS.b, BS.b], writes=[BPU.b])
            S.op("dve", lambda e: e.match_replace(CS2.t[:], BS.t[:, h, 0:8], CS.t[:], -1e30), reads=[CS.b, BS.b], writes=[CS2.b])
            S.op("dve", lambda e: e.max(BS.t[:, h, 8:16], CS2.t[:]), reads=[CS2.b], writes=[BS.b])
            S.op("dve", lambda e: e.max_index(BPU.t[:, 8:16], BS.t[:, h, 8:16], CS2.t[:]), reads=[CS2.b, BS.b], writes=[BPU.b])
            S.op("dve", lambda e: e.tensor_copy(BPF.t[:], BPU.t[:]), reads=[BPU.b], writes=[BPF.b])
            for k in range(16):
                S.op("dve", lambda e, k=k: e.scalar_tensor_tensor(JK.t[:], IOT.t[:], BPF.t[:, k:k + 1], CI.t[:], ALU.is_equal, ALU.mult,
                                                                  accum_out=IDF.t[:, h * 16 + k:h * 16 + k + 1]),
                     reads=[IOT.b, BPF.b, CI.b], writes=[JK.b, IDF.b])
            S.op("dve", lambda e: e.tensor_scalar(NM.t[:, h:h + 1], BS.t[:, h, 0:1], -1.0, None, ALU.mult), reads=[BS.b], writes=[NM.b])
            S.op("act", lambda e: e.activation(EX.t[:, h, :], BS.t[:, h, :], AF.Exp, bias=NM.t[:, h:h + 1], accum_out=SM.t[:, h:h + 1]),
                 reads=[BS.b, NM.b], writes=[EX.b, SM.b])
        for h in range(8):
            head(h)
        S.op("dve", lambda e: e.tensor_copy(IDX.t[:], IDF.t[:]), reads=[IDF.b], writes=[IDX.b])
        S.op("dve", lambda e: e.reciprocal(SM.t[:], SM.t[:]), reads=[SM.b], writes=[SM.b])
        for h in range(8):
            S.op("dve", lambda e, h=h: e.tensor_scalar(GW.t[:, h * 16:(h + 1) * 16], EX.t[:, h, :], SM.t[:, h:h + 1], None, ALU.mult),
                 reads=[EX.b, SM.b], writes=[GW.b])

        def gather(tab, s):
            ub = UB[cnt["ub"] % 4]; cnt["ub"] += 1
            S.dma("pool", None, None, ub.sem, reads=[IDX.b], writes=[ub.b],
                  fn=lambda e: e.indirect_dma_start(out=ub.t[:], out_offset=None, in_=tab,
                                                    in_offset=bass.IndirectOffsetOnAxis(ap=IDX.t[:, s:s + 1], axis=0)))
            return ub

        def uslot(s):
            ub = gather(ud, s)
            S.op("dve", lambda e: e.scalar_tensor_tensor(JK2.t[:], ub.t[:], 1.0, X1.t[:], ALU.mult, ALU.mult, accum_out=H.t[:, s:s + 1]),
                 reads=[ub.b, X1.b], writes=[JK2.b, H.b])
        for s in range(128):
            uslot(s)
        S.op("act", lambda e: e.activation(HG.t[:], H.t[:], AF.Gelu), reads=[H.b], writes=[HG.b])
        S.op("dve", lambda e: e.tensor_tensor(HG.t[:], HG.t[:], GW.t[:], ALU.mult), reads=[HG.b, GW.b], writes=[HG.b])
        S.op("dve", lambda e: e.memset(ACC.t[:], 0.0), writes=[ACC.b])

        def vslot(s):
            ub = gather(vd, s)
            S.op("dve", lambda e: e.scalar_tensor_tensor(ACC.t[:], ub.t[:], HG.t[:, s:s + 1], ACC.t[:], ALU.mult, ALU.add),
                 reads=[ub.b, HG.b, ACC.b], writes=[ACC.b])
        for s in range(128):
            vslot(s)
        S.op("dve", lambda e: e.scalar_tensor_tensor(Z.t[:], X1.t[:], DN_ALPHA, ACC.t[:], ALU.mult, ALU.add), reads=[X1.b, ACC.b], writes=[Z.b])
        layer_norm_tile(S, Z, OUT, LNG, LNB, stat, mv, rstd)
        st(S, OUT, od[r0:r0 + 128, :], OUT.t[:])
    for ti in range(ntile):
        tile(ti)
    S.emit()
    return nc, S


def _run(nc, in_maps):
    res = run_bass_kernel_spmd(nc, in_maps, core_ids=list(range(len(in_maps))))
    return res.results


def kernel(**inputs):
    c_ = np.ascontiguousarray
    x = np.asarray(inputs["x"], np.float32)
    P = {k: np.asarray(v, np.float32)[0] for k, v in inputs.items() if k != "x"}
    B = x.shape[0]
    nc1, _ = build_l1()
    r1 = _run(nc1, [l1_inputs(x[b], P["w_in"], g, P) for b in range(B) for g in range(2)])
    nc2, _ = build_l2()
    r2 = _run(nc2, [l2_inputs(r) for r in r1])
    nc3, _ = build_l3()
    r3 = _run(nc3, [l3_inputs(r1[c], P["rwkv_ln_g"], P["rwkv_ln_b"], c % 2) for c in range(2 * B)])
    del r1
    yfT = [np.concatenate([r2[2 * b]["yf"], r2[2 * b + 1]["yf"]], 0) for b in range(B)]
    yr = [np.concatenate([r3[2 * b]["yr"], r3[2 * b + 1]["yr"]], 1) for b in range(B)]
    bc = lambda v: c_(np.broadcast_to(v[None, :], (128, v.shape[0])))
    nc4, _ = build_l4()
    im4 = []
    for b in range(B):
        for t in range(2):
            tok = slice(t * NTOK, (t + 1) * NTOK)
            im4.append({"xT": c_(x[b, tok].T), "x": c_(x[b, tok]), "yfT": c_(yfT[b][:, tok]), "yrT": c_(yr[b][tok].T),
                        "wg": c_(P["w_in"][:, 3336:]), "pa": P["p_fox"], "pb": P["p_rwkv"], "wo": P["w_o"],
                        "lg": bc(P["ln1_g"]), "lb": bc(P["ln1_b"])})
    r4 = _run(nc4, im4)
    del im4
    nc5, _ = build_l5()
    sk = c_(P["peer_sub_keys"].reshape(16, 128, 128).transpose(2, 0, 1).reshape(128, 16 * 128))
    iota = c_(np.broadcast_to(np.arange(256, dtype=np.float32)[None, :], (128, 256)))
    im5 = [{"x1": r["x1"], "x1T": c_(r["x1"].T), "wq": P["peer_w_q"], "sk": sk, "u": P["peer_u"], "v": P["peer_v"],
            "lg": bc(P["ln2_g"]), "lb": bc(P["ln2_b"]), "iota": iota} for r in r4]
    r5 = _run(nc5, im5)
    out = np.empty_like(x)
    for b in range(B):
        for t in range(2):
            out[b, t * NTOK:(t + 1) * NTOK] = r5[2 * b + t]["out"]
    return out
```

```python
import numpy as np
import concourse.bass as bass
import concourse.mybir as mybir
from concourse.bass_utils import run_bass_kernel_spmd
from contextlib import ExitStack

F32 = mybir.dt.float32
BF16 = mybir.dt.bfloat16
U32 = mybir.dt.uint32
AF = mybir.ActivationFunctionType
ALU = mybir.AluOpType
AX = mybir.AxisListType

SYNC_SAME_ENGINE = True
EPOCH = 10 ** 9
DMA_MAX = 10 ** 6


class Buf:
    __slots__ = ("name", "w", "r")

    def __init__(self, name):
        self.name = name
        self.w = None
        self.r = {}


class DmaSem:
    __slots__ = ("handle", "count", "uid")
    _n = 0

    def __init__(self, handle):
        self.handle = handle
        self.count = 0
        DmaSem._n += 1
        self.uid = DmaSem._n


class Sched:
    ENG = ("sp", "act", "dve", "pool", "pe")

    def __init__(self, nc):
        self.nc = nc
        self.streams = {e: [] for e in self.ENG}
        self.seq = {e: 0 for e in self.ENG}
        self.known = {e: {} for e in self.ENG}
        self.esem = {}
        self.stack = ExitStack()
        self.nsem = 0
        self.ninstr = 0
        self.out_toks = []
        self.pi = 0
        self.pstack = None
        self.all_dmasems = []
        self.free_dmasems = {"hw": [], "sw": []}
        self.phase_tiles = []

    def sbuf(self, name, shape, dtype):
        stk = self.pstack if self.pstack is not None else self.stack
        return stk.enter_context(self.nc.sbuf_tensor(name, list(shape), dtype))

    def phase_begin(self):
        self.pstack = ExitStack()

    def phase_end(self, final=False):
        if final:
            for tok in self.out_toks:
                self._wait("sp", tok)
        self.barrier()
        self.emit_block()
        self.pstack.close()
        self.pstack = None
        for tl in self.phase_tiles:
            for kind, sm in tl._sems.items():
                self.free_dmasems[kind].append(sm)
            tl._sems = {}
        self.phase_tiles = []

    def barrier(self):
        for e in self.ENG:
            for f in self.ENG:
                if f != e and self.seq[f] > 0:
                    self._wait(e, ("eng", f, self.seq[f]))
            for sem in self.all_dmasems:
                if sem.count > 0:
                    self._wait(e, ("dma", sem, 16 * sem.count))

    def emit_block(self):
        nc = self.nc
        with nc.Block() as block:
            for e, deco in (("sp", block.sync), ("act", block.scalar), ("dve", block.vector),
                            ("pool", block.gpsimd), ("pe", block.tensor)):
                stream = self.streams[e]

                def body(eng, stream=stream):
                    for th in stream:
                        th(eng)
                deco(body)
        self.streams = {e: [] for e in self.ENG}

    def psum(self, name, shape, dtype):
        return self.stack.enter_context(self.nc.psum_tensor(name, list(shape), dtype))

    def newsem(self, name):
        self.nsem += 1
        return self.nc.alloc_semaphore(name=f"{name}_{self.nsem}")

    def dmasem(self, name="d", kind="hw"):
        if self.free_dmasems[kind]:
            return self.free_dmasems[kind].pop()
        d = DmaSem(self.newsem(name + kind))
        self.all_dmasems.append(d)
        return d

    def _esem(self, e, epoch):
        k = (e, epoch)
        if k not in self.esem:
            self.esem[k] = self.newsem(f"e_{e}{epoch}")
        return self.esem[k]

    def _wait(self, e, tok):
        if tok is None:
            return
        kind, ident, val = tok
        key = ident if kind == "eng" else ("dma", ident.uid)
        if self.known[e].get(key, 0) >= val:
            return
        self.known[e][key] = val
        if kind == "eng":
            sem = self._esem(ident, (val - 1) // EPOCH)
            v = (val - 1) % EPOCH + 1
        else:
            sem = ident.handle
            v = val
        self.streams[e].append(lambda eng, sem=sem, v=v: eng.wait_ge(sem, v))
        self.ninstr += 1

    def _deps(self, e, reads, writes, nowaw=False):
        strict = SYNC_SAME_ENGINE and e != "pe"
        for b in reads:
            if b.w is not None:
                if not (e == "pe" and b.w[0] == "eng" and b.w[1] == "pe"):
                    self._wait(e, b.w)
        for b in writes:
            if b.w is not None and not nowaw:
                if strict or not (b.w[0] == "eng" and b.w[1] == e):
                    self._wait(e, b.w)
            for tok in b.r.values():
                if strict or not (tok[0] == "eng" and tok[1] == e):
                    self._wait(e, tok)

    def _record(self, tok, reads, writes):
        for b in writes:
            b.w = tok
            b.r = {}
        key = tok[1] if tok[0] == "eng" else ("dma", tok[1].uid)
        for b in reads:
            b.r[key] = tok

    def op(self, e, fn, reads=(), writes=()):
        self._deps(e, reads, writes)
        self.seq[e] += 1
        s = self.seq[e]
        sem = self._esem(e, (s - 1) // EPOCH)
        self.streams[e].append(lambda eng, fn=fn, sem=sem: fn(eng).then_inc(sem, 1))
        self.ninstr += 1
        tok = ("eng", e, s)
        self._record(tok, reads, writes)
        return tok

    def dma(self, q, out, in_, sem, reads=(), writes=(), nowaw=False, fn=None, **kw):
        self._deps(q, reads, writes, nowaw=nowaw)
        sem.count += 1
        assert sem.count < DMA_MAX, "dma sem overflow"
        h = sem.handle
        if fn is None:
            self.streams[q].append(
                lambda eng, out=out, in_=in_, h=h, kw=kw: eng.dma_start(out=out, in_=in_, **kw).then_inc(h, 16))
        else:
            self.streams[q].append(lambda eng, fn=fn, h=h: fn(eng).then_inc(h, 16))
        self.ninstr += 1
        tok = ("dma", sem, 16 * sem.count)
        self._record(tok, reads, writes)
        return tok

    def emit(self):
        for tok in self.out_toks:
            self._wait("sp", tok)
        nc = self.nc
        with nc.Block() as block:
            for e, deco in (("sp", block.sync), ("act", block.scalar), ("dve", block.vector),
                            ("pool", block.gpsimd), ("pe", block.tensor)):
                stream = self.streams[e]

                def body(eng, stream=stream):
                    for th in stream:
                        th(eng)
                deco(body)
        self.stack.close()


class T:
    def __init__(self, S, name, shape, dtype=F32, psum=False):
        self.S = S
        S.ntile = getattr(S, "ntile", 0) + 1
        self.t = (S.psum if psum else S.sbuf)(f"t{S.ntile}_" + name, shape, dtype)
        self.b = Buf(name)
        self._sems = {}
        self.name = name
        if not psum:
            S.phase_tiles.append(self)

    def semq(self, q):
        kind = "sw" if q == "pool" else "hw"
        if kind not in self._sems:
            self._sems[kind] = self.S.dmasem(self.name, kind)
        return self._sems[kind]

    @property
    def sem(self):
        return self.semq("sp")


def ld(S, tl, dst, src, q="sp", nowaw=False):
    return S.dma(q, dst, src, tl.semq(q), writes=[tl.b], nowaw=nowaw)


def st(S, tl, dst, src, q="sp", final=False):
    tok = S.dma(q, dst, src, tl.semq(q), reads=[tl.b])
    if final:
        S.out_toks.append(tok)
    return tok


def dscr(nc, name, shape, dt=F32):
    return nc.dram_tensor(name, list(shape), dt, kind="Internal").ap()


def mk_psum(S, n=8):
    return [T(S, f"ps{i}", [128, 512], F32, psum=True) for i in range(n)]


def nxt(S, PS):
    p = PS[S.pi % len(PS)]
    S.pi += 1
    return p


def din(nc, name, shape, dt=F32):
    return nc.dram_tensor(name, list(shape), dt, kind="ExternalInput").ap()


def dout(nc, name, shape, dt=F32):
    return nc.dram_tensor(name, list(shape), dt, kind="ExternalOutput").ap()


SEQ = 8192
NSC = SEQ // 512
NEG_E = -0.6065306597126334


NTOK = 4096
DN_ALPHA = 2.0 ** 0.25
LN_EPS = 1e-5


def layer_norm_tile(S, Z, OUT, LNG, LNB, stat, mv, rstd):
    for hf in range(2):
        S.op("dve", lambda e, hf=hf: e.bn_stats(stat.t[:, hf * 6:(hf + 1) * 6], Z.t[:, hf * 512:(hf + 1) * 512]),
             reads=[Z.b], writes=[stat.b])
    S.op("dve", lambda e: e.bn_aggr(mv.t[:], stat.t[:]), reads=[stat.b], writes=[mv.b])
    S.op("dve", lambda e: e.tensor_scalar(rstd.t[:], mv.t[:, 1:2], LN_EPS, None, ALU.add), reads=[mv.b], writes=[rstd.b])
    S.op("act", lambda e: e.activation(rstd.t[:], rstd.t[:], AF.Sqrt), reads=[rstd.b], writes=[rstd.b])
    S.op("dve", lambda e: e.reciprocal(rstd.t[:], rstd.t[:]), reads=[rstd.b], writes=[rstd.b])
    S.op("dve", lambda e: e.tensor_scalar(Z.t[:], Z.t[:], mv.t[:, 0:1], rstd.t[:, 0:1], ALU.subtract, ALU.mult),
         reads=[Z.b, mv.b, rstd.b], writes=[Z.b])
    S.op("dve", lambda e: e.tensor_tensor(Z.t[:], Z.t[:], LNG.t[:], ALU.mult), reads=[Z.b, LNG.b], writes=[Z.b])
    S.op("dve", lambda e: e.tensor_tensor(OUT.t[:], Z.t[:], LNB.t[:], ALU.add), reads=[Z.b, LNB.b], writes=[OUT.b])


def phase1(S, PS, D):
    xv = D["xT"].rearrange("(kc p) t -> p kc t", p=128)
    xov = D["xTo"].rearrange("(kc p) t -> p kc t", p=128)
    W = T(S, "W", [128, 8, 1796], BF16); WS = [T(S, f"WS{i}", [128, 1796]) for i in range(2)]
    PC = T(S, "PC", [128, 19]); W2 = T(S, "W2", [128, 256]); G2 = T(S, "G2", [128, 256])
    BD = T(S, "BD", [128, 128]); CM = T(S, "CM", [128, 512]); IDN = T(S, "IDN", [128, 128]); MS = T(S, "MS", [128, 2])
    ONE = T(S, "ONE", [4, 512]); CAR = T(S, "CAR", [4, 1])
    XF = [T(S, f"XF{i}", [128, 8, 512]) for i in range(2)]
    X = [T(S, f"X{i}", [128, 8, 512], BF16) for i in range(2)]

    def load_x(i, src):
        xf = XF[i % 2]; xt = X[i % 2]
        ld(S, xf, xf.t[:], src)
        S.op("pool", lambda e: e.tensor_copy(xt.t[:], xf.t[:]), reads=[xf.b], writes=[xt.b])
        return xt
    ld(S, BD, BD.t[:], D["bd"]); ld(S, CM, CM.t[:], D["cm"]); ld(S, IDN, IDN.t[:], D["idn"]); ld(S, MS, MS.t[:], D["msel"])
    S.op("dve", lambda e: e.memset(ONE.t[:], 1.0), writes=[ONE.b])
    Pn = ["r0", "r1", "k0", "k1", "v0", "v1", "l", "g"]
    P = {n: T(S, "P" + n, [128, 513]) for n in Pn}
    SH = {n: T(S, "SH" + n, [128, 512]) for n in Pn}
    tmp = T(S, "tmp", [128, 512])
    FO = [T(S, f"FO{i}", [128, 512], BF16) for i in range(2)]
    TME = [T(S, f"TME{i}", [128, 512]) for i in range(2)]
    FL = T(S, "FL", [4, 512]); FC = T(S, "FCt", [4, 512]); FN = T(S, "FNt", [4, 512])
    SP3 = T(S, "SP3", [8, 3, 512], BF16); SPR = T(S, "SPR", [8, 512])
    TH = T(S, "TH", [64, 512]); SGL = T(S, "SGL", [128, 512])
    nm = ["lw", "a", "g", "t1", "sq", "nr", "kk", "t2", "kmod", "cs", "en", "ep", "d2", "epv",
          "alpha", "t3", "beta", "nbeta", "kappa", "rho", "pr", "bo"]
    R = {n: T(S, "R" + n, [128, 512]) for n in nm}
    GC = T(S, "GC", [128, 8])
    pcmap = {"r0": 0, "r1": 1, "k0": 2, "k1": 3, "v0": 4, "v1": 5, "l": 6, "g": 7}
    blkmap = {"r0": 6, "r1": 7, "k0": 8, "k1": 9, "v0": 10, "v1": 11, "l": 12, "g": 13}
    bfc = Buf("fc_scr")
    cnt = {"fo": 0, "tme": 0}

    def mm_block(xt, col0, ncol):
        ps = nxt(S, PS)
        for kc in range(8):
            S.op("pe", lambda e, kc=kc: e.matmul(ps.t[0:ncol, :], W.t[:, kc, col0:col0 + ncol], xt.t[:, kc, :],
                                                 start=(kc == 0), stop=(kc == 7)), reads=[W.b, xt.b], writes=[ps.b])
        return ps

    def split3(src, n, dst_ap):
        S.op("dve", lambda e: e.tensor_copy(SP3.t[0:n, 0, :], src.t[0:n, :]), reads=[src.b], writes=[SP3.b])
        S.op("dve", lambda e: e.tensor_tensor(SPR.t[0:n, :], src.t[0:n, :], SP3.t[0:n, 0, :], ALU.subtract), reads=[src.b, SP3.b], writes=[SPR.b])
        S.op("dve", lambda e: e.tensor_copy(SP3.t[0:n, 1, :], SPR.t[0:n, :]), reads=[SPR.b], writes=[SP3.b])
        S.op("dve", lambda e: e.tensor_tensor(SPR.t[0:n, :], SPR.t[0:n, :], SP3.t[0:n, 1, :], ALU.subtract), reads=[SPR.b, SP3.b], writes=[SPR.b])
        S.op("dve", lambda e: e.tensor_copy(SP3.t[0:n, 2, :], SPR.t[0:n, :]), reads=[SPR.b], writes=[SP3.b])
        st(S, SP3, dst_ap, SP3.t[0:n, :, :])

    def group(g):
        Wv = D["Wg"][g].rearrange("(kc p) c -> p kc c", p=128)
        for kc in range(8):
            def wl(kc=kc):
                ws = WS[kc % 2]
                ld(S, ws, ws.t[:], Wv[:, kc, :])
                S.op("pool", lambda e: e.tensor_copy(W.t[:, kc, :], ws.t[:]), reads=[ws.b], writes=[W.b])
            wl()
        ld(S, PC, PC.t[:], D["pcol"][g]); ld(S, W2, W2.t[:], D["w2a2"][g]); ld(S, G2, G2.t[:], D["g2"][g])
        S.op("dve", lambda e: e.memset(CAR.t[:], 0.0), writes=[CAR.b])
        for n in Pn:
            S.op("dve", lambda e, n=n: e.memset(P[n].t[:, 0:1], 0.0), writes=[P[n].b])
        pc = lambda j: PC.t[:, j:j + 1]

        def superchunk(sc):
            tsl = slice(sc * 512, (sc + 1) * 512)
            xt = load_x(sc, xv[:, :, tsl])

            def foxk(blk):
                ps = mm_block(xt, blk * 128, 128)
                fo = FO[cnt["fo"] % 2]; cnt["fo"] += 1
                S.op("act", lambda e: e.activation(fo.t[:], ps.t[:], AF.Copy), reads=[ps.b], writes=[fo.b])
                r0 = g * 256 + (blk % 2) * 128
                st(S, fo, D["fk"][r0:r0 + 128, tsl], fo.t[:])
            foxk(2); foxk(3)

            def foxv(pair):
                ps = nxt(S, PS)
                for j in range(2):
                    tt = pair * 2 + j
                    for kc in range(8):
                        S.op("pe", lambda e, j=j, tt=tt, kc=kc: e.matmul(ps.t[:, j * 256:(j + 1) * 256], xt.t[:, kc, tt * 128:(tt + 1) * 128],
                                                                         W.t[:, kc, 512:768], start=(kc == 0), stop=(kc == 7)),
                             reads=[W.b, xt.b], writes=[ps.b])
                fo = FO[cnt["fo"] % 2]; cnt["fo"] += 1
                S.op("act", lambda e: e.activation(fo.t[:], ps.t[:], AF.Copy), reads=[ps.b], writes=[fo.b])
                r0 = sc * 512 + pair * 256
                st(S, fo, D["fvtm"][r0:r0 + 256, g * 256:(g + 1) * 256].rearrange("(j p) c -> p j c", p=128),
                   fo.t[:].rearrange("p (j c) -> p j c", c=256))
            foxv(0); foxv(1)
            ps = mm_block(xt, 1792, 4)
            S.op("act", lambda e: e.activation(FL.t[:], ps.t[0:4, :], AF.Sigmoid, bias=PC.t[0:4, 18:19]), reads=[ps.b, PC.b], writes=[FL.b])
            S.op("act", lambda e: e.activation(FL.t[:], FL.t[:], AF.Ln), reads=[FL.b], writes=[FL.b])
            S.op("dve", lambda e: e.tensor_tensor_scan(FC.t[:], ONE.t[:], FL.t[:], CAR.t[:, 0:1], ALU.mult, ALU.add),
                 reads=[ONE.b, FL.b, CAR.b], writes=[FC.b])
            S.op("dve", lambda e: e.tensor_copy(CAR.t[:], FC.t[:, 511:512]), reads=[FC.b], writes=[CAR.b])
            S.op("dve", lambda e: e.tensor_scalar(FN.t[:], FC.t[:], -1.0, None, ALU.mult), reads=[FC.b], writes=[FN.b])
            S.dma("sp", D["fc"][g * 4:(g + 1) * 4, tsl], FC.t[:], FC.sem, reads=[FC.b], writes=[bfc], nowaw=True)
            split3(FN, 4, D["fnc3"][g * 4:(g + 1) * 4, :, tsl])

            def proj_shift(n):
                ps = mm_block(xt, blkmap[n] * 128, 128)
                p = P[n]; sh = SH[n]; mu = pc(pcmap[n])
                S.op("act", lambda e: e.activation(p.t[:, 1:513], ps.t[:], AF.Copy), reads=[ps.b], writes=[p.b])
                S.op("dve", lambda e: e.tensor_tensor(tmp.t[:], p.t[:, 0:512], p.t[:, 1:513], ALU.subtract), reads=[p.b], writes=[tmp.b])
                S.op("dve", lambda e: e.scalar_tensor_tensor(sh.t[:], tmp.t[:], mu, p.t[:, 1:513], ALU.mult, ALU.add),
                     reads=[p.b, tmp.b, PC.b], writes=[sh.b])
                S.op("dve", lambda e: e.tensor_copy(p.t[:, 0:1], p.t[:, 512:513]), reads=[p.b], writes=[p.b])
            for n in Pn:
                proj_shift(n)
            S.op("act", lambda e: e.activation(TH.t[:], SH["l"].t[0:64, :], AF.Tanh), reads=[SH["l"].b], writes=[TH.b])
            S.op("act", lambda e: e.activation(SGL.t[:], SH["g"].t[:], AF.Sigmoid), reads=[SH["g"].b], writes=[SGL.b])

            def rwkv_block(b):
                cs_ = slice(b * 128, b * 128 + 128)
                shr, shk, shv = SH[f"r{b}"], SH[f"k{b}"], SH[f"v{b}"]

                def mm1(lhs, rhs, reads):
                    ps = nxt(S, PS)
                    S.op("pe", lambda e: e.matmul(ps.t[:], lhs, rhs, start=True, stop=True), reads=reads, writes=[ps.b])
                    return ps

                def act(dst, src, func, rd, **kw):
                    S.op("act", lambda e: e.activation(dst.t[:], src, func, **kw), reads=rd, writes=[dst.b])

                def tt(dst, a, b_, op):
                    S.op("dve", lambda e: e.tensor_tensor(dst.t[:], a.t[:], b_.t[:], op), reads=[a.b, b_.b], writes=[dst.b])

                def ts(dst, a, s1, s2, op0, op1=None, extra=()):
                    if op1 is None:
                        S.op("dve", lambda e: e.tensor_scalar(dst.t[:], a.t[:], s1, s2, op0), reads=[a.b, *extra], writes=[dst.b])
                    else:
                        S.op("dve", lambda e: e.tensor_scalar(dst.t[:], a.t[:], s1, s2, op0, op1), reads=[a.b, *extra], writes=[dst.b])
                ps = mm1(W2.t[0:64, cs_], TH.t[:], [W2.b, TH.b])
                act(R["lw"], ps.t[:], AF.Sigmoid, [ps.b, PC.b], bias=pc(8 + b))
                ts(R["lw"], R["lw"], NEG_E, None, ALU.mult)
                ps = mm1(W2.t[64:128, cs_], SH["l"].t[64:128, :], [W2.b, SH["l"].b])
                act(R["a"], ps.t[:], AF.Sigmoid, [ps.b, PC.b], bias=pc(10 + b))
                ps = mm1(G2.t[:, cs_], SGL.t[:], [G2.b, SGL.b])
                act(R["g"], ps.t[:], AF.Copy, [ps.b])
                ts(R["t1"], shk, pc(12 + b), None, ALU.mult, extra=[PC.b])
                tt(R["sq"], R["t1"], R["t1"], ALU.mult)
                ps = mm1(BD.t[:], R["sq"].t[:], [BD.b, R["sq"].b])
                act(R["nr"], ps.t[:], AF.Sqrt, [ps.b])
                ts(R["nr"], R["nr"], 1e-12, None, ALU.max)
                S.op("dve", lambda e: e.reciprocal(R["nr"].t[:], R["nr"].t[:]), reads=[R["nr"].b], writes=[R["nr"].b])
                tt(R["kk"], R["t1"], R["nr"], ALU.mult)
                ts(R["t2"], R["a"], -1.0, pc(14 + b), ALU.add, ALU.mult, extra=[PC.b])
                S.op("dve", lambda e: e.scalar_tensor_tensor(R["kmod"].t[:], R["t2"].t[:], 1.0, shk.t[:], ALU.add, ALU.mult),
                     reads=[R["t2"].b, shk.b], writes=[R["kmod"].b])
                S.op("dve", lambda e: e.tensor_tensor_scan(R["cs"].t[:], CM.t[:], R["lw"].t[:], 0.0, ALU.mult, ALU.add),
                     reads=[CM.b, R["lw"].b], writes=[R["cs"].b])
                act(R["en"], R["cs"].t[:], AF.Exp, [R["cs"].b], scale=-1.0)
                act(R["ep"], R["cs"].t[:], AF.Exp, [R["cs"].b])
                tt(R["d2"], R["cs"], R["lw"], ALU.subtract)
                act(R["epv"], R["d2"].t[:], AF.Exp, [R["d2"].b])
                tt(R["alpha"], R["kk"], R["epv"], ALU.mult)
                tt(R["t3"], R["kk"], R["a"], ALU.mult)
                tt(R["beta"], R["t3"], R["en"], ALU.mult)
                ts(R["nbeta"], R["beta"], -1.0, None, ALU.mult)
                tt(R["kappa"], R["kmod"], R["en"], ALU.mult)
                tt(R["rho"], shr, R["ep"], ALU.mult)
                S.op("dve", lambda e: e.tensor_copy(GC.t[:], R["ep"].t[:, 63::64]), reads=[R["ep"].b], writes=[GC.b])
                S.op("dve", lambda e: e.scalar_tensor_tensor(R["pr"].t[:], shr.t[:], pc(16 + b), R["kmod"].t[:], ALU.mult, ALU.mult),
                     reads=[shr.b, PC.b, R["kmod"].b], writes=[R["pr"].b])
                ps = mm1(BD.t[:], R["pr"].t[:], [BD.b, R["pr"].b])
                act(R["bo"], ps.t[:], AF.Copy, [ps.b])
                r0 = g * 256 + b * 128
                for on, tl in (("al", R["alpha"]), ("be", R["beta"]), ("ka", R["kappa"]), ("rh", R["rho"])):
                    st(S, tl, D[on][r0:r0 + 128, tsl], tl.t[:])
                st(S, GC, D["gc"][r0:r0 + 128, sc * 8:(sc + 1) * 8], GC.t[:])

                def tmaj(on, tl):
                    ps = nxt(S, PS)
                    for t4 in range(4):
                        S.op("pe", lambda e, t4=t4: e.matmul(ps.t[:, t4 * 128:(t4 + 1) * 128], tl.t[:, t4 * 128:(t4 + 1) * 128], IDN.t[:],
                                                             start=True, stop=True), reads=[tl.b, IDN.b], writes=[ps.b])
                    te = TME[cnt["tme"] % 2]; cnt["tme"] += 1
                    S.op("act", lambda e: e.activation(te.t[:], ps.t[:], AF.Copy), reads=[ps.b], writes=[te.b])
                    st(S, te, D[on][tsl, r0:r0 + 128].rearrange("(t4 p) c -> p t4 c", p=128), te.t[:].rearrange("p (t4 c) -> p t4 c", c=128))
                for on, tl in (("nbe_tm", R["nbeta"]), ("ka_tm", R["kappa"]), ("rv_tm", shv), ("rg_tm", R["g"]), ("bo_tm", R["bo"])):
                    tmaj(on, tl)
            for b in range(2):
                rwkv_block(b)
        for sc in range(NSC):
            superchunk(sc)

        def ownq(i):
            xt = load_x(i, xov[:, :, i * 512:(i + 1) * 512])
            for blk in range(2):
                def one(blk=blk):
                    ps = mm_block(xt, blk * 128, 128)
                    fo = FO[cnt["fo"] % 2]; cnt["fo"] += 1
                    S.op("act", lambda e: e.activation(fo.t[:], ps.t[:], AF.Copy, scale=0.125), reads=[ps.b], writes=[fo.b])
                    r0 = g * 256 + blk * 128
                    st(S, fo, D["fqo"][r0:r0 + 128, i * 512:(i + 1) * 512], fo.t[:])
                one()
        for i in range(NTOK // 512):
            ownq(i)
    for g in range(2):
        group(g)
    FCA = T(S, "FCA", [8, 512]); FCB = T(S, "FCB", [8, 512])

    def blend(i):
        S.dma("sp", FCA.t[:], D["fc"][:, (2 * i) * 512:(2 * i + 1) * 512], FCA.sem, reads=[bfc], writes=[FCA.b])
        S.dma("sp", FCB.t[:], D["fc"][:, (2 * i + 1) * 512:(2 * i + 2) * 512], FCB.sem, reads=[bfc], writes=[FCB.b])
        S.op("dve", lambda e: e.tensor_scalar(FCA.t[:], FCA.t[:], MS.t[0:8, 0:1], None, ALU.mult), reads=[FCA.b, MS.b], writes=[FCA.b])
        S.op("dve", lambda e: e.scalar_tensor_tensor(FCA.t[:], FCB.t[:], MS.t[0:8, 1:2], FCA.t[:], ALU.mult, ALU.add),
             reads=[FCA.b, FCB.b, MS.b], writes=[FCA.b])
        split3(FCA, 8, D["co3"][:, :, i * 512:(i + 1) * 512])
    for i in range(NTOK // 512):
        blend(i)


def phase2(S, PS, D):
    SCP = PS[0:2]; ACC = PS[2:4]
    QA = T(S, "QA", [70, NTOK], BF16); KA = T(S, "KA", [70, SEQ], BF16); VO = T(S, "VO", [128, 64, 128], BF16)
    NEG = T(S, "NEG", [128, 8, 512])
    PT = [T(S, f"PT{i}", [128, 512], BF16) for i in range(3)]
    TMP = [T(S, f"TMPm{i}", [128, 512]) for i in range(2)]
    RD = T(S, "RD", [128, 512]); Y = T(S, "Y", [64, 512], BF16)
    ld(S, NEG, NEG.t[:], D["neg"].rearrange("j p q -> p j q"))
    S.op("dve", lambda e: e.memset(QA.t[64:70, :], 1.0), writes=[QA.b])
    S.op("dve", lambda e: e.memset(KA.t[64:70, :], 1.0), writes=[KA.b])
    S.op("dve", lambda e: e.memset(VO.t[:, :, 64:128], 1.0), writes=[VO.b])
    cnt = {"pi": 0, "pt": 0, "tm": 0}

    def head(hh):
        r = slice(hh * 64, (hh + 1) * 64)
        for i in range(2):
            sl = slice(i * 2048, (i + 1) * 2048)
            ld(S, QA, QA.t[0:64, sl], D["fqo"][r, sl], nowaw=(i > 0))
        ld(S, QA, QA.t[64:67, :], D["co3"][hh], nowaw=True)
        for i in range(4):
            sl = slice(i * 2048, (i + 1) * 2048)
            ld(S, KA, KA.t[0:64, sl], D["fk"][r, sl], nowaw=(i > 0))
        ld(S, KA, KA.t[67:70, :], D["fnc3"][hh], nowaw=True)
        ld(S, VO, VO.t[:, :, 0:64], D["fvtm"][:, r].rearrange("(kb p) d -> p kb d", p=128), nowaw=True)

        def qchunk(i):
            acc = ACC[i % 2]
            nkb = 8 * i + 8

            def scores(kb):
                j = kb - 8 * i
                ps = SCP[cnt["pi"] % 2]; cnt["pi"] += 1
                pt = PT[cnt["pt"] % 3]; cnt["pt"] += 1
                S.op("pe", lambda e: e.matmul(ps.t[:], KA.t[:, kb * 128:(kb + 1) * 128], QA.t[:, i * 512:(i + 1) * 512], start=True, stop=True),
                     reads=[KA.b, QA.b], writes=[ps.b])
                if j >= 0:
                    tm = TMP[cnt["tm"] % 2]; cnt["tm"] += 1
                    S.op("dve", lambda e: e.tensor_tensor(tm.t[:], ps.t[:], NEG.t[:, j, :], ALU.add), reads=[ps.b, NEG.b], writes=[tm.b])
                    S.op("act", lambda e: e.activation(pt.t[:], tm.t[:], AF.Exp), reads=[tm.b], writes=[pt.b])
                else:
                    S.op("act", lambda e: e.activation(pt.t[:], ps.t[:], AF.Exp), reads=[ps.b], writes=[pt.b])
                return kb, pt

            def pv(kb, pt):
                S.op("pe", lambda e: e.matmul(acc.t[:], VO.t[:, kb, :], pt.t[:], start=(kb == 0), stop=(kb == nkb - 1)),
                     reads=[VO.b, pt.b], writes=[acc.b])
            prev = None
            for kb in range(nkb):
                cur = scores(kb)
                if prev is not None:
                    pv(*prev)
                prev = cur
                yield
            pv(*prev)
            S.op("dve", lambda e: e.reciprocal(RD.t[64:128, :], acc.t[64:128, :]), reads=[acc.b], writes=[RD.b])
            S.op("dve", lambda e: e.tensor_tensor(Y.t[:], acc.t[0:64, :], RD.t[64:128, :], ALU.mult), reads=[acc.b, RD.b], writes=[Y.b])
            st(S, Y, D["yfo"][r, i * 512:(i + 1) * 512], Y.t[:])
        for i in range(NTOK // 512):
            yield from qchunk(i)
    for hh in range(8):
        yield from head(hh)


GN_EPS = 64e-5


def phase3(S, PS, D):
    MK = [T(S, f"MK{i}", [128, 512]) for i in range(5)]
    for i in range(5):
        ld(S, MK[i], MK[i].t[0:64, :], D["mk"][i])
        ld(S, MK[i], MK[i].t[64:128, :], D["mk"][i], nowaw=True)
    MSL, MSU, MIU, NMIU, ID8 = MK
    FM = [[T(S, f"FM{p}_{i}", [128, 512]) for i in range(4)] for p in range(2)]
    TM = [[T(S, f"TM{p}_{i}", [128, 8, 64]) for i in range(5)] for p in range(2)]
    GC = T(S, "GC3", [128, 128]); LG = T(S, "LG", [128, 64]); LB = T(S, "LB", [128, 64])
    mk3 = lambda n: T(S, n, [128, 8, 64])
    A = mk3("A"); AT = mk3("AT"); Wt = [mk3("W0"), mk3("W1")]; Pt = [mk3("P0"), mk3("P1")]; PTt = [mk3("PT0"), mk3("PT1")]
    AakT = mk3("AakT"); nArbT = mk3("nArbT"); ArkT = mk3("ArkT")
    ST = [T(S, "ST0", [128, 64]), T(S, "ST1", [128, 64])]
    STg = T(S, "STg", [128, 64]); RHS = T(S, "RHS", [128, 64]); US = T(S, "US", [128, 64])
    YO = [mk3("YO0"), mk3("YO1")]; YT = [T(S, "YT0", [128, 512], BF16), T(S, "YT1", [128, 512], BF16)]
    stat = T(S, "stat3", [128, 6]); mv = T(S, "mv3", [128, 2]); rstd = T(S, "rstd3", [128, 1]); yn = T(S, "yn", [128, 64]); bt = T(S, "bt", [128, 64])
    cnt = {"st": 0, "cast": 0}
    fmn = ("al", "be", "ka", "rh"); tmn = ("nbe_tm", "ka_tm", "rv_tm", "rg_tm", "bo_tm")
    HS = (slice(0, 64), slice(64, 128))
    CF = [T(S, f"CF{i}", [128, 4096]) for i in range(2)]; CB = [T(S, f"CB{i}", [128, 4096], BF16) for i in range(2)]
    NCH = 16384 * 1024 // 128 // 4096

    def cast_step():
        k = cnt["cast"]
        if k >= 2 * NCH:
            return
        cnt["cast"] += 1
        src = D["u" if k < NCH else "v"].rearrange("(p r) d -> p (r d)", p=128)
        dst = D["uvb"].rearrange("(p r) (two d) -> p r two d", p=128, two=2)
        c = k % NCH
        cf = CF[k % 2]; cb = CB[k % 2]
        ld(S, cf, cf.t[:], src[:, c * 4096:(c + 1) * 4096], q="pool")
        S.op("pool", lambda e: e.tensor_copy(cb.t[:], cf.t[:]), reads=[cf.b], writes=[cb.b])
        st(S, cb, dst[:, c * 4:(c + 1) * 4, 0 if k < NCH else 1, :], cb.t[:].rearrange("p (r d) -> p r d", d=1024), q="pool")

    def mm2(ps, col, lhs, rhs, reads, start=True, stop=True):
        for hs in HS:
            S.op("pe", lambda e, hs=hs: e.matmul(ps.t[hs, col * 64:(col + 1) * 64], lhs(hs), rhs(hs), start=start, stop=stop),
                 reads=reads, writes=[ps.b])

    def batch_mm(lhs_of, rhs_of, reads):
        ps = nxt(S, PS)
        for j in range(8):
            mm2(ps, j, (lambda hs, j=j: lhs_of(j, hs)), (lambda hs, j=j: rhs_of(j, hs)), reads)
        return ps

    def pair(hp):
        hh = 2 * hp
        r = slice(hh * 64, (hh + 2) * 64)
        ld(S, GC, GC.t[:], D["gc"][r, :])
        ld(S, LG, LG.t[:], D["lg3"][hh:hh + 2].rearrange("h p c -> (h p) c"))
        ld(S, LB, LB.t[:], D["lb3"][hh:hh + 2].rearrange("h p c -> (h p) c"))
        s0 = ST[cnt["st"] % 2]
        S.op("dve", lambda e: e.memset(s0.t[:], 0.0), writes=[s0.b])

        def superchunk(sc):
            fm = FM[sc % 2]; tm = TM[sc % 2]
            tsl = slice(sc * 512, (sc + 1) * 512)
            for i in range(4):
                ld(S, fm[i], fm[i].t[:], D[fmn[i]][r, tsl])
            for i in range(5):
                for k_, hs in enumerate(HS):
                    rr = slice((hh + k_) * 64, (hh + k_ + 1) * 64)
                    ld(S, tm[i], tm[i].t[hs, :, :], D[tmn[i]][tsl, rr].rearrange("(c t) k -> t c k", t=64), nowaw=(k_ > 0))
            alT, beT, kaT, rhT = fm
            nbe, ka, vm, gt, bo = tm
            fsl = lambda t_, j, hs: t_.t[hs, j * 64:(j + 1) * 64]
            f3 = lambda t_, j, hs: t_.t[hs, j, :]
            flat = lambda t_: t_.t[:].rearrange("p a b -> p (a b)")

            def evac_mask(dst, ps, mk):
                S.op("dve", lambda e: e.tensor_tensor(flat(dst), ps.t[:], mk.t[:], ALU.mult), reads=[ps.b, mk.b], writes=[dst.b])

            def evac_copy(dst, ps):
                S.op("act", lambda e: e.activation(flat(dst), ps.t[:], AF.Copy), reads=[ps.b], writes=[dst.b])

            def evac_add(dst, ps, src):
                S.op("dve", lambda e: e.tensor_tensor(flat(dst), ps.t[:], flat(src), ALU.add), reads=[ps.b, src.b], writes=[dst.b])

            def bmm3(l, r_):
                return batch_mm(lambda j, hs: f3(l, j, hs), lambda j, hs: f3(r_, j, hs), [l.b, r_.b])

            def bmmf(l, r_):
                return batch_mm(lambda j, hs: fsl(l, j, hs), lambda j, hs: fsl(r_, j, hs), [l.b, r_.b])

            evac_mask(A, bmmf(alT, beT), MSL)
            evac_mask(AT, bmmf(beT, alT), MSU)
            W0 = Wt[0]
            S.op("dve", lambda e: e.tensor_tensor(flat(W0), ID8.t[:], flat(AT), ALU.subtract), reads=[ID8.b, AT.b], writes=[W0.b])
            evac_copy(Pt[0], bmm3(AT, A))
            evac_copy(PTt[0], bmm3(A, AT))
            for i in range(5):
                Wc, Pc, PTc = Wt[i % 2], Pt[i % 2], PTt[i % 2]
                Wn, Pn_, PTn = Wt[(i + 1) % 2], Pt[(i + 1) % 2], PTt[(i + 1) % 2]
                evac_add(Wn, bmm3(Pc, Wc), Wc)
                if i < 4:
                    evac_copy(Pn_, bmm3(PTc, Pc))
                    evac_copy(PTn, bmm3(Pc, PTc))
            Wf = Wt[5 % 2]
            evac_mask(AakT, bmmf(kaT, alT), MSU)
            evac_mask(nArbT, bmmf(beT, rhT), NMIU)
            evac_mask(ArkT, bmmf(kaT, rhT), MIU)
            yo = YO[sc % 2]; yt = YT[sc % 2]
            yield

            def chunk(j):
                c = sc * 8 + j
                st0 = ST[cnt["st"] % 2]; st1 = ST[(cnt["st"] + 1) % 2]; cnt["st"] += 1
                gcol = GC.t[:, c:c + 1]
                sT = lambda t_: (lambda hs: t_.t[hs, :])
                psr = nxt(S, PS)
                mm2(psr, 0, lambda hs: fsl(alT, j, hs), sT(st0), [alT.b, st0.b], True, False)
                mm2(psr, 0, lambda hs: f3(AakT, j, hs), lambda hs: f3(vm, j, hs), [AakT.b, vm.b], False, True)
                S.op("act", lambda e: e.activation(RHS.t[:], psr.t[:, 0:64], AF.Copy), reads=[psr.b], writes=[RHS.b])
                S.op("dve", lambda e: e.tensor_scalar(STg.t[:], st0.t[:], gcol, None, ALU.mult), reads=[st0.b, GC.b], writes=[STg.b])
                psu = nxt(S, PS)
                mm2(psu, 0, lambda hs: f3(Wf, j, hs), sT(RHS), [Wf.b, RHS.b])
                S.op("act", lambda e: e.activation(US.t[:], psu.t[:, 0:64], AF.Copy), reads=[psu.b], writes=[US.b])
                psy = nxt(S, PS)
                mm2(psy, 0, lambda hs: fsl(rhT, j, hs), sT(st0), [rhT.b, st0.b], True, False)
                mm2(psy, 0, lambda hs: f3(nArbT, j, hs), sT(US), [nArbT.b, US.b], False, False)
                mm2(psy, 0, lambda hs: f3(ArkT, j, hs), lambda hs: f3(vm, j, hs), [ArkT.b, vm.b], False, True)
                psd = nxt(S, PS)
                mm2(psd, 0, lambda hs: f3(nbe, j, hs), sT(US), [nbe.b, US.b], True, False)
                mm2(psd, 0, lambda hs: f3(ka, j, hs), lambda hs: f3(vm, j, hs), [ka.b, vm.b], False, True)
                S.op("dve", lambda e: e.scalar_tensor_tensor(st1.t[:], psd.t[:, 0:64], gcol, STg.t[:], ALU.mult, ALU.add),
                     reads=[psd.b, GC.b, STg.b], writes=[st1.b])
                py = psy.t[:, 0:64]
                S.op("dve", lambda e: e.bn_stats(stat.t[:], py), reads=[psy.b], writes=[stat.b])
                S.op("dve", lambda e: e.bn_aggr(mv.t[:], stat.t[:]), reads=[stat.b], writes=[mv.b])
                S.op("dve", lambda e: e.tensor_scalar(rstd.t[:], mv.t[:, 1:2], GN_EPS, None, ALU.add), reads=[mv.b], writes=[rstd.b])
                S.op("act", lambda e: e.activation(rstd.t[:], rstd.t[:], AF.Sqrt), reads=[rstd.b], writes=[rstd.b])
                S.op("dve", lambda e: e.reciprocal(rstd.t[:], rstd.t[:]), reads=[rstd.b], writes=[rstd.b])
                S.op("dve", lambda e: e.tensor_scalar(yn.t[:], py, mv.t[:, 0:1], rstd.t[:, 0:1], ALU.subtract, ALU.mult),
                     reads=[psy.b, mv.b, rstd.b], writes=[yn.b])
                S.op("dve", lambda e: e.tensor_tensor(yn.t[:], yn.t[:], LG.t[:], ALU.mult), reads=[yn.b, LG.b], writes=[yn.b])
                S.op("dve", lambda e: e.tensor_tensor(yn.t[:], yn.t[:], LB.t[:], ALU.add), reads=[yn.b, LB.b], writes=[yn.b])
                S.op("dve", lambda e: e.tensor_tensor(bt.t[:], bo.t[:, j, :], vm.t[:, j, :], ALU.mult), reads=[bo.b, vm.b], writes=[bt.b])
                S.op("dve", lambda e: e.tensor_tensor(yn.t[:], yn.t[:], bt.t[:], ALU.add), reads=[yn.b, bt.b], writes=[yn.b])
                S.op("dve", lambda e: e.tensor_tensor(yo.t[:, j, :], yn.t[:], gt.t[:, j, :], ALU.mult), reads=[yn.b, gt.b], writes=[yo.b])
            for j in range(8):
                chunk(j)
                yield
            ps = batch_mm(lambda j, hs: f3(yo, j, hs), lambda j, hs: ID8.t[hs, 0:64], [yo.b, ID8.b])
            S.op("act", lambda e: e.activation(yt.t[:], ps.t[:], AF.Copy), reads=[ps.b], writes=[yt.b])
            st(S, yt, D["yr"][r, tsl], yt.t[:])
            cast_step(); cast_step()
        for sc in range(NSC):
            yield from superchunk(sc)
    for hp in range(4):
        yield from pair(hp)
    while cnt["cast"] < 2 * NCH:
        cast_step()


def phase4(S, PS, D):
    WG = T(S, "WG", [128, 8, 1024], BF16); PA = T(S, "PA", [128, 4, 1024], BF16); PB = T(S, "PB", [128, 4, 1024], BF16)
    WO = T(S, "WO", [128, 8, 1024], BF16); WST = [T(S, f"WST{i}", [128, 1024]) for i in range(2)]
    LNG = T(S, "LNG", [128, 1024]); LNB = T(S, "LNB", [128, 1024]); MS = T(S, "MS4", [128, 2]); IDN = T(S, "IDN4", [128, 128])
    wgv = D["wg"].rearrange("(k p) c -> p k c", p=128)
    cw = {"n": 0}

    def ldw(dst, dslice, src):
        ws = WST[cw["n"] % 2]; cw["n"] += 1
        ld(S, ws, ws.t[:], src)
        S.op("pool", lambda e: e.tensor_copy(dslice, ws.t[:]), reads=[ws.b], writes=[dst.b])
    for kc in range(8):
        ldw(WO, WO.t[:, kc, :], D["wo"].rearrange("(k p) c -> p k c", p=128)[:, kc, :])
    for kc in range(4):
        ldw(PA, PA.t[:, kc, :], D["pa"].rearrange("(k p) c -> p k c", p=128)[:, kc, :])
        ldw(PB, PB.t[:, kc, :], D["pb"].rearrange("(k p) c -> p k c", p=128)[:, kc, :])
    ld(S, LNG, LNG.t[:], D["lg1"]); ld(S, LNB, LNB.t[:], D["lb1"]); ld(S, MS, MS.t[:], D["msel"]); ld(S, IDN, IDN.t[:], D["idn"])
    XTF = T(S, "XTF", [128, 8, 512]); XT = T(S, "XT", [128, 8, 512], BF16)
    YF = T(S, "YF", [128, 4, 512], BF16); YR = T(S, "YR", [128, 4, 512], BF16); YRb = T(S, "YRb", [128, 4, 512], BF16)
    MT = T(S, "MT", [128, 8, 512], BF16); SG = T(S, "SG", [128, 512])
    XR = T(S, "XR", [128, 1024]); Z = T(S, "Z", [128, 1024]); X1 = T(S, "X14", [128, 1024]); TP = T(S, "TP", [128, 512])
    stat = T(S, "stat4", [128, 12]); mv = T(S, "mv4", [128, 2]); rstd = T(S, "rstd4", [128, 1])
    yrv = D["yr"].rearrange("(k p) t -> p k t", p=128)

    def superchunk(i):
        tsl = slice(i * 512, (i + 1) * 512)
        ld(S, XTF, XTF.t[:], D["xTo"].rearrange("(k p) t -> p k t", p=128)[:, :, tsl])
        S.op("pool", lambda e: e.tensor_copy(XT.t[:], XTF.t[:]), reads=[XTF.b], writes=[XT.b])
        ld(S, YF, YF.t[:], D["yfo"].rearrange("(k p) t -> p k t", p=128)[:, :, tsl])
        ld(S, YR, YR.t[:], yrv[:, :, (2 * i) * 512:(2 * i + 1) * 512])
        ld(S, YRb, YRb.t[:], yrv[:, :, (2 * i + 1) * 512:(2 * i + 2) * 512])
        fl = lambda t_: t_.t[:].rearrange("p a b -> p (a b)")
        S.op("dve", lambda e: e.tensor_scalar(fl(YR), fl(YR), MS.t[:, 0:1], None, ALU.mult), reads=[YR.b, MS.b], writes=[YR.b])
        S.op("dve", lambda e: e.scalar_tensor_tensor(fl(YR), fl(YRb), MS.t[:, 1:2], fl(YR), ALU.mult, ALU.add),
             reads=[YR.b, YRb.b, MS.b], writes=[YR.b])

        def branch(goff, PW, Y, first):
            for kc in range(8):
                ldw(WG, WG.t[:, kc, :], wgv[:, kc, goff:goff + 1024])

            def nblock(nb):
                psg = nxt(S, PS)
                for kc in range(8):
                    S.op("pe", lambda e, kc=kc: e.matmul(psg.t[:], WG.t[:, kc, nb * 128:nb * 128 + 128], XT.t[:, kc, :],
                                                         start=(kc == 0), stop=(kc == 7)), reads=[WG.b, XT.b], writes=[psg.b])
                S.op("act", lambda e: e.activation(SG.t[:], psg.t[:], AF.Sigmoid), reads=[psg.b], writes=[SG.b])
                psz = nxt(S, PS)
                for kc in range(4):
                    S.op("pe", lambda e, kc=kc: e.matmul(psz.t[:], PW.t[:, kc, nb * 128:nb * 128 + 128], Y.t[:, kc, :],
                                                         start=(kc == 0), stop=(kc == 3)), reads=[PW.b, Y.b], writes=[psz.b])
                if first:
                    S.op("dve", lambda e: e.tensor_tensor(MT.t[:, nb, :], psz.t[:], SG.t[:], ALU.mult), reads=[psz.b, SG.b], writes=[MT.b])
                else:
                    S.op("dve", lambda e: e.tensor_tensor(SG.t[:], psz.t[:], SG.t[:], ALU.mult), reads=[psz.b, SG.b], writes=[SG.b])
                    S.op("dve", lambda e: e.tensor_tensor(MT.t[:, nb, :], MT.t[:, nb, :], SG.t[:], ALU.add), reads=[MT.b, SG.b], writes=[MT.b])
            for nb in range(8):
                nblock(nb)
        branch(0, PA, YF, True)
        branch(1024, PB, YR, False)

        def ttile(tt):
            r0 = i * 512 + tt * 128
            ld(S, XR, XR.t[:], D["xo"][r0:r0 + 128, :])

            def half(hf):
                ps = nxt(S, PS)
                for nb in range(8):
                    S.op("pe", lambda e, nb=nb: e.matmul(ps.t[:], MT.t[:, nb, tt * 128:(tt + 1) * 128], WO.t[:, nb, hf * 512:(hf + 1) * 512],
                                                         start=(nb == 0), stop=(nb == 7)), reads=[MT.b, WO.b], writes=[ps.b])
                S.op("dve", lambda e: e.scalar_tensor_tensor(Z.t[:, hf * 512:(hf + 1) * 512], XR.t[:, hf * 512:(hf + 1) * 512], DN_ALPHA,
                                                             ps.t[:], ALU.mult, ALU.add), reads=[XR.b, ps.b], writes=[Z.b])
            half(0); half(1)
            layer_norm_tile(S, Z, X1, LNG, LNB, stat, mv, rstd)
            st(S, X1, D["x1"][r0:r0 + 128, :], X1.t[:])

            def tgroup(gq):
                ps = nxt(S, PS)
                for bi in range(4):
                    kc = gq * 4 + bi
                    S.op("pe", lambda e, bi=bi, kc=kc: e.matmul(ps.t[:, bi * 128:(bi + 1) * 128], X1.t[:, kc * 128:(kc + 1) * 128], IDN.t[:],
                                                                start=True, stop=True), reads=[X1.b, IDN.b], writes=[ps.b])
                S.op("act", lambda e: e.activation(TP.t[:], ps.t[:], AF.Copy), reads=[ps.b], writes=[TP.b])
                st(S, TP, D["x1T"][gq * 512:(gq + 1) * 512, r0:r0 + 128].rearrange("(bi p) t -> p bi t", p=128),
                   TP.t[:].rearrange("p (bi t) -> p bi t", t=128))
            tgroup(0); tgroup(1)
        for tt in range(4):
            ttile(tt)
    for i in range(NTOK // 512):
        superchunk(i)


def phase5(S, PS, D, ntile=NTOK // 128):
    x1d = D["x1"]; x1Td = D["x1T"]; wqd = D["wq"]; skd = D["sk"]
    lgd = D["lg2"]; lbd = D["lb2"]; iod = D["iota"]; od = D["out"]
    WQ = T(S, "WQ", [128, 8, 2048]); SK = T(S, "SK", [128, 16, 128])
    LNG = T(S, "LNG", [128, 1024]); LNB = T(S, "LNB", [128, 1024])
    for kc in range(8):
        ld(S, WQ, WQ.t[:, kc, :], wqd.rearrange("(k p) c -> p k c", p=128)[:, kc, :], nowaw=True)
    ld(S, SK, SK.t[:], skd.rearrange("p (b k) -> p b k", k=128))
    ld(S, LNG, LNG.t[:], lgd); ld(S, LNB, LNB.t[:], lbd)
    IOT = T(S, "IOT", [128, 256]); ld(S, IOT, IOT.t[:], iod)
    BPU = T(S, "BPU", [128, 16], U32); BPF = T(S, "BPF", [128, 16])
    X1 = [T(S, f"X1_{i}", [128, 1024]) for i in range(2)]; X1T = T(S, "X1T", [128, 8, 128])
    QT = T(S, "QT", [128, 16, 128]); SC = T(S, "SC", [128, 16, 128]); SC2 = T(S, "SC2", [128, 128])
    TS = T(S, "TS", [128, 16, 16]); TI = T(S, "TI", [128, 16, 16], U32); TIF = T(S, "TIF", [128, 16, 16]); TI128 = T(S, "TI128", [128, 16, 16])
    CS = T(S, "CS", [128, 256]); CI = T(S, "CI", [128, 256]); CS2 = T(S, "CS2", [128, 256]); JK = T(S, "JK", [128, 256])
    BS = T(S, "BS", [128, 8, 16]); BP = T(S, "BP", [128, 8], U32); IDF = T(S, "IDF", [128, 128]); IDX = [T(S, f"IDX{i}", [128, 128], U32) for i in range(2)]
    NM = T(S, "NM", [128, 8]); EX = T(S, "EX", [128, 8, 16]); SM = T(S, "SM", [128, 8]); GW = [T(S, f"GW{i}", [128, 128]) for i in range(2)]
    UV = [[T(S, f"UV{p}_{i}", [128, 2048], BF16) for i in range(8)] for p in range(2)]
    DG = [T(S, f"DG{i}", [128, 128], BF16) for i in range(4)]; IDB = T(S, "IDB", [128, 128], BF16); IDF32 = T(S, "IDF32", [128, 128])
    ld(S, IDF32, IDF32.t[:], D["idn"])
    S.op("dve", lambda e: e.tensor_copy(IDB.t[:], IDF32.t[:]), reads=[IDF32.b], writes=[IDB.b])
    X1B = [T(S, f"X1B{i}", [128, 1024], BF16) for i in range(2)]
    JK2 = T(S, "JK2", [128, 1024], BF16); H = T(S, "H", [128, 128]); HG = T(S, "HG", [128, 128])
    Z = T(S, "Z", [128, 1024]); OUT = T(S, "OUT", [128, 1024])
    ACCP = PS[6:8]; PS = PS[0:6]; uvd = D["uvb"]
    stat = T(S, "stat5", [128, 12]); mv = T(S, "mv5", [128, 2]); rstd = T(S, "rstd5", [128, 1])
    cnt = {"ub": 0, "dg": 0}

    def front(ti):
        r0 = ti * 128
        X1c, X1Bc, IDXc, GWc = X1[ti % 2], X1B[ti % 2], IDX[ti % 2], GW[ti % 2]
        ld(S, X1c, X1c.t[:], x1d[r0:r0 + 128, :])
        S.op("dve", lambda e: e.tensor_copy(X1Bc.t[:], X1c.t[:]), reads=[X1c.b], writes=[X1Bc.b])
        ld(S, X1T, X1T.t[:], x1Td.rearrange("(k p) t -> p k t", p=128)[:, :, r0:r0 + 128])

        def qgroup(gq):
            ps = nxt(S, PS)
            for bi in range(4):
                blk = gq * 4 + bi
                for kc in range(8):
                    S.op("pe", lambda e, bi=bi, blk=blk, kc=kc: e.matmul(ps.t[:, bi * 128:(bi + 1) * 128], WQ.t[:, kc, blk * 128:(blk + 1) * 128],
                                                                         X1T.t[:, kc, :], start=(kc == 0), stop=(kc == 7)),
                         reads=[WQ.b, X1T.b], writes=[ps.b])
            S.op("act", lambda e: e.activation(QT.t[:, gq * 4:(gq + 1) * 4, :].rearrange("p a b -> p (a b)"), ps.t[:], AF.Copy),
                 reads=[ps.b], writes=[QT.b])
        for gq in range(4):
            qgroup(gq)
            yield

        def sgroup(gq):
            ps = nxt(S, PS)
            for bi in range(4):
                blk = gq * 4 + bi
                S.op("pe", lambda e, bi=bi, blk=blk: e.matmul(ps.t[:, bi * 128:(bi + 1) * 128], QT.t[:, blk, :], SK.t[:, blk, :],
                                                              start=True, stop=True), reads=[QT.b, SK.b], writes=[ps.b])
            S.op("act", lambda e: e.activation(SC.t[:, gq * 4:(gq + 1) * 4, :].rearrange("p a b -> p (a b)"), ps.t[:], AF.Copy),
                 reads=[ps.b], writes=[SC.b])
        for gq in range(4):
            sgroup(gq)
            yield

        def top16(blk):
            S.op("dve", lambda e: e.max(TS.t[:, blk, 0:8], SC.t[:, blk, :]), reads=[SC.b], writes=[TS.b])
            S.op("dve", lambda e: e.max_index(TI.t[:, blk, 0:8], TS.t[:, blk, 0:8], SC.t[:, blk, :]), reads=[SC.b, TS.b], writes=[TI.b])
            S.op("dve", lambda e: e.match_replace(SC2.t[:], TS.t[:, blk, 0:8], SC.t[:, blk, :], -1e30), reads=[SC.b, TS.b], writes=[SC2.b])
            S.op("dve", lambda e: e.max(TS.t[:, blk, 8:16], SC2.t[:]), reads=[SC2.b], writes=[TS.b])
            S.op("dve", lambda e: e.max_index(TI.t[:, blk, 8:16], TS.t[:, blk, 8:16], SC2.t[:]), reads=[SC2.b, TS.b], writes=[TI.b])
        for blk in range(16):
            top16(blk)
            yield
        S.op("dve", lambda e: e.tensor_copy(TIF.t[:], TI.t[:]), reads=[TI.b], writes=[TIF.b])
        S.op("dve", lambda e: e.tensor_scalar(TI128.t[:], TIF.t[:], 128.0, None, ALU.mult), reads=[TIF.b], writes=[TI128.b])

        def head(h):
            for a in range(16):
                S.op("dve", lambda e, a=a: e.tensor_scalar(CS.t[:, a * 16:(a + 1) * 16], TS.t[:, 2 * h + 1, :], TS.t[:, 2 * h, a:a + 1], None, ALU.add),
                     reads=[TS.b], writes=[CS.b])
                S.op("dve", lambda e, a=a: e.tensor_scalar(CI.t[:, a * 16:(a + 1) * 16], TIF.t[:, 2 * h + 1, :], TI128.t[:, 2 * h, a:a + 1], None, ALU.add),
                     reads=[TIF.b, TI128.b], writes=[CI.b])
            S.op("dve", lambda e: e.max(BS.t[:, h, 0:8], CS.t[:]), reads=[CS.b], writes=[BS.b])
            S.op("dve", lambda e: e.max_index(BPU.t[:, 0:8], BS.t[:, h, 0:8], CS.t[:]), reads=[CS.b, BS.b], writes=[BPU.b])
            S.op("dve", lambda e: e.match_replace(CS2.t[:], BS.t[:, h, 0:8], CS.t[:], -1e30), reads=[CS.b, BS.b], writes=[CS2.b])
            S.op("dve", lambda e: e.max(BS.t[:, h, 8:16], CS2.t[:]), reads=[CS2.b], writes=[BS.b])
            S.op("dve", lambda e: e.max_index(BPU.t[:, 8:16], BS.t[:, h, 8:16], CS2.t[:]), reads=[CS2.b, BS.b], writes=[BPU.b])
            S.op("dve", lambda e: e.tensor_copy(BPF.t[:], BPU.t[:]), reads=[BPU.b], writes=[BPF.b])
            for k in range(16):
                S.op("dve", lambda e, k=k: e.scalar_tensor_tensor(JK.t[:], IOT.t[:], BPF.t[:, k:k + 1], CI.t[:], ALU.is_equal, ALU.mult,
                                                                  accum_out=IDF.t[:, h * 16 + k:h * 16 + k + 1]),
                     reads=[IOT.b, BPF.b, CI.b], writes=[JK.b, IDF.b])
            S.op("dve", lambda e: e.tensor_scalar(NM.t[:, h:h + 1], BS.t[:, h, 0:1], -1.0, None, ALU.mult), reads=[BS.b], writes=[NM.b])
            S.op("act", lambda e: e.activation(EX.t[:, h, :], BS.t[:, h, :], AF.Exp, bias=NM.t[:, h:h + 1], accum_out=SM.t[:, h:h + 1]),
                 reads=[BS.b, NM.b], writes=[EX.b, SM.b])
        for h in range(8):
            head(h)
            yield
        S.op("dve", lambda e: e.tensor_copy(IDXc.t[:], IDF.t[:]), reads=[IDF.b], writes=[IDXc.b])
        S.op("dve", lambda e: e.reciprocal(SM.t[:], SM.t[:]), reads=[SM.b], writes=[SM.b])
        for h in range(8):
            S.op("dve", lambda e, h=h: e.tensor_scalar(GWc.t[:, h * 16:(h + 1) * 16], EX.t[:, h, :], SM.t[:, h:h + 1], None, ALU.mult),
                 reads=[EX.b, SM.b], writes=[GWc.b])

    def back(ti, fg):
        r0 = ti * 128
        X1c, X1Bc, IDXc, GWc = X1[ti % 2], X1B[ti % 2], IDX[ti % 2], GW[ti % 2]

        def group(gi):
            bufs = UV[gi % 2]
            for k in range(8):
                def g1(k=k):
                    sl_ = gi * 8 + k
                    ub = bufs[k]
                    S.dma("pool", None, None, ub.semq("pool"), reads=[IDXc.b], writes=[ub.b],
                          fn=lambda e: e.indirect_dma_start(out=ub.t[:], out_offset=None, in_=uvd,
                                                            in_offset=bass.IndirectOffsetOnAxis(ap=IDXc.t[:, sl_:sl_ + 1], axis=0)))
                g1()
            for k in range(8):
                def d1(k=k):
                    sl_ = gi * 8 + k
                    ub = bufs[k]
                    S.op("dve", lambda e: e.scalar_tensor_tensor(JK2.t[:], ub.t[:, 0:1024], 1.0, X1Bc.t[:], ALU.mult, ALU.mult,
                                                                 accum_out=H.t[:, sl_:sl_ + 1]), reads=[ub.b, X1Bc.b], writes=[JK2.b, H.b])
                d1()
            gs = slice(gi * 8, gi * 8 + 8)
            S.op("act", lambda e: e.activation(HG.t[:, gs], H.t[:, gs], AF.Gelu), reads=[H.b], writes=[HG.b])
            S.op("dve", lambda e: e.tensor_tensor(HG.t[:, gs], HG.t[:, gs], GWc.t[:, gs], ALU.mult), reads=[HG.b, GWc.b], writes=[HG.b])
            for k in range(8):
                def v1(k=k):
                    sl_ = gi * 8 + k
                    ub = bufs[k]
                    dg = DG[cnt["dg"] % 4]; cnt["dg"] += 1
                    S.op("act", lambda e: e.activation(dg.t[:], IDB.t[:], AF.Copy, scale=HG.t[:, sl_:sl_ + 1]), reads=[IDB.b, HG.b], writes=[dg.b])
                    for hf in range(2):
                        S.op("pe", lambda e, hf=hf: e.matmul(ACCP[hf].t[:], dg.t[:], ub.t[:, 1024 + hf * 512:1024 + (hf + 1) * 512],
                                                             start=(sl_ == 0), stop=(sl_ == 127)), reads=[dg.b, ub.b], writes=[ACCP[hf].b])
                v1()
        for gi in range(16):
            group(gi)
            next(fg, None); next(fg, None)
        for hf in range(2):
            S.op("dve", lambda e, hf=hf: e.scalar_tensor_tensor(Z.t[:, hf * 512:(hf + 1) * 512], X1c.t[:, hf * 512:(hf + 1) * 512], DN_ALPHA,
                                                                ACCP[hf].t[:], ALU.mult, ALU.add), reads=[X1c.b, ACCP[hf].b], writes=[Z.b])
        layer_norm_tile(S, Z, OUT, LNG, LNB, stat, mv, rstd)
        st(S, OUT, od[r0:r0 + 128, :], OUT.t[:], final=True)
    fg = front(0)
    for _ in fg:
        pass
    for ti in range(ntile):
        nfg = front(ti + 1) if ti + 1 < ntile else iter(())
        back(ti, nfg)
        for _ in nfg:
            pass


def phase23(S, PS, D):
    g2 = phase2(S, PS[0:4], D)
    g3 = phase3(S, PS[4:8], D)
    import os
    mode = os.environ.get("MK_MODE", "seq")
    if mode == "seq":
        for _ in g2:
            pass
        for _ in g3:
            pass
        return
    done2 = done3 = False
    while not (done2 and done3):
        for _ in range(2):
            if not done2:
                try:
                    next(g2)
                except StopIteration:
                    done2 = True
        if not done3:
            try:
                next(g3)
            except StopIteration:
                done3 = True


def build_fused(nph=4):
    nc = bass.Bass("TRN2", target_bir_lowering=False)
    D = {}
    for n, shp in (("xT", [1024, SEQ]), ("xTo", [1024, NTOK]), ("xo", [NTOK, 1024]), ("Wg", [2, 1024, 1796]), ("pcol", [2, 128, 19]),
                   ("w2a2", [2, 128, 256]), ("g2", [2, 128, 256]), ("bd", [128, 128]), ("cm", [128, 512]), ("idn", [128, 128]),
                   ("msel", [128, 2]), ("neg", [8, 128, 512]), ("mk", [5, 64, 512]), ("lg3", [8, 64, 64]), ("lb3", [8, 64, 64]),
                   ("wg", [1024, 2048]), ("pa", [512, 1024]), ("pb", [512, 1024]), ("wo", [1024, 1024]), ("lg1", [128, 1024]),
                   ("lb1", [128, 1024]), ("wq", [1024, 2048]), ("sk", [128, 2048]), ("u", [16384, 1024]), ("v", [16384, 1024]),
                   ("lg2", [128, 1024]), ("lb2", [128, 1024]), ("iota", [128, 256])):
        D[n] = din(nc, n, shp)
    D["out"] = dout(nc, "out", [NTOK, 1024])
    for n, shp in (("fc", [8, SEQ]),
                   ("al", [512, SEQ]), ("be", [512, SEQ]), ("ka", [512, SEQ]), ("rh", [512, SEQ]), ("gc", [512, SEQ // 64]),
                   ("nbe_tm", [SEQ, 512]), ("ka_tm", [SEQ, 512]), ("rv_tm", [SEQ, 512]), ("rg_tm", [SEQ, 512]), ("bo_tm", [SEQ, 512]),
                   ("x1", [NTOK, 1024]), ("x1T", [1024, NTOK])):
        D[n] = dscr(nc, "s_" + n, shp)
    for n, shp in (("fqo", [512, NTOK]), ("fk", [512, SEQ]), ("fvtm", [SEQ, 512]), ("fnc3", [8, 3, SEQ]), ("co3", [8, 3, NTOK]), ("yfo", [512, NTOK]), ("yr", [512, SEQ]),
                   ("uvb", [16384, 2048])):
        D[n] = dscr(nc, "s_" + n, shp, BF16)
    S = Sched(nc)
    PS = mk_psum(S)
    phases = (phase1, phase23, phase4, phase5)[:nph]
    for i, ph in enumerate(phases):
        S.phase_begin()
        ph(S, PS, D)
        S.phase_end(final=(i == len(phases) - 1))
    S.stack.close()
    return nc, S


def core_inputs(x, P, b, t):
    c_ = np.ascontiguousarray
    xb = x[b]
    own = xb.reshape(16, 512, 1024)[t::2].reshape(NTOK, 1024)
    bc = lambda v: c_(np.broadcast_to(v[None, :], (128, v.shape[0])))
    RB = 1544
    mu = P["rwkv_mu"]
    Wg = []; pcols = []; w2a2 = []; g2 = []
    two = lambda v: v.reshape(2, 128).T
    for g in range(2):
        ch = slice(256 * g, 256 * g + 256)
        cols = np.concatenate([
            np.arange(256 * g, 256 * g + 256), 512 + np.arange(256 * g, 256 * g + 256), 1024 + np.arange(256 * g, 256 * g + 256),
            RB + np.arange(256 * g, 256 * g + 256), RB + 512 + np.arange(256 * g, 256 * g + 256),
            RB + 1024 + np.arange(256 * g, 256 * g + 256), RB + np.arange(1536, 1792), 1536 + np.arange(4 * g, 4 * g + 4)])
        Wg.append(P["w_in"][:, cols])
        pc = np.zeros((128, 19), np.float32)
        pc[:, 0:2] = two(mu[0:512][ch]); pc[:, 2:4] = two(mu[512:1024][ch]); pc[:, 4:6] = two(mu[1024:1536][ch])
        pc[:, 6] = mu[1536:1664]; pc[:, 7] = mu[1664:1792]
        pc[:, 8:10] = two(P["rwkv_w0"][ch]); pc[:, 10:12] = two(P["rwkv_a0"][ch])
        pc[:, 12:14] = two(P["rwkv_k_k"][ch]); pc[:, 14:16] = two(P["rwkv_k_a"][ch]); pc[:, 16:18] = two(P["rwkv_r_k"][ch])
        pc[0:4, 18] = P["fox_f_bias"][4 * g:4 * g + 4]
        pcols.append(pc)
        w2a2.append(np.concatenate([P["rwkv_w2"][:, ch], P["rwkv_a2"][:, ch]], 0))
        g2.append(P["rwkv_g2"][:, ch])
    bd = np.kron(np.eye(2, dtype=np.float32), np.ones((64, 64), np.float32))
    cm = np.ones((128, 512), np.float32); cm[:, ::64] = 0.0
    msel = np.zeros((128, 2), np.float32); msel[:, t] = 1.0
    kpos = (128 * np.arange(8)[:, None, None] + np.arange(128)[None, :, None])
    qpos = 512 * t + np.arange(512)[None, None, :]
    neg = np.where(kpos <= qpos, 0.0, -30000.0).astype(np.float32)
    one = np.ones((64, 64), np.float32)
    rep = lambda m: np.tile(m, (1, 8))
    mk = np.stack([rep(np.tril(one, -1)), rep(np.triu(one, 1)), rep(np.triu(one)), rep(-np.triu(one)), rep(np.eye(64, dtype=np.float32))])
    lg3 = np.stack([np.broadcast_to(P["rwkv_ln_g"][h * 64:(h + 1) * 64][None, :], (64, 64)) for h in range(8)])
    lb3 = np.stack([np.broadcast_to(P["rwkv_ln_b"][h * 64:(h + 1) * 64][None, :], (64, 64)) for h in range(8)])
    sk = P["peer_sub_keys"].reshape(16, 128, 128).transpose(2, 0, 1).reshape(128, 16 * 128)
    iota = np.broadcast_to(np.arange(256, dtype=np.float32)[None, :], (128, 256))
    m = {"xT": xb.T, "xTo": own.T, "xo": own, "Wg": np.stack(Wg), "pcol": np.stack(pcols), "w2a2": np.stack(w2a2), "g2": np.stack(g2),
         "bd": bd, "cm": cm, "idn": np.eye(128, dtype=np.float32), "msel": msel, "neg": neg, "mk": mk, "lg3": lg3, "lb3": lb3,
         "wg": P["w_in"][:, 3336:], "pa": P["p_fox"], "pb": P["p_rwkv"], "wo": P["w_o"], "lg1": bc(P["ln1_g"]), "lb1": bc(P["ln1_b"]),
         "wq": P["peer_w_q"], "sk": sk, "u": P["peer_u"], "v": P["peer_v"], "lg2": bc(P["ln2_g"]), "lb2": bc(P["ln2_b"]), "iota": iota}
    return {k: c_(np.asarray(v, np.float32)) for k, v in m.items()}


def kernel(**inputs):
    x = np.asarray(inputs["x"], np.float32)
    P = {k: np.asarray(v, np.float32)[0] for k, v in inputs.items() if k != "x"}
    B = x.shape[0]
    nc, _ = build_fused()
    in_maps = [core_inputs(x, P, b, t) for b in range(B) for t in range(2)]
    res = run_bass_kernel_spmd(nc, in_maps, core_ids=list(range(2 * B)))
    out = np.empty_like(x)
    for b in range(B):
        ob = out[b].reshape(16, 512, 1024)
        for t in range(2):
            ob[t::2] = res.results[2 * b + t]["out"].reshape(8, 512, 1024)
    return out
```

```python
import numpy as np
import concourse.bass as bass
import concourse.mybir as mybir
from concourse.bass_utils import run_bass_kernel_spmd
from contextlib import ExitStack

F32 = mybir.dt.float32
BF16 = mybir.dt.bfloat16
U32 = mybir.dt.uint32
AF = mybir.ActivationFunctionType
ALU = mybir.AluOpType
AX = mybir.AxisListType

SYNC_SAME_ENGINE = True
EPOCH = 10 ** 9
DMA_MAX = 10 ** 6


class Buf:
    __slots__ = ("name", "w", "r")

    def __init__(self, name):
        self.name = name
        self.w = None
        self.r = {}


class DmaSem:
    __slots__ = ("handle", "count", "uid")
    _n = 0

    def __init__(self, handle):
        self.handle = handle
        self.count = 0
        DmaSem._n += 1
        self.uid = DmaSem._n


class Sched:
    ENG = ("sp", "act", "dve", "pool", "pe")

    def __init__(self, nc):
        self.nc = nc
        self.streams = {e: [] for e in self.ENG}
        self.seq = {e: 0 for e in self.ENG}
        self.known = {e: {} for e in self.ENG}
        self.esem = {}
        self.stack = ExitStack()
        self.nsem = 0
        self.ninstr = 0
        self.out_toks = []
        self.pi = 0
        self.pstack = None
        self.all_dmasems = []
        self.free_dmasems = {"hw": [], "sw": []}
        self.phase_tiles = []

    def sbuf(self, name, shape, dtype):
        stk = self.pstack if self.pstack is not None else self.stack
        return stk.enter_context(self.nc.sbuf_tensor(name, list(shape), dtype))

    def phase_begin(self):
        self.pstack = ExitStack()

    def phase_end(self, final=False):
        if final:
            for tok in self.out_toks:
                self._wait("sp", tok)
        self.barrier()
        self.emit_block()
        self.pstack.close()
        self.pstack = None
        for tl in self.phase_tiles:
            for kind, sm in tl._sems.items():
                self.free_dmasems[kind].append(sm)
            tl._sems = {}
        self.phase_tiles = []

    def barrier(self):
        for e in self.ENG:
            for f in self.ENG:
                if f != e and self.seq[f] > 0:
                    self._wait(e, ("eng", f, self.seq[f]))
            for sem in self.all_dmasems:
                if sem.count > 0:
                    self._wait(e, ("dma", sem, 16 * sem.count))

    def emit_block(self):
        nc = self.nc
        with nc.Block() as block:
            for e, deco in (("sp", block.sync), ("act", block.scalar), ("dve", block.vector),
                            ("pool", block.gpsimd), ("pe", block.tensor)):
                stream = self.streams[e]

                def body(eng, stream=stream):
                    for th in stream:
                        th(eng)
                deco(body)
        self.streams = {e: [] for e in self.ENG}

    def psum(self, name, shape, dtype):
        return self.stack.enter_context(self.nc.psum_tensor(name, list(shape), dtype))

    def newsem(self, name):
        self.nsem += 1
        return self.nc.alloc_semaphore(name=f"{name}_{self.nsem}")

    def dmasem(self, name="d", kind="hw"):
        if self.free_dmasems[kind]:
            return self.free_dmasems[kind].pop()
        d = DmaSem(self.newsem(name + kind))
        self.all_dmasems.append(d)
        return d

    def _esem(self, e, epoch):
        k = (e, epoch)
        if k not in self.esem:
            self.esem[k] = self.newsem(f"e_{e}{epoch}")
        return self.esem[k]

    def _wait(self, e, tok):
        if tok is None:
            return
        kind, ident, val = tok
        key = ident if kind == "eng" else ("dma", ident.uid)
        if self.known[e].get(key, 0) >= val:
            return
        self.known[e][key] = val
        if kind == "eng":
            sem = self._esem(ident, (val - 1) // EPOCH)
            v = (val - 1) % EPOCH + 1
        else:
            sem = ident.handle
            v = val
        self.streams[e].append(lambda eng, sem=sem, v=v: eng.wait_ge(sem, v))
        self.ninstr += 1

    def _deps(self, e, reads, writes, nowaw=False, soft=()):
        strict = SYNC_SAME_ENGINE and e != "pe"
        for b in reads:
            if b.w is not None:
                if not (e == "pe" and b.w[0] == "eng" and b.w[1] == "pe"):
                    self._wait(e, b.w)
        for b in writes:
            if b.w is not None and not nowaw:
                same = (b.w[0] == "eng" and b.w[1] == e)
                if (strict and not (same and b in soft)) or not same:
                    self._wait(e, b.w)
            for tok in b.r.values():
                if strict or not (tok[0] == "eng" and tok[1] == e):
                    self._wait(e, tok)

    def _record(self, tok, reads, writes):
        for b in writes:
            b.w = tok
            b.r = {}
        key = tok[1] if tok[0] == "eng" else ("dma", tok[1].uid)
        for b in reads:
            b.r[key] = tok

    def op(self, e, fn, reads=(), writes=(), soft=()):
        self._deps(e, reads, writes, soft=soft)
        self.seq[e] += 1
        s = self.seq[e]
        sem = self._esem(e, (s - 1) // EPOCH)
        self.streams[e].append(lambda eng, fn=fn, sem=sem: fn(eng).then_inc(sem, 1))
        self.ninstr += 1
        tok = ("eng", e, s)
        self._record(tok, reads, writes)
        return tok

    def dma(self, q, out, in_, sem, reads=(), writes=(), nowaw=False, fn=None, **kw):
        self._deps(q, reads, writes, nowaw=nowaw)
        sem.count += 1
        assert sem.count < DMA_MAX, "dma sem overflow"
        h = sem.handle
        if fn is None:
            self.streams[q].append(
                lambda eng, out=out, in_=in_, h=h, kw=kw: eng.dma_start(out=out, in_=in_, **kw).then_inc(h, 16))
        else:
            self.streams[q].append(lambda eng, fn=fn, h=h: fn(eng).then_inc(h, 16))
        self.ninstr += 1
        tok = ("dma", sem, 16 * sem.count)
        self._record(tok, reads, writes)
        return tok

    def emit(self):
        for tok in self.out_toks:
            self._wait("sp", tok)
        nc = self.nc
        with nc.Block() as block:
            for e, deco in (("sp", block.sync), ("act", block.scalar), ("dve", block.vector),
                            ("pool", block.gpsimd), ("pe", block.tensor)):
                stream = self.streams[e]

                def body(eng, stream=stream):
                    for th in stream:
                        th(eng)
                deco(body)
        self.stack.close()


class T:
    def __init__(self, S, name, shape, dtype=F32, psum=False):
        self.S = S
        S.ntile = getattr(S, "ntile", 0) + 1
        self.t = (S.psum if psum else S.sbuf)(f"t{S.ntile}_" + name, shape, dtype)
        self.b = Buf(name)
        self._sems = {}
        self.name = name
        if not psum:
            S.phase_tiles.append(self)

    def semq(self, q):
        kind = "sw" if q == "pool" else "hw"
        if kind not in self._sems:
            self._sems[kind] = self.S.dmasem(self.name, kind)
        return self._sems[kind]

    @property
    def sem(self):
        return self.semq("sp")


def ld(S, tl, dst, src, q="sp", nowaw=False):
    return S.dma(q, dst, src, tl.semq(q), writes=[tl.b], nowaw=nowaw)


def st(S, tl, dst, src, q="sp", final=False):
    tok = S.dma(q, dst, src, tl.semq(q), reads=[tl.b])
    if final:
        S.out_toks.append(tok)
    return tok


def dscr(nc, name, shape, dt=F32):
    return nc.dram_tensor(name, list(shape), dt, kind="Internal").ap()


def mk_psum(S, n=8):
    return [T(S, f"ps{i}", [128, 512], F32, psum=True) for i in range(n)]


def nxt(S, PS):
    p = PS[S.pi % len(PS)]
    S.pi += 1
    return p


def din(nc, name, shape, dt=F32):
    return nc.dram_tensor(name, list(shape), dt, kind="ExternalInput").ap()


def dout(nc, name, shape, dt=F32):
    return nc.dram_tensor(name, list(shape), dt, kind="ExternalOutput").ap()


SEQ = 8192
NSC = SEQ // 512
NEG_E = -0.6065306597126334


NTOK = 4096
DN_ALPHA = 2.0 ** 0.25
LN_EPS = 1e-5


def layer_norm_tile(S, Z, OUT, LNG, LNB, stat, mv, rstd):
    for hf in range(2):
        S.op("dve", lambda e, hf=hf: e.bn_stats(stat.t[:, hf * 6:(hf + 1) * 6], Z.t[:, hf * 512:(hf + 1) * 512]),
             reads=[Z.b], writes=[stat.b])
    S.op("dve", lambda e: e.bn_aggr(mv.t[:], stat.t[:]), reads=[stat.b], writes=[mv.b])
    S.op("dve", lambda e: e.tensor_scalar(rstd.t[:], mv.t[:, 1:2], LN_EPS, None, ALU.add), reads=[mv.b], writes=[rstd.b])
    S.op("act", lambda e: e.activation(rstd.t[:], rstd.t[:], AF.Sqrt), reads=[rstd.b], writes=[rstd.b])
    S.op("dve", lambda e: e.reciprocal(rstd.t[:], rstd.t[:]), reads=[rstd.b], writes=[rstd.b])
    S.op("dve", lambda e: e.tensor_scalar(Z.t[:], Z.t[:], mv.t[:, 0:1], rstd.t[:, 0:1], ALU.subtract, ALU.mult),
         reads=[Z.b, mv.b, rstd.b], writes=[Z.b])
    S.op("dve", lambda e: e.tensor_tensor(Z.t[:], Z.t[:], LNG.t[:], ALU.mult), reads=[Z.b, LNG.b], writes=[Z.b])
    S.op("dve", lambda e: e.tensor_tensor(OUT.t[:], Z.t[:], LNB.t[:], ALU.add), reads=[Z.b, LNB.b], writes=[OUT.b])


def phase1(S, PS, D):
    xv = D["xT"].rearrange("(kc p) t -> p kc t", p=128)
    xov = D["xTo"].rearrange("(kc p) t -> p kc t", p=128)
    W = T(S, "W", [128, 8, 1796], BF16); WS = [T(S, f"WS{i}", [128, 1796]) for i in range(2)]
    PC = T(S, "PC", [128, 19]); W2 = T(S, "W2", [128, 256]); G2 = T(S, "G2", [128, 256])
    BD = T(S, "BD", [128, 128]); CM = T(S, "CM", [128, 512]); IDN = T(S, "IDN", [128, 128]); MS = T(S, "MS", [128, 2])
    ONE = T(S, "ONE", [4, 512]); CAR = T(S, "CAR", [4, 1])
    XF = [T(S, f"XF{i}", [128, 8, 512]) for i in range(2)]
    X = [T(S, f"X{i}", [128, 8, 512], BF16) for i in range(2)]

    def load_x(i, src):
        xf = XF[i % 2]; xt = X[i % 2]
        ld(S, xf, xf.t[:], src)
        S.op("pool", lambda e: e.tensor_copy(xt.t[:], xf.t[:]), reads=[xf.b], writes=[xt.b])
        return xt
    ld(S, BD, BD.t[:], D["bd"]); ld(S, CM, CM.t[:], D["cm"]); ld(S, IDN, IDN.t[:], D["idn"]); ld(S, MS, MS.t[:], D["msel"])
    S.op("dve", lambda e: e.memset(ONE.t[:], 1.0), writes=[ONE.b])
    Pn = ["r0", "r1", "k0", "k1", "v0", "v1", "l", "g"]
    P = {n: T(S, "P" + n, [128, 513]) for n in Pn}
    SH = {n: T(S, "SH" + n, [128, 512]) for n in Pn}
    tmp = T(S, "tmp", [128, 512])
    FO = [T(S, f"FO{i}", [128, 512], BF16) for i in range(2)]
    TME = [T(S, f"TME{i}", [128, 512]) for i in range(2)]
    FL = T(S, "FL", [4, 512]); FC = T(S, "FCt", [4, 512]); FN = T(S, "FNt", [4, 512])
    SP3 = T(S, "SP3", [8, 3, 512], BF16); SPR = T(S, "SPR", [8, 512])
    TH = T(S, "TH", [64, 512]); SGL = T(S, "SGL", [128, 512])
    nm = ["lw", "a", "g", "t1", "sq", "nr", "kk", "t2", "kmod", "cs", "en", "ep", "d2", "epv",
          "alpha", "t3", "beta", "nbeta", "kappa", "rho", "pr", "bo"]
    R = {n: T(S, "R" + n, [128, 512]) for n in nm}
    GC = T(S, "GC", [128, 8])
    pcmap = {"r0": 0, "r1": 1, "k0": 2, "k1": 3, "v0": 4, "v1": 5, "l": 6, "g": 7}
    blkmap = {"r0": 6, "r1": 7, "k0": 8, "k1": 9, "v0": 10, "v1": 11, "l": 12, "g": 13}
    bfc = Buf("fc_scr")
    cnt = {"fo": 0, "tme": 0}

    def mm_block(xt, col0, ncol):
        ps = nxt(S, PS)
        for kc in range(8):
            S.op("pe", lambda e, kc=kc: e.matmul(ps.t[0:ncol, :], W.t[:, kc, col0:col0 + ncol], xt.t[:, kc, :],
                                                 start=(kc == 0), stop=(kc == 7)), reads=[W.b, xt.b], writes=[ps.b])
        return ps

    def split3(src, n, dst_ap):
        S.op("dve", lambda e: e.tensor_copy(SP3.t[0:n, 0, :], src.t[0:n, :]), reads=[src.b], writes=[SP3.b])
        S.op("dve", lambda e: e.tensor_tensor(SPR.t[0:n, :], src.t[0:n, :], SP3.t[0:n, 0, :], ALU.subtract), reads=[src.b, SP3.b], writes=[SPR.b])
        S.op("dve", lambda e: e.tensor_copy(SP3.t[0:n, 1, :], SPR.t[0:n, :]), reads=[SPR.b], writes=[SP3.b])
        S.op("dve", lambda e: e.tensor_tensor(SPR.t[0:n, :], SPR.t[0:n, :], SP3.t[0:n, 1, :], ALU.subtract), reads=[SPR.b, SP3.b], writes=[SPR.b])
        S.op("dve", lambda e: e.tensor_copy(SP3.t[0:n, 2, :], SPR.t[0:n, :]), reads=[SPR.b], writes=[SP3.b])
        st(S, SP3, dst_ap, SP3.t[0:n, :, :])

    def group(g):
        Wv = D["Wg"][g].rearrange("(kc p) c -> p kc c", p=128)
        for kc in range(8):
            def wl(kc=kc):
                ws = WS[kc % 2]
                ld(S, ws, ws.t[:], Wv[:, kc, :])
                S.op("pool", lambda e: e.tensor_copy(W.t[:, kc, :], ws.t[:]), reads=[ws.b], writes=[W.b])
            wl()
        ld(S, PC, PC.t[:], D["pcol"][g]); ld(S, W2, W2.t[:], D["w2a2"][g]); ld(S, G2, G2.t[:], D["g2"][g])
        S.op("dve", lambda e: e.memset(CAR.t[:], 0.0), writes=[CAR.b])
        for n in Pn:
            S.op("dve", lambda e, n=n: e.memset(P[n].t[:, 0:1], 0.0), writes=[P[n].b])
        pc = lambda j: PC.t[:, j:j + 1]

        def superchunk(sc):
            tsl = slice(sc * 512, (sc + 1) * 512)
            xt = load_x(sc, xv[:, :, tsl])

            def foxk(blk):
                ps = mm_block(xt, blk * 128, 128)
                fo = FO[cnt["fo"] % 2]; cnt["fo"] += 1
                S.op("act", lambda e: e.activation(fo.t[:], ps.t[:], AF.Copy), reads=[ps.b], writes=[fo.b])
                r0 = g * 256 + (blk % 2) * 128
                st(S, fo, D["fk"][r0:r0 + 128, tsl], fo.t[:])
            foxk(2); foxk(3)

            def foxv(pair):
                ps = nxt(S, PS)
                for j in range(2):
                    tt = pair * 2 + j
                    for kc in range(8):
                        S.op("pe", lambda e, j=j, tt=tt, kc=kc: e.matmul(ps.t[:, j * 256:(j + 1) * 256], xt.t[:, kc, tt * 128:(tt + 1) * 128],
                                                                         W.t[:, kc, 512:768], start=(kc == 0), stop=(kc == 7)),
                             reads=[W.b, xt.b], writes=[ps.b])
                fo = FO[cnt["fo"] % 2]; cnt["fo"] += 1
                S.op("act", lambda e: e.activation(fo.t[:], ps.t[:], AF.Copy), reads=[ps.b], writes=[fo.b])
                r0 = sc * 512 + pair * 256
                st(S, fo, D["fvtm"][r0:r0 + 256, g * 256:(g + 1) * 256].rearrange("(j p) c -> p j c", p=128),
                   fo.t[:].rearrange("p (j c) -> p j c", c=256))
            foxv(0); foxv(1)
            ps = mm_block(xt, 1792, 4)
            S.op("act", lambda e: e.activation(FL.t[:], ps.t[0:4, :], AF.Sigmoid, bias=PC.t[0:4, 18:19]), reads=[ps.b, PC.b], writes=[FL.b])
            S.op("act", lambda e: e.activation(FL.t[:], FL.t[:], AF.Ln), reads=[FL.b], writes=[FL.b])
            S.op("dve", lambda e: e.tensor_tensor_scan(FC.t[:], ONE.t[:], FL.t[:], CAR.t[:, 0:1], ALU.mult, ALU.add),
                 reads=[ONE.b, FL.b, CAR.b], writes=[FC.b])
            S.op("dve", lambda e: e.tensor_copy(CAR.t[:], FC.t[:, 511:512]), reads=[FC.b], writes=[CAR.b])
            S.op("dve", lambda e: e.tensor_scalar(FN.t[:], FC.t[:], -1.0, None, ALU.mult), reads=[FC.b], writes=[FN.b])
            S.dma("sp", D["fc"][g * 4:(g + 1) * 4, tsl], FC.t[:], FC.sem, reads=[FC.b], writes=[bfc], nowaw=True)
            split3(FN, 4, D["fnc3"][g * 4:(g + 1) * 4, :, tsl])

            def proj_shift(n):
                ps = mm_block(xt, blkmap[n] * 128, 128)
                p = P[n]; sh = SH[n]; mu = pc(pcmap[n])
                S.op("act", lambda e: e.activation(p.t[:, 1:513], ps.t[:], AF.Copy), reads=[ps.b], writes=[p.b])
                S.op("dve", lambda e: e.tensor_tensor(tmp.t[:], p.t[:, 0:512], p.t[:, 1:513], ALU.subtract), reads=[p.b], writes=[tmp.b])
                S.op("dve", lambda e: e.scalar_tensor_tensor(sh.t[:], tmp.t[:], mu, p.t[:, 1:513], ALU.mult, ALU.add),
                     reads=[p.b, tmp.b, PC.b], writes=[sh.b])
                S.op("dve", lambda e: e.tensor_copy(p.t[:, 0:1], p.t[:, 512:513]), reads=[p.b], writes=[p.b])
            for n in Pn:
                proj_shift(n)
            S.op("act", lambda e: e.activation(TH.t[:], SH["l"].t[0:64, :], AF.Tanh), reads=[SH["l"].b], writes=[TH.b])
            S.op("act", lambda e: e.activation(SGL.t[:], SH["g"].t[:], AF.Sigmoid), reads=[SH["g"].b], writes=[SGL.b])

            def rwkv_block(b):
                cs_ = slice(b * 128, b * 128 + 128)
                shr, shk, shv = SH[f"r{b}"], SH[f"k{b}"], SH[f"v{b}"]

                def mm1(lhs, rhs, reads):
                    ps = nxt(S, PS)
                    S.op("pe", lambda e: e.matmul(ps.t[:], lhs, rhs, start=True, stop=True), reads=reads, writes=[ps.b])
                    return ps

                def act(dst, src, func, rd, **kw):
                    S.op("act", lambda e: e.activation(dst.t[:], src, func, **kw), reads=rd, writes=[dst.b])

                def tt(dst, a, b_, op):
                    S.op("dve", lambda e: e.tensor_tensor(dst.t[:], a.t[:], b_.t[:], op), reads=[a.b, b_.b], writes=[dst.b])

                def ts(dst, a, s1, s2, op0, op1=None, extra=()):
                    if op1 is None:
                        S.op("dve", lambda e: e.tensor_scalar(dst.t[:], a.t[:], s1, s2, op0), reads=[a.b, *extra], writes=[dst.b])
                    else:
                        S.op("dve", lambda e: e.tensor_scalar(dst.t[:], a.t[:], s1, s2, op0, op1), reads=[a.b, *extra], writes=[dst.b])
                ps = mm1(W2.t[0:64, cs_], TH.t[:], [W2.b, TH.b])
                act(R["lw"], ps.t[:], AF.Sigmoid, [ps.b, PC.b], bias=pc(8 + b))
                ts(R["lw"], R["lw"], NEG_E, None, ALU.mult)
                ps = mm1(W2.t[64:128, cs_], SH["l"].t[64:128, :], [W2.b, SH["l"].b])
                act(R["a"], ps.t[:], AF.Sigmoid, [ps.b, PC.b], bias=pc(10 + b))
                ps = mm1(G2.t[:, cs_], SGL.t[:], [G2.b, SGL.b])
                act(R["g"], ps.t[:], AF.Copy, [ps.b])
                ts(R["t1"], shk, pc(12 + b), None, ALU.mult, extra=[PC.b])
                tt(R["sq"], R["t1"], R["t1"], ALU.mult)
                ps = mm1(BD.t[:], R["sq"].t[:], [BD.b, R["sq"].b])
                act(R["nr"], ps.t[:], AF.Sqrt, [ps.b])
                ts(R["nr"], R["nr"], 1e-12, None, ALU.max)
                S.op("dve", lambda e: e.reciprocal(R["nr"].t[:], R["nr"].t[:]), reads=[R["nr"].b], writes=[R["nr"].b])
                tt(R["kk"], R["t1"], R["nr"], ALU.mult)
                ts(R["t2"], R["a"], -1.0, pc(14 + b), ALU.add, ALU.mult, extra=[PC.b])
                S.op("dve", lambda e: e.scalar_tensor_tensor(R["kmod"].t[:], R["t2"].t[:], 1.0, shk.t[:], ALU.add, ALU.mult),
                     reads=[R["t2"].b, shk.b], writes=[R["kmod"].b])
                S.op("dve", lambda e: e.tensor_tensor_scan(R["cs"].t[:], CM.t[:], R["lw"].t[:], 0.0, ALU.mult, ALU.add),
                     reads=[CM.b, R["lw"].b], writes=[R["cs"].b])
                act(R["en"], R["cs"].t[:], AF.Exp, [R["cs"].b], scale=-1.0)
                act(R["ep"], R["cs"].t[:], AF.Exp, [R["cs"].b])
                tt(R["d2"], R["cs"], R["lw"], ALU.subtract)
                act(R["epv"], R["d2"].t[:], AF.Exp, [R["d2"].b])
                tt(R["alpha"], R["kk"], R["epv"], ALU.mult)
                tt(R["t3"], R["kk"], R["a"], ALU.mult)
                tt(R["beta"], R["t3"], R["en"], ALU.mult)
                ts(R["nbeta"], R["beta"], -1.0, None, ALU.mult)
                tt(R["kappa"], R["kmod"], R["en"], ALU.mult)
                tt(R["rho"], shr, R["ep"], ALU.mult)
                S.op("dve", lambda e: e.tensor_copy(GC.t[:], R["ep"].t[:, 63::64]), reads=[R["ep"].b], writes=[GC.b])
                S.op("dve", lambda e: e.scalar_tensor_tensor(R["pr"].t[:], shr.t[:], pc(16 + b), R["kmod"].t[:], ALU.mult, ALU.mult),
                     reads=[shr.b, PC.b, R["kmod"].b], writes=[R["pr"].b])
                ps = mm1(BD.t[:], R["pr"].t[:], [BD.b, R["pr"].b])
                act(R["bo"], ps.t[:], AF.Copy, [ps.b])
                r0 = g * 256 + b * 128
                for on, tl in (("al", R["alpha"]), ("be", R["beta"]), ("ka", R["kappa"]), ("rh", R["rho"])):
                    st(S, tl, D[on][r0:r0 + 128, tsl], tl.t[:])
                st(S, GC, D["gc"][r0:r0 + 128, sc * 8:(sc + 1) * 8], GC.t[:])

                def tmaj(on, tl):
                    ps = nxt(S, PS)
                    for t4 in range(4):
                        S.op("pe", lambda e, t4=t4: e.matmul(ps.t[:, t4 * 128:(t4 + 1) * 128], tl.t[:, t4 * 128:(t4 + 1) * 128], IDN.t[:],
                                                             start=True, stop=True), reads=[tl.b, IDN.b], writes=[ps.b])
                    te = TME[cnt["tme"] % 2]; cnt["tme"] += 1
                    S.op("act", lambda e: e.activation(te.t[:], ps.t[:], AF.Copy), reads=[ps.b], writes=[te.b])
                    st(S, te, D[on][tsl, r0:r0 + 128].rearrange("(t4 p) c -> p t4 c", p=128), te.t[:].rearrange("p (t4 c) -> p t4 c", c=128))
                for on, tl in (("nbe_tm", R["nbeta"]), ("ka_tm", R["kappa"]), ("rv_tm", shv), ("rg_tm", R["g"]), ("bo_tm", R["bo"])):
                    tmaj(on, tl)
            for b in range(2):
                rwkv_block(b)
        for sc in range(NSC):
            superchunk(sc)

        def ownq(i):
            xt = load_x(i, xov[:, :, i * 512:(i + 1) * 512])
            for blk in range(2):
                def one(blk=blk):
                    ps = mm_block(xt, blk * 128, 128)
                    fo = FO[cnt["fo"] % 2]; cnt["fo"] += 1
                    S.op("act", lambda e: e.activation(fo.t[:], ps.t[:], AF.Copy, scale=0.125), reads=[ps.b], writes=[fo.b])
                    r0 = g * 256 + blk * 128
                    st(S, fo, D["fqo"][r0:r0 + 128, i * 512:(i + 1) * 512], fo.t[:])
                one()
        for i in range(NTOK // 512):
            ownq(i)
    for g in range(2):
        group(g)
    FCA = T(S, "FCA", [8, 512]); FCB = T(S, "FCB", [8, 512])

    def blend(i):
        S.dma("sp", FCA.t[:], D["fc"][:, (2 * i) * 512:(2 * i + 1) * 512], FCA.sem, reads=[bfc], writes=[FCA.b])
        S.dma("sp", FCB.t[:], D["fc"][:, (2 * i + 1) * 512:(2 * i + 2) * 512], FCB.sem, reads=[bfc], writes=[FCB.b])
        S.op("dve", lambda e: e.tensor_scalar(FCA.t[:], FCA.t[:], MS.t[0:8, 0:1], None, ALU.mult), reads=[FCA.b, MS.b], writes=[FCA.b])
        S.op("dve", lambda e: e.scalar_tensor_tensor(FCA.t[:], FCB.t[:], MS.t[0:8, 1:2], FCA.t[:], ALU.mult, ALU.add),
             reads=[FCA.b, FCB.b, MS.b], writes=[FCA.b])
        split3(FCA, 8, D["co3"][:, :, i * 512:(i + 1) * 512])
    for i in range(NTOK // 512):
        blend(i)


def phase2(S, PS, D):
    SCP = PS[0:2]; ACC = PS[2:4]
    QA = T(S, "QA", [70, NTOK], BF16); KA = T(S, "KA", [70, SEQ], BF16); VO = T(S, "VO", [128, 64, 128], BF16)
    NEG = T(S, "NEG", [128, 8, 512])
    PT = [T(S, f"PT{i}", [128, 512], BF16) for i in range(3)]
    TMP = [T(S, f"TMPm{i}", [128, 512]) for i in range(2)]
    RD = T(S, "RD", [128, 512]); Y = T(S, "Y", [64, 512], BF16)
    ld(S, NEG, NEG.t[:], D["neg"].rearrange("j p q -> p j q"))
    S.op("dve", lambda e: e.memset(QA.t[64:70, :], 1.0), writes=[QA.b])
    S.op("dve", lambda e: e.memset(KA.t[64:70, :], 1.0), writes=[KA.b])
    S.op("dve", lambda e: e.memset(VO.t[:, :, 64:128], 1.0), writes=[VO.b])
    cnt = {"pi": 0, "pt": 0, "tm": 0}

    def head(hh):
        r = slice(hh * 64, (hh + 1) * 64)
        for i in range(2):
            sl = slice(i * 2048, (i + 1) * 2048)
            ld(S, QA, QA.t[0:64, sl], D["fqo"][r, sl], nowaw=(i > 0))
        ld(S, QA, QA.t[64:67, :], D["co3"][hh], nowaw=True)
        for i in range(4):
            sl = slice(i * 2048, (i + 1) * 2048)
            ld(S, KA, KA.t[0:64, sl], D["fk"][r, sl], nowaw=(i > 0))
        ld(S, KA, KA.t[67:70, :], D["fnc3"][hh], nowaw=True)
        ld(S, VO, VO.t[:, :, 0:64], D["fvtm"][:, r].rearrange("(kb p) d -> p kb d", p=128), nowaw=True)

        def qchunk(i):
            acc = ACC[i % 2]
            nkb = 8 * i + 8

            def scores(kb):
                j = kb - 8 * i
                ps = SCP[cnt["pi"] % 2]; cnt["pi"] += 1
                pt = PT[cnt["pt"] % 3]; cnt["pt"] += 1
                S.op("pe", lambda e: e.matmul(ps.t[:], KA.t[:, kb * 128:(kb + 1) * 128], QA.t[:, i * 512:(i + 1) * 512], start=True, stop=True),
                     reads=[KA.b, QA.b], writes=[ps.b])
                if j >= 0:
                    tm = TMP[cnt["tm"] % 2]; cnt["tm"] += 1
                    S.op("dve", lambda e: e.tensor_tensor(tm.t[:], ps.t[:], NEG.t[:, j, :], ALU.add), reads=[ps.b, NEG.b], writes=[tm.b])
                    S.op("act", lambda e: e.activation(pt.t[:], tm.t[:], AF.Exp), reads=[tm.b], writes=[pt.b])
                else:
                    S.op("act", lambda e: e.activation(pt.t[:], ps.t[:], AF.Exp), reads=[ps.b], writes=[pt.b])
                return kb, pt

            def pv(kb, pt):
                S.op("pe", lambda e: e.matmul(acc.t[:], VO.t[:, kb, :], pt.t[:], start=(kb == 0), stop=(kb == nkb - 1)),
                     reads=[VO.b, pt.b], writes=[acc.b])
            prev = None
            for kb in range(nkb):
                cur = scores(kb)
                if prev is not None:
                    pv(*prev)
                prev = cur
                yield
            pv(*prev)
            S.op("dve", lambda e: e.reciprocal(RD.t[64:128, :], acc.t[64:128, :]), reads=[acc.b], writes=[RD.b])
            S.op("dve", lambda e: e.tensor_tensor(Y.t[:], acc.t[0:64, :], RD.t[64:128, :], ALU.mult), reads=[acc.b, RD.b], writes=[Y.b])
            st(S, Y, D["yfo"][r, i * 512:(i + 1) * 512], Y.t[:])
        for i in range(NTOK // 512):
            yield from qchunk(i)
    for hh in range(8):
        yield from head(hh)


GN_EPS = 64e-5


def phase3(S, PS, D):
    MK = [T(S, f"MK{i}", [128, 512]) for i in range(5)]
    for i in range(5):
        ld(S, MK[i], MK[i].t[0:64, :], D["mk"][i])
        ld(S, MK[i], MK[i].t[64:128, :], D["mk"][i], nowaw=True)
    MSL, MSU, MIU, NMIU, ID8 = MK
    FM = [[T(S, f"FM{p}_{i}", [128, 512]) for i in range(4)] for p in range(2)]
    TM = [[T(S, f"TM{p}_{i}", [128, 8, 64]) for i in range(5)] for p in range(2)]
    GC = T(S, "GC3", [128, 128]); LG = T(S, "LG", [128, 64]); LB = T(S, "LB", [128, 64])
    mk3 = lambda n: T(S, n, [128, 8, 64])
    A = mk3("A"); AT = mk3("AT"); Wt = [mk3("W0"), mk3("W1")]; Pt = [mk3("P0"), mk3("P1")]; PTt = [mk3("PT0"), mk3("PT1")]
    AakT = mk3("AakT"); nArbT = mk3("nArbT"); ArkT = mk3("ArkT")
    ST = [T(S, "ST0", [128, 64]), T(S, "ST1", [128, 64])]
    STg = T(S, "STg", [128, 64]); RHS = T(S, "RHS", [128, 64]); US = T(S, "US", [128, 64])
    YO = [mk3("YO0"), mk3("YO1")]; YT = [T(S, "YT0", [128, 512], BF16), T(S, "YT1", [128, 512], BF16)]
    stat = T(S, "stat3", [128, 6]); mv = T(S, "mv3", [128, 2]); rstd = T(S, "rstd3", [128, 1]); yn = T(S, "yn", [128, 64]); bt = T(S, "bt", [128, 64])
    cnt = {"st": 0, "cast": 0}
    fmn = ("al", "be", "ka", "rh"); tmn = ("nbe_tm", "ka_tm", "rv_tm", "rg_tm", "bo_tm")
    HS = (slice(0, 64), slice(64, 128))
    CF = [T(S, f"CF{i}", [128, 4096]) for i in range(2)]; CB = [T(S, f"CB{i}", [128, 4096], BF16) for i in range(2)]
    NCH = 16384 * 1024 // 128 // 4096

    def cast_step():
        k = cnt["cast"]
        if k >= 2 * NCH:
            return
        cnt["cast"] += 1
        src = D["u" if k < NCH else "v"].rearrange("(p r) d -> p (r d)", p=128)
        dst = D["uvb"].rearrange("(p r) (two d) -> p r two d", p=128, two=2)
        c = k % NCH
        cf = CF[k % 2]; cb = CB[k % 2]
        ld(S, cf, cf.t[:], src[:, c * 4096:(c + 1) * 4096], q="pool")
        S.op("pool", lambda e: e.tensor_copy(cb.t[:], cf.t[:]), reads=[cf.b], writes=[cb.b])
        st(S, cb, dst[:, c * 4:(c + 1) * 4, 0 if k < NCH else 1, :], cb.t[:].rearrange("p (r d) -> p r d", d=1024), q="pool")

    def mm2(ps, col, lhs, rhs, reads, start=True, stop=True):
        for hs in HS:
            S.op("pe", lambda e, hs=hs: e.matmul(ps.t[hs, col * 64:(col + 1) * 64], lhs(hs), rhs(hs), start=start, stop=stop),
                 reads=reads, writes=[ps.b])

    def batch_mm(lhs_of, rhs_of, reads):
        ps = nxt(S, PS)
        for j in range(8):
            mm2(ps, j, (lambda hs, j=j: lhs_of(j, hs)), (lambda hs, j=j: rhs_of(j, hs)), reads)
        return ps

    def pair(hp):
        hh = 2 * hp
        r = slice(hh * 64, (hh + 2) * 64)
        ld(S, GC, GC.t[:], D["gc"][r, :])
        ld(S, LG, LG.t[:], D["lg3"][hh:hh + 2].rearrange("h p c -> (h p) c"))
        ld(S, LB, LB.t[:], D["lb3"][hh:hh + 2].rearrange("h p c -> (h p) c"))
        s0 = ST[cnt["st"] % 2]
        S.op("dve", lambda e: e.memset(s0.t[:], 0.0), writes=[s0.b])

        def superchunk(sc):
            fm = FM[sc % 2]; tm = TM[sc % 2]
            tsl = slice(sc * 512, (sc + 1) * 512)
            for i in range(4):
                ld(S, fm[i], fm[i].t[:], D[fmn[i]][r, tsl])
            for i in range(5):
                for k_, hs in enumerate(HS):
                    rr = slice((hh + k_) * 64, (hh + k_ + 1) * 64)
                    ld(S, tm[i], tm[i].t[hs, :, :], D[tmn[i]][tsl, rr].rearrange("(c t) k -> t c k", t=64), nowaw=(k_ > 0))
            alT, beT, kaT, rhT = fm
            nbe, ka, vm, gt, bo = tm
            fsl = lambda t_, j, hs: t_.t[hs, j * 64:(j + 1) * 64]
            f3 = lambda t_, j, hs: t_.t[hs, j, :]
            flat = lambda t_: t_.t[:].rearrange("p a b -> p (a b)")

            def evac_mask(dst, ps, mk):
                S.op("dve", lambda e: e.tensor_tensor(flat(dst), ps.t[:], mk.t[:], ALU.mult), reads=[ps.b, mk.b], writes=[dst.b])

            def evac_copy(dst, ps):
                S.op("act", lambda e: e.activation(flat(dst), ps.t[:], AF.Copy), reads=[ps.b], writes=[dst.b])

            def evac_add(dst, ps, src):
                S.op("dve", lambda e: e.tensor_tensor(flat(dst), ps.t[:], flat(src), ALU.add), reads=[ps.b, src.b], writes=[dst.b])

            def bmm3(l, r_):
                return batch_mm(lambda j, hs: f3(l, j, hs), lambda j, hs: f3(r_, j, hs), [l.b, r_.b])

            def bmmf(l, r_):
                return batch_mm(lambda j, hs: fsl(l, j, hs), lambda j, hs: fsl(r_, j, hs), [l.b, r_.b])

            evac_mask(A, bmmf(alT, beT), MSL)
            evac_mask(AT, bmmf(beT, alT), MSU)
            W0 = Wt[0]
            S.op("dve", lambda e: e.tensor_tensor(flat(W0), ID8.t[:], flat(AT), ALU.subtract), reads=[ID8.b, AT.b], writes=[W0.b])
            evac_copy(Pt[0], bmm3(AT, A))
            evac_copy(PTt[0], bmm3(A, AT))
            for i in range(5):
                Wc, Pc, PTc = Wt[i % 2], Pt[i % 2], PTt[i % 2]
                Wn, Pn_, PTn = Wt[(i + 1) % 2], Pt[(i + 1) % 2], PTt[(i + 1) % 2]
                evac_add(Wn, bmm3(Pc, Wc), Wc)
                if i < 4:
                    evac_copy(Pn_, bmm3(PTc, Pc))
                    evac_copy(PTn, bmm3(Pc, PTc))
            Wf = Wt[5 % 2]
            evac_mask(AakT, bmmf(kaT, alT), MSU)
            evac_mask(nArbT, bmmf(beT, rhT), NMIU)
            evac_mask(ArkT, bmmf(kaT, rhT), MIU)
            yo = YO[sc % 2]; yt = YT[sc % 2]
            yield

            def chunk(j):
                c = sc * 8 + j
                st0 = ST[cnt["st"] % 2]; st1 = ST[(cnt["st"] + 1) % 2]; cnt["st"] += 1
                gcol = GC.t[:, c:c + 1]
                sT = lambda t_: (lambda hs: t_.t[hs, :])
                psr = nxt(S, PS)
                mm2(psr, 0, lambda hs: fsl(alT, j, hs), sT(st0), [alT.b, st0.b], True, False)
                mm2(psr, 0, lambda hs: f3(AakT, j, hs), lambda hs: f3(vm, j, hs), [AakT.b, vm.b], False, True)
                S.op("act", lambda e: e.activation(RHS.t[:], psr.t[:, 0:64], AF.Copy), reads=[psr.b], writes=[RHS.b])
                S.op("dve", lambda e: e.tensor_scalar(STg.t[:], st0.t[:], gcol, None, ALU.mult), reads=[st0.b, GC.b], writes=[STg.b])
                psu = nxt(S, PS)
                mm2(psu, 0, lambda hs: f3(Wf, j, hs), sT(RHS), [Wf.b, RHS.b])
                S.op("act", lambda e: e.activation(US.t[:], psu.t[:, 0:64], AF.Copy), reads=[psu.b], writes=[US.b])
                psy = nxt(S, PS)
                mm2(psy, 0, lambda hs: fsl(rhT, j, hs), sT(st0), [rhT.b, st0.b], True, False)
                mm2(psy, 0, lambda hs: f3(nArbT, j, hs), sT(US), [nArbT.b, US.b], False, False)
                mm2(psy, 0, lambda hs: f3(ArkT, j, hs), lambda hs: f3(vm, j, hs), [ArkT.b, vm.b], False, True)
                psd = nxt(S, PS)
                mm2(psd, 0, lambda hs: f3(nbe, j, hs), sT(US), [nbe.b, US.b], True, False)
                mm2(psd, 0, lambda hs: f3(ka, j, hs), lambda hs: f3(vm, j, hs), [ka.b, vm.b], False, True)
                S.op("dve", lambda e: e.scalar_tensor_tensor(st1.t[:], psd.t[:, 0:64], gcol, STg.t[:], ALU.mult, ALU.add),
                     reads=[psd.b, GC.b, STg.b], writes=[st1.b])
                py = psy.t[:, 0:64]
                S.op("dve", lambda e: e.bn_stats(stat.t[:], py), reads=[psy.b], writes=[stat.b])
                S.op("dve", lambda e: e.bn_aggr(mv.t[:], stat.t[:]), reads=[stat.b], writes=[mv.b])
                S.op("dve", lambda e: e.tensor_scalar(rstd.t[:], mv.t[:, 1:2], GN_EPS, None, ALU.add), reads=[mv.b], writes=[rstd.b])
                S.op("act", lambda e: e.activation(rstd.t[:], rstd.t[:], AF.Sqrt), reads=[rstd.b], writes=[rstd.b])
                S.op("dve", lambda e: e.reciprocal(rstd.t[:], rstd.t[:]), reads=[rstd.b], writes=[rstd.b])
                S.op("dve", lambda e: e.tensor_scalar(yn.t[:], py, mv.t[:, 0:1], rstd.t[:, 0:1], ALU.subtract, ALU.mult),
                     reads=[psy.b, mv.b, rstd.b], writes=[yn.b])
                S.op("dve", lambda e: e.tensor_tensor(yn.t[:], yn.t[:], LG.t[:], ALU.mult), reads=[yn.b, LG.b], writes=[yn.b])
                S.op("dve", lambda e: e.tensor_tensor(yn.t[:], yn.t[:], LB.t[:], ALU.add), reads=[yn.b, LB.b], writes=[yn.b])
                S.op("dve", lambda e: e.tensor_tensor(bt.t[:], bo.t[:, j, :], vm.t[:, j, :], ALU.mult), reads=[bo.b, vm.b], writes=[bt.b])
                S.op("dve", lambda e: e.tensor_tensor(yn.t[:], yn.t[:], bt.t[:], ALU.add), reads=[yn.b, bt.b], writes=[yn.b])
                S.op("dve", lambda e: e.tensor_tensor(yo.t[:, j, :], yn.t[:], gt.t[:, j, :], ALU.mult), reads=[yn.b, gt.b], writes=[yo.b])
            for j in range(8):
                chunk(j)
                yield
            ps = batch_mm(lambda j, hs: f3(yo, j, hs), lambda j, hs: ID8.t[hs, 0:64], [yo.b, ID8.b])
            S.op("act", lambda e: e.activation(yt.t[:], ps.t[:], AF.Copy), reads=[ps.b], writes=[yt.b])
            st(S, yt, D["yr"][r, tsl], yt.t[:])
            cast_step(); cast_step()
        for sc in range(NSC):
            yield from superchunk(sc)
    for hp in range(4):
        yield from pair(hp)
    while cnt["cast"] < 2 * NCH:
        cast_step()


def phase4(S, PS, D):
    WG = T(S, "WG", [128, 8, 1024], BF16); PA = T(S, "PA", [128, 4, 1024], BF16); PB = T(S, "PB", [128, 4, 1024], BF16)
    WO = T(S, "WO", [128, 8, 1024], BF16); WST = [T(S, f"WST{i}", [128, 1024]) for i in range(2)]
    LNG = T(S, "LNG", [128, 1024]); LNB = T(S, "LNB", [128, 1024]); MS = T(S, "MS4", [128, 2]); IDN = T(S, "IDN4", [128, 128])
    wgv = D["wg"].rearrange("(k p) c -> p k c", p=128)
    cw = {"n": 0}

    def ldw(dst, dslice, src):
        ws = WST[cw["n"] % 2]; cw["n"] += 1
        ld(S, ws, ws.t[:], src)
        S.op("pool", lambda e: e.tensor_copy(dslice, ws.t[:]), reads=[ws.b], writes=[dst.b])
    for kc in range(8):
        ldw(WO, WO.t[:, kc, :], D["wo"].rearrange("(k p) c -> p k c", p=128)[:, kc, :])
    for kc in range(4):
        ldw(PA, PA.t[:, kc, :], D["pa"].rearrange("(k p) c -> p k c", p=128)[:, kc, :])
        ldw(PB, PB.t[:, kc, :], D["pb"].rearrange("(k p) c -> p k c", p=128)[:, kc, :])
    ld(S, LNG, LNG.t[:], D["lg1"]); ld(S, LNB, LNB.t[:], D["lb1"]); ld(S, MS, MS.t[:], D["msel"]); ld(S, IDN, IDN.t[:], D["idn"])
    XTF = T(S, "XTF", [128, 8, 512]); XT = T(S, "XT", [128, 8, 512], BF16)
    YF = T(S, "YF", [128, 4, 512], BF16); YR = T(S, "YR", [128, 4, 512], BF16); YRb = T(S, "YRb", [128, 4, 512], BF16)
    MT = T(S, "MT", [128, 8, 512], BF16); SG = T(S, "SG", [128, 512])
    XR = T(S, "XR", [128, 1024]); Z = T(S, "Z", [128, 1024]); X1 = T(S, "X14", [128, 1024]); TP = T(S, "TP", [128, 512])
    stat = T(S, "stat4", [128, 12]); mv = T(S, "mv4", [128, 2]); rstd = T(S, "rstd4", [128, 1])
    yrv = D["yr"].rearrange("(k p) t -> p k t", p=128)

    def superchunk(i):
        tsl = slice(i * 512, (i + 1) * 512)
        ld(S, XTF, XTF.t[:], D["xTo"].rearrange("(k p) t -> p k t", p=128)[:, :, tsl])
        S.op("pool", lambda e: e.tensor_copy(XT.t[:], XTF.t[:]), reads=[XTF.b], writes=[XT.b])
        ld(S, YF, YF.t[:], D["yfo"].rearrange("(k p) t -> p k t", p=128)[:, :, tsl])
        ld(S, YR, YR.t[:], yrv[:, :, (2 * i) * 512:(2 * i + 1) * 512])
        ld(S, YRb, YRb.t[:], yrv[:, :, (2 * i + 1) * 512:(2 * i + 2) * 512])
        fl = lambda t_: t_.t[:].rearrange("p a b -> p (a b)")
        S.op("dve", lambda e: e.tensor_scalar(fl(YR), fl(YR), MS.t[:, 0:1], None, ALU.mult), reads=[YR.b, MS.b], writes=[YR.b])
        S.op("dve", lambda e: e.scalar_tensor_tensor(fl(YR), fl(YRb), MS.t[:, 1:2], fl(YR), ALU.mult, ALU.add),
             reads=[YR.b, YRb.b, MS.b], writes=[YR.b])

        def branch(goff, PW, Y, first):
            for kc in range(8):
                ldw(WG, WG.t[:, kc, :], wgv[:, kc, goff:goff + 1024])

            def nblock(nb):
                psg = nxt(S, PS)
                for kc in range(8):
                    S.op("pe", lambda e, kc=kc: e.matmul(psg.t[:], WG.t[:, kc, nb * 128:nb * 128 + 128], XT.t[:, kc, :],
                                                         start=(kc == 0), stop=(kc == 7)), reads=[WG.b, XT.b], writes=[psg.b])
                S.op("act", lambda e: e.activation(SG.t[:], psg.t[:], AF.Sigmoid), reads=[psg.b], writes=[SG.b])
                psz = nxt(S, PS)
                for kc in range(4):
                    S.op("pe", lambda e, kc=kc: e.matmul(psz.t[:], PW.t[:, kc, nb * 128:nb * 128 + 128], Y.t[:, kc, :],
                                                         start=(kc == 0), stop=(kc == 3)), reads=[PW.b, Y.b], writes=[psz.b])
                if first:
                    S.op("dve", lambda e: e.tensor_tensor(MT.t[:, nb, :], psz.t[:], SG.t[:], ALU.mult), reads=[psz.b, SG.b], writes=[MT.b])
                else:
                    S.op("dve", lambda e: e.tensor_tensor(SG.t[:], psz.t[:], SG.t[:], ALU.mult), reads=[psz.b, SG.b], writes=[SG.b])
                    S.op("dve", lambda e: e.tensor_tensor(MT.t[:, nb, :], MT.t[:, nb, :], SG.t[:], ALU.add), reads=[MT.b, SG.b], writes=[MT.b])
            for nb in range(8):
                nblock(nb)
        branch(0, PA, YF, True)
        branch(1024, PB, YR, False)

        def ttile(tt):
            r0 = i * 512 + tt * 128
            ld(S, XR, XR.t[:], D["xo"][r0:r0 + 128, :])

            def half(hf):
                ps = nxt(S, PS)
                for nb in range(8):
                    S.op("pe", lambda e, nb=nb: e.matmul(ps.t[:], MT.t[:, nb, tt * 128:(tt + 1) * 128], WO.t[:, nb, hf * 512:(hf + 1) * 512],
                                                         start=(nb == 0), stop=(nb == 7)), reads=[MT.b, WO.b], writes=[ps.b])
                S.op("dve", lambda e: e.scalar_tensor_tensor(Z.t[:, hf * 512:(hf + 1) * 512], XR.t[:, hf * 512:(hf + 1) * 512], DN_ALPHA,
                                                             ps.t[:], ALU.mult, ALU.add), reads=[XR.b, ps.b], writes=[Z.b])
            half(0); half(1)
            layer_norm_tile(S, Z, X1, LNG, LNB, stat, mv, rstd)
            st(S, X1, D["x1"][r0:r0 + 128, :], X1.t[:])

            def tgroup(gq):
                ps = nxt(S, PS)
                for bi in range(4):
                    kc = gq * 4 + bi
                    S.op("pe", lambda e, bi=bi, kc=kc: e.matmul(ps.t[:, bi * 128:(bi + 1) * 128], X1.t[:, kc * 128:(kc + 1) * 128], IDN.t[:],
                                                                start=True, stop=True), reads=[X1.b, IDN.b], writes=[ps.b])
                S.op("act", lambda e: e.activation(TP.t[:], ps.t[:], AF.Copy), reads=[ps.b], writes=[TP.b])
                st(S, TP, D["x1T"][gq * 512:(gq + 1) * 512, r0:r0 + 128].rearrange("(bi p) t -> p bi t", p=128),
                   TP.t[:].rearrange("p (bi t) -> p bi t", t=128))
            tgroup(0); tgroup(1)
        for tt in range(4):
            ttile(tt)
    for i in range(NTOK // 512):
        superchunk(i)


def phase5(S, PS, D, ntile=NTOK // 128):
    x1d = D["x1"]; x1Td = D["x1T"]; wqd = D["wq"]; skd = D["sk"]
    lgd = D["lg2"]; lbd = D["lb2"]; iod = D["iota"]; od = D["out"]
    WQ = T(S, "WQ", [128, 8, 2048]); SK = T(S, "SK", [128, 16, 128])
    LNG = T(S, "LNG", [128, 1024]); LNB = T(S, "LNB", [128, 1024])
    for kc in range(8):
        ld(S, WQ, WQ.t[:, kc, :], wqd.rearrange("(k p) c -> p k c", p=128)[:, kc, :], nowaw=True)
    ld(S, SK, SK.t[:], skd.rearrange("p (b k) -> p b k", k=128))
    ld(S, LNG, LNG.t[:], lgd); ld(S, LNB, LNB.t[:], lbd)
    IOT = T(S, "IOT", [128, 256]); ld(S, IOT, IOT.t[:], iod)
    BPU = T(S, "BPU", [128, 16], U32); BPF = T(S, "BPF", [128, 16])
    X1 = [T(S, f"X1_{i}", [128, 1024]) for i in range(2)]; X1T = T(S, "X1T", [128, 8, 128])
    QT = T(S, "QT", [128, 16, 128]); SC = T(S, "SC", [128, 16, 128]); SC2 = T(S, "SC2", [128, 128])
    TS = T(S, "TS", [128, 16, 16]); TI = T(S, "TI", [128, 16, 16], U32); TIF = T(S, "TIF", [128, 16, 16]); TI128 = T(S, "TI128", [128, 16, 16])
    CS = T(S, "CS", [128, 256]); CI = T(S, "CI", [128, 256]); CS2 = T(S, "CS2", [128, 256]); JKs = [T(S, f"JK{i}", [128, 256]) for i in range(2)]
    BS = T(S, "BS", [128, 8, 16]); BP = T(S, "BP", [128, 8], U32); IDF = T(S, "IDF", [128, 128]); IDX = [T(S, f"IDX{i}", [128, 128], U32) for i in range(2)]
    NM = T(S, "NM", [128, 8]); EX = T(S, "EX", [128, 8, 16]); SM = T(S, "SM", [128, 8]); GW = [T(S, f"GW{i}", [128, 128]) for i in range(2)]
    UV = [[T(S, f"UV{p}_{i}", [128, 2048], BF16) for i in range(8)] for p in range(2)]
    DG = [T(S, f"DG{i}", [128, 128], BF16) for i in range(4)]; IDB = T(S, "IDB", [128, 128], BF16); IDF32 = T(S, "IDF32", [128, 128])
    ld(S, IDF32, IDF32.t[:], D["idn"])
    S.op("dve", lambda e: e.tensor_copy(IDB.t[:], IDF32.t[:]), reads=[IDF32.b], writes=[IDB.b])
    X1B = [T(S, f"X1B{i}", [128, 1024], BF16) for i in range(2)]
    JK2s = [T(S, f"JK2_{i}", [128, 1024], BF16) for i in range(2)]; H = T(S, "H", [128, 128]); HG = T(S, "HG", [128, 128])
    Z = T(S, "Z", [128, 1024]); OUT = T(S, "OUT", [128, 1024])
    ACCP = PS[6:8]; PS = PS[0:6]; uvd = D["uvb"]
    stat = T(S, "stat5", [128, 12]); mv = T(S, "mv5", [128, 2]); rstd = T(S, "rstd5", [128, 1])
    cnt = {"ub": 0, "dg": 0, "jk": 0, "jk2": 0}

    def front(ti):
        r0 = ti * 128
        X1c, X1Bc, IDXc, GWc = X1[ti % 2], X1B[ti % 2], IDX[ti % 2], GW[ti % 2]
        ld(S, X1c, X1c.t[:], x1d[r0:r0 + 128, :])
        S.op("dve", lambda e: e.tensor_copy(X1Bc.t[:], X1c.t[:]), reads=[X1c.b], writes=[X1Bc.b])
        ld(S, X1T, X1T.t[:], x1Td.rearrange("(k p) t -> p k t", p=128)[:, :, r0:r0 + 128])

        def qgroup(gq):
            ps = nxt(S, PS)
            for bi in range(4):
                blk = gq * 4 + bi
                for kc in range(8):
                    S.op("pe", lambda e, bi=bi, blk=blk, kc=kc: e.matmul(ps.t[:, bi * 128:(bi + 1) * 128], WQ.t[:, kc, blk * 128:(blk + 1) * 128],
                                                                         X1T.t[:, kc, :], start=(kc == 0), stop=(kc == 7)),
                         reads=[WQ.b, X1T.b], writes=[ps.b])
            S.op("act", lambda e: e.activation(QT.t[:, gq * 4:(gq + 1) * 4, :].rearrange("p a b -> p (a b)"), ps.t[:], AF.Copy),
                 reads=[ps.b], writes=[QT.b])
        for gq in range(4):
            qgroup(gq)
            yield

        def sgroup(gq):
            ps = nxt(S, PS)
            for bi in range(4):
                blk = gq * 4 + bi
                S.op("pe", lambda e, bi=bi, blk=blk: e.matmul(ps.t[:, bi * 128:(bi + 1) * 128], QT.t[:, blk, :], SK.t[:, blk, :],
                                                              start=True, stop=True), reads=[QT.b, SK.b], writes=[ps.b])
            S.op("act", lambda e: e.activation(SC.t[:, gq * 4:(gq + 1) * 4, :].rearrange("p a b -> p (a b)"), ps.t[:], AF.Copy),
                 reads=[ps.b], writes=[SC.b])
        for gq in range(4):
            sgroup(gq)
            yield

        def top16(blk):
            S.op("dve", lambda e: e.max(TS.t[:, blk, 0:8], SC.t[:, blk, :]), reads=[SC.b], writes=[TS.b])
            S.op("dve", lambda e: e.max_index(TI.t[:, blk, 0:8], TS.t[:, blk, 0:8], SC.t[:, blk, :]), reads=[SC.b, TS.b], writes=[TI.b])
            S.op("dve", lambda e: e.match_replace(SC2.t[:], TS.t[:, blk, 0:8], SC.t[:, blk, :], -1e30), reads=[SC.b, TS.b], writes=[SC2.b])
            S.op("dve", lambda e: e.max(TS.t[:, blk, 8:16], SC2.t[:]), reads=[SC2.b], writes=[TS.b])
            S.op("dve", lambda e: e.max_index(TI.t[:, blk, 8:16], TS.t[:, blk, 8:16], SC2.t[:]), reads=[SC2.b, TS.b], writes=[TI.b])
        for blk in range(16):
            top16(blk)
            yield
        S.op("dve", lambda e: e.tensor_copy(TIF.t[:], TI.t[:]), reads=[TI.b], writes=[TIF.b])
        S.op("dve", lambda e: e.tensor_scalar(TI128.t[:], TIF.t[:], 128.0, None, ALU.mult), reads=[TIF.b], writes=[TI128.b])

        def head(h):
            for a in range(16):
                S.op("dve", lambda e, a=a: e.tensor_scalar(CS.t[:, a * 16:(a + 1) * 16], TS.t[:, 2 * h + 1, :], TS.t[:, 2 * h, a:a + 1], None, ALU.add),
                     reads=[TS.b], writes=[CS.b], soft=((CS.b,) if a > 0 else ()))
                S.op("dve", lambda e, a=a: e.tensor_scalar(CI.t[:, a * 16:(a + 1) * 16], TIF.t[:, 2 * h + 1, :], TI128.t[:, 2 * h, a:a + 1], None, ALU.add),
                     reads=[TIF.b, TI128.b], writes=[CI.b], soft=((CI.b,) if a > 0 else ()))
            S.op("dve", lambda e: e.max(BS.t[:, h, 0:8], CS.t[:]), reads=[CS.b], writes=[BS.b])
            S.op("dve", lambda e: e.max_index(BPU.t[:, 0:8], BS.t[:, h, 0:8], CS.t[:]), reads=[CS.b, BS.b], writes=[BPU.b])
            S.op("dve", lambda e: e.match_replace(CS2.t[:], BS.t[:, h, 0:8], CS.t[:], -1e30), reads=[CS.b, BS.b], writes=[CS2.b])
            S.op("dve", lambda e: e.max(BS.t[:, h, 8:16], CS2.t[:]), reads=[CS2.b], writes=[BS.b])
            S.op("dve", lambda e: e.max_index(BPU.t[:, 8:16], BS.t[:, h, 8:16], CS2.t[:]), reads=[CS2.b, BS.b], writes=[BPU.b])
            S.op("dve", lambda e: e.tensor_copy(BPF.t[:], BPU.t[:]), reads=[BPU.b], writes=[BPF.b])
            for k in range(16):
                def pick(k=k):
                    jk = JKs[cnt["jk"] % 2]; cnt["jk"] += 1
                    S.op("dve", lambda e: e.scalar_tensor_tensor(jk.t[:], IOT.t[:], BPF.t[:, k:k + 1], CI.t[:], ALU.is_equal, ALU.mult,
                                                                 accum_out=IDF.t[:, h * 16 + k:h * 16 + k + 1]),
                         reads=[IOT.b, BPF.b, CI.b], writes=[jk.b, IDF.b], soft=((IDF.b,) if (h > 0 or k > 0) else ()))
                pick()
            S.op("dve", lambda e: e.tensor_scalar(NM.t[:, h:h + 1], BS.t[:, h, 0:1], -1.0, None, ALU.mult), reads=[BS.b], writes=[NM.b])
            S.op("act", lambda e: e.activation(EX.t[:, h, :], BS.t[:, h, :], AF.Exp, bias=NM.t[:, h:h + 1], accum_out=SM.t[:, h:h + 1]),
                 reads=[BS.b, NM.b], writes=[EX.b, SM.b])
        for h in range(8):
            head(h)
            yield
        S.op("dve", lambda e: e.tensor_copy(IDXc.t[:], IDF.t[:]), reads=[IDF.b], writes=[IDXc.b])
        S.op("dve", lambda e: e.reciprocal(SM.t[:], SM.t[:]), reads=[SM.b], writes=[SM.b])
        for h in range(8):
            S.op("dve", lambda e, h=h: e.tensor_scalar(GWc.t[:, h * 16:(h + 1) * 16], EX.t[:, h, :], SM.t[:, h:h + 1], None, ALU.mult),
                 reads=[EX.b, SM.b], writes=[GWc.b])

    def back(ti, fg):
        r0 = ti * 128
        X1c, X1Bc, IDXc, GWc = X1[ti % 2], X1B[ti % 2], IDX[ti % 2], GW[ti % 2]

        def group(gi):
            bufs = UV[gi % 2]
            for k in range(8):
                def g1(k=k):
                    sl_ = gi * 8 + k
                    ub = bufs[k]
                    S.dma("pool", None, None, ub.semq("pool"), reads=[IDXc.b], writes=[ub.b],
                          fn=lambda e: e.indirect_dma_start(out=ub.t[:], out_offset=None, in_=uvd,
                                                            in_offset=bass.IndirectOffsetOnAxis(ap=IDXc.t[:, sl_:sl_ + 1], axis=0)))
                g1()
            for k in range(8):
                def d1(k=k):
                    sl_ = gi * 8 + k
                    ub = bufs[k]
                    jk2 = JK2s[cnt["jk2"] % 2]; cnt["jk2"] += 1
                    S.op("dve", lambda e: e.scalar_tensor_tensor(jk2.t[:], ub.t[:, 0:1024], 1.0, X1Bc.t[:], ALU.mult, ALU.mult,
                                                                 accum_out=H.t[:, sl_:sl_ + 1]), reads=[ub.b, X1Bc.b], writes=[jk2.b, H.b],
                         soft=((H.b,) if k > 0 else ()))
                d1()
            gs = slice(gi * 8, gi * 8 + 8)
            S.op("act", lambda e: e.activation(HG.t[:, gs], H.t[:, gs], AF.Gelu), reads=[H.b], writes=[HG.b])
            S.op("dve", lambda e: e.tensor_tensor(HG.t[:, gs], HG.t[:, gs], GWc.t[:, gs], ALU.mult), reads=[HG.b, GWc.b], writes=[HG.b])
            for k in range(8):
                def v1(k=k):
                    sl_ = gi * 8 + k
                    ub = bufs[k]
                    dg = DG[cnt["dg"] % 4]; cnt["dg"] += 1
                    S.op("act", lambda e: e.activation(dg.t[:], IDB.t[:], AF.Copy, scale=HG.t[:, sl_:sl_ + 1]), reads=[IDB.b, HG.b], writes=[dg.b])
                    for hf in range(2):
                        S.op("pe", lambda e, hf=hf: e.matmul(ACCP[hf].t[:], dg.t[:], ub.t[:, 1024 + hf * 512:1024 + (hf + 1) * 512],
                                                             start=(sl_ == 0), stop=(sl_ == 127)), reads=[dg.b, ub.b], writes=[ACCP[hf].b])
                v1()
        for gi in range(16):
            group(gi)
            next(fg, None); next(fg, None)
        for hf in range(2):
            S.op("dve", lambda e, hf=hf: e.scalar_tensor_tensor(Z.t[:, hf * 512:(hf + 1) * 512], X1c.t[:, hf * 512:(hf + 1) * 512], DN_ALPHA,
                                                                ACCP[hf].t[:], ALU.mult, ALU.add), reads=[X1c.b, ACCP[hf].b], writes=[Z.b])
        layer_norm_tile(S, Z, OUT, LNG, LNB, stat, mv, rstd)
        st(S, OUT, od[r0:r0 + 128, :], OUT.t[:], final=True)
    fg = front(0)
    for _ in fg:
        pass
    for ti in range(ntile):
        nfg = front(ti + 1) if ti + 1 < ntile else iter(())
        back(ti, nfg)
        for _ in nfg:
            pass


def phase23(S, PS, D):
    g2 = phase2(S, PS[0:4], D)
    g3 = phase3(S, PS[4:8], D)
    import os
    mode = os.environ.get("MK_MODE", "seq")
    if mode == "seq":
        for _ in g2:
            pass
        for _ in g3:
            pass
        return
    done2 = done3 = False
    while not (done2 and done3):
        for _ in range(2):
            if not done2:
                try:
                    next(g2)
                except StopIteration:
                    done2 = True
        if not done3:
            try:
                next(g3)
            except StopIteration:
                done3 = True


def build_fused(nph=4):
    nc = bass.Bass("TRN2", target_bir_lowering=False)
    D = {}
    for n, shp in (("xT", [1024, SEQ]), ("xTo", [1024, NTOK]), ("xo", [NTOK, 1024]), ("Wg", [2, 1024, 1796]), ("pcol", [2, 128, 19]),
                   ("w2a2", [2, 128, 256]), ("g2", [2, 128, 256]), ("bd", [128, 128]), ("cm", [128, 512]), ("idn", [128, 128]),
                   ("msel", [128, 2]), ("neg", [8, 128, 512]), ("mk", [5, 64, 512]), ("lg3", [8, 64, 64]), ("lb3", [8, 64, 64]),
                   ("wg", [1024, 2048]), ("pa", [512, 1024]), ("pb", [512, 1024]), ("wo", [1024, 1024]), ("lg1", [128, 1024]),
                   ("lb1", [128, 1024]), ("wq", [1024, 2048]), ("sk", [128, 2048]), ("u", [16384, 1024]), ("v", [16384, 1024]),
                   ("lg2", [128, 1024]), ("lb2", [128, 1024]), ("iota", [128, 256])):
        D[n] = din(nc, n, shp)
    D["out"] = dout(nc, "out", [NTOK, 1024])
    for n, shp in (("fc", [8, SEQ]),
                   ("al", [512, SEQ]), ("be", [512, SEQ]), ("ka", [512, SEQ]), ("rh", [512, SEQ]), ("gc", [512, SEQ // 64]),
                   ("nbe_tm", [SEQ, 512]), ("ka_tm", [SEQ, 512]), ("rv_tm", [SEQ, 512]), ("rg_tm", [SEQ, 512]), ("bo_tm", [SEQ, 512]),
                   ("x1", [NTOK, 1024]), ("x1T", [1024, NTOK])):
        D[n] = dscr(nc, "s_" + n, shp)
    for n, shp in (("fqo", [512, NTOK]), ("fk", [512, SEQ]), ("fvtm", [SEQ, 512]), ("fnc3", [8, 3, SEQ]), ("co3", [8, 3, NTOK]), ("yfo", [512, NTOK]), ("yr", [512, SEQ]),
                   ("uvb", [16384, 2048])):
        D[n] = dscr(nc, "s_" + n, shp, BF16)
    S = Sched(nc)
    PS = mk_psum(S)
    phases = (phase1, phase23, phase4, phase5)[:nph]
    for i, ph in enumerate(phases):
        S.phase_begin()
        ph(S, PS, D)
        S.phase_end(final=(i == len(phases) - 1))
    S.stack.close()
    return nc, S


def core_inputs(x, P, b, t):
    c_ = np.ascontiguousarray
    xb = x[b]
    own = xb.reshape(16, 512, 1024)[t::2].reshape(NTOK, 1024)
    bc = lambda v: c_(np.broadcast_to(v[None, :], (128, v.shape[0])))
    RB = 1544
    mu = P["rwkv_mu"]
    Wg = []; pcols = []; w2a2 = []; g2 = []
    two = lambda v: v.reshape(2, 128).T
    for g in range(2):
        ch = slice(256 * g, 256 * g + 256)
        cols = np.concatenate([
            np.arange(256 * g, 256 * g + 256), 512 + np.arange(256 * g, 256 * g + 256), 1024 + np.arange(256 * g, 256 * g + 256),
            RB + np.arange(256 * g, 256 * g + 256), RB + 512 + np.arange(256 * g, 256 * g + 256),
            RB + 1024 + np.arange(256 * g, 256 * g + 256), RB + np.arange(1536, 1792), 1536 + np.arange(4 * g, 4 * g + 4)])
        Wg.append(P["w_in"][:, cols])
        pc = np.zeros((128, 19), np.float32)
        pc[:, 0:2] = two(mu[0:512][ch]); pc[:, 2:4] = two(mu[512:1024][ch]); pc[:, 4:6] = two(mu[1024:1536][ch])
        pc[:, 6] = mu[1536:1664]; pc[:, 7] = mu[1664:1792]
        pc[:, 8:10] = two(P["rwkv_w0"][ch]); pc[:, 10:12] = two(P["rwkv_a0"][ch])
        pc[:, 12:14] = two(P["rwkv_k_k"][ch]); pc[:, 14:16] = two(P["rwkv_k_a"][ch]); pc[:, 16:18] = two(P["rwkv_r_k"][ch])
        pc[0:4, 18] = P["fox_f_bias"][4 * g:4 * g + 4]
        pcols.append(pc)
        w2a2.append(np.concatenate([P["rwkv_w2"][:, ch], P["rwkv_a2"][:, ch]], 0))
        g2.append(P["rwkv_g2"][:, ch])
    bd = np.kron(np.eye(2, dtype=np.float32), np.ones((64, 64), np.float32))
    cm = np.ones((128, 512), np.float32); cm[:, ::64] = 0.0
    msel = np.zeros((128, 2), np.float32); msel[:, t] = 1.0
    kpos = (128 * np.arange(8)[:, None, None] + np.arange(128)[None, :, None])
    qpos = 512 * t + np.arange(512)[None, None, :]
    neg = np.where(kpos <= qpos, 0.0, -30000.0).astype(np.float32)
    one = np.ones((64, 64), np.float32)
    rep = lambda m: np.tile(m, (1, 8))
    mk = np.stack([rep(np.tril(one, -1)), rep(np.triu(one, 1)), rep(np.triu(one)), rep(-np.triu(one)), rep(np.eye(64, dtype=np.float32))])
    lg3 = np.stack([np.broadcast_to(P["rwkv_ln_g"][h * 64:(h + 1) * 64][None, :], (64, 64)) for h in range(8)])
    lb3 = np.stack([np.broadcast_to(P["rwkv_ln_b"][h * 64:(h + 1) * 64][None, :], (64, 64)) for h in range(8)])
    sk = P["peer_sub_keys"].reshape(16, 128, 128).transpose(2, 0, 1).reshape(128, 16 * 128)
    iota = np.broadcast_to(np.arange(256, dtype=np.float32)[None, :], (128, 256))
    m = {"xT": xb.T, "xTo": own.T, "xo": own, "Wg": np.stack(Wg), "pcol": np.stack(pcols), "w2a2": np.stack(w2a2), "g2": np.stack(g2),
         "bd": bd, "cm": cm, "idn": np.eye(128, dtype=np.float32), "msel": msel, "neg": neg, "mk": mk, "lg3": lg3, "lb3": lb3,
         "wg": P["w_in"][:, 3336:], "pa": P["p_fox"], "pb": P["p_rwkv"], "wo": P["w_o"], "lg1": bc(P["ln1_g"]), "lb1": bc(P["ln1_b"]),
         "wq": P["peer_w_q"], "sk": sk, "u": P["peer_u"], "v": P["peer_v"], "lg2": bc(P["ln2_g"]), "lb2": bc(P["ln2_b"]), "iota": iota}
    return {k: c_(np.asarray(v, np.float32)) for k, v in m.items()}


def kernel(**inputs):
    x = np.asarray(inputs["x"], np.float32)
    P = {k: np.asarray(v, np.float32)[0] for k, v in inputs.items() if k != "x"}
    B = x.shape[0]
    nc, _ = build_fused()
    in_maps = [core_inputs(x, P, b, t) for b in range(B) for t in range(2)]
    res = run_bass_kernel_spmd(nc, in_maps, core_ids=list(range(2 * B)))
    out = np.empty_like(x)
    for b in range(B):
        ob = out[b].reshape(16, 512, 1024)
        for t in range(2):
            ob[t::2] = res.results[2 * b + t]["out"].reshape(8, 512, 1024)
    return out
```

```python
import numpy as np
import concourse.bass as bass
import concourse.mybir as mybir
from concourse.bass_utils import run_bass_kernel_spmd
from contextlib import ExitStack

F32 = mybir.dt.float32
BF16 = mybir.dt.bfloat16
U32 = mybir.dt.uint32
AF = mybir.ActivationFunctionType
ALU = mybir.AluOpType
AX = mybir.AxisListType

SYNC_SAME_ENGINE = True
EPOCH = 10 ** 9
DMA_MAX = 10 ** 6


class Buf:
    __slots__ = ("name", "w", "r")

    def __init__(self, name):
        self.name = name
        self.w = None
        self.r = {}


class DmaSem:
    __slots__ = ("handle", "count", "uid")
    _n = 0

    def __init__(self, handle):
        self.handle = handle
        self.count = 0
        DmaSem._n += 1
        self.uid = DmaSem._n


class Sched:
    ENG = ("sp", "act", "dve", "pool", "pe")

    def __init__(self, nc):
        self.nc = nc
        self.streams = {e: [] for e in self.ENG}
        self.seq = {e: 0 for e in self.ENG}
        self.known = {e: {} for e in self.ENG}
        self.esem = {}
        self.stack = ExitStack()
        self.nsem = 0
        self.ninstr = 0
        self.out_toks = []
        self.pi = 0
        self.pstack = None
        self.all_dmasems = []
        self.free_dmasems = {"hw": [], "sw": []}
        self.phase_tiles = []

    def sbuf(self, name, shape, dtype):
        stk = self.pstack if self.pstack is not None else self.stack
        return stk.enter_context(self.nc.sbuf_tensor(name, list(shape), dtype))

    def phase_begin(self):
        self.pstack = ExitStack()

    def phase_end(self, final=False):
        if final:
            for tok in self.out_toks:
                self._wait("sp", tok)
        self.barrier()
        self.emit_block()
        self.pstack.close()
        self.pstack = None
        for tl in self.phase_tiles:
            for kind, sm in tl._sems.items():
                self.free_dmasems[kind].append(sm)
            tl._sems = {}
        self.phase_tiles = []

    def barrier(self):
        for e in self.ENG:
            for f in self.ENG:
                if f != e and self.seq[f] > 0:
                    self._wait(e, ("eng", f, self.seq[f]))
            for sem in self.all_dmasems:
                if sem.count > 0:
                    self._wait(e, ("dma", sem, 16 * sem.count))

    def emit_block(self):
        nc = self.nc
        with nc.Block() as block:
            for e, deco in (("sp", block.sync), ("act", block.scalar), ("dve", block.vector),
                            ("pool", block.gpsimd), ("pe", block.tensor)):
                stream = self.streams[e]

                def body(eng, stream=stream):
                    for th in stream:
                        th(eng)
                deco(body)
        self.streams = {e: [] for e in self.ENG}

    def psum(self, name, shape, dtype):
        return self.stack.enter_context(self.nc.psum_tensor(name, list(shape), dtype))

    def newsem(self, name):
        self.nsem += 1
        return self.nc.alloc_semaphore(name=f"{name}_{self.nsem}")

    def dmasem(self, name="d", kind="hw"):
        if self.free_dmasems[kind]:
            return self.free_dmasems[kind].pop()
        d = DmaSem(self.newsem(name + kind))
        self.all_dmasems.append(d)
        return d

    def _esem(self, e, epoch):
        k = (e, epoch)
        if k not in self.esem:
            self.esem[k] = self.newsem(f"e_{e}{epoch}")
        return self.esem[k]

    def _wait(self, e, tok):
        if tok is None:
            return
        kind, ident, val = tok
        key = ident if kind == "eng" else ("dma", ident.uid)
        if self.known[e].get(key, 0) >= val:
            return
        self.known[e][key] = val
        if kind == "eng":
            sem = self._esem(ident, (val - 1) // EPOCH)
            v = (val - 1) % EPOCH + 1
        else:
            sem = ident.handle
            v = val
        self.streams[e].append(lambda eng, sem=sem, v=v: eng.wait_ge(sem, v))
        self.ninstr += 1

    def _deps(self, e, reads, writes, nowaw=False, soft=()):
        strict = SYNC_SAME_ENGINE and e != "pe"
        for b in reads:
            if b.w is not None:
                if not (e == "pe" and b.w[0] == "eng" and b.w[1] == "pe"):
                    self._wait(e, b.w)
        for b in writes:
            if b.w is not None and not nowaw:
                same = (b.w[0] == "eng" and b.w[1] == e)
                if (strict and not (same and b in soft)) or not same:
                    self._wait(e, b.w)
            for tok in b.r.values():
                if strict or not (tok[0] == "eng" and tok[1] == e):
                    self._wait(e, tok)

    def _record(self, tok, reads, writes):
        for b in writes:
            b.w = tok
            b.r = {}
        key = tok[1] if tok[0] == "eng" else ("dma", tok[1].uid)
        for b in reads:
            b.r[key] = tok

    def op(self, e, fn, reads=(), writes=(), soft=()):
        self._deps(e, reads, writes, soft=soft)
        self.seq[e] += 1
        s = self.seq[e]
        sem = self._esem(e, (s - 1) // EPOCH)
        self.streams[e].append(lambda eng, fn=fn, sem=sem: fn(eng).then_inc(sem, 1))
        self.ninstr += 1
        tok = ("eng", e, s)
        self._record(tok, reads, writes)
        return tok

    def dma(self, q, out, in_, sem, reads=(), writes=(), nowaw=False, fn=None, **kw):
        self._deps(q, reads, writes, nowaw=nowaw)
        sem.count += 1
        assert sem.count < DMA_MAX, "dma sem overflow"
        h = sem.handle
        if fn is None:
            self.streams[q].append(
                lambda eng, out=out, in_=in_, h=h, kw=kw: eng.dma_start(out=out, in_=in_, **kw).then_inc(h, 16))
        else:
            self.streams[q].append(lambda eng, fn=fn, h=h: fn(eng).then_inc(h, 16))
        self.ninstr += 1
        tok = ("dma", sem, 16 * sem.count)
        self._record(tok, reads, writes)
        return tok

    def emit(self):
        for tok in self.out_toks:
            self._wait("sp", tok)
        nc = self.nc
        with nc.Block() as block:
            for e, deco in (("sp", block.sync), ("act", block.scalar), ("dve", block.vector),
                            ("pool", block.gpsimd), ("pe", block.tensor)):
                stream = self.streams[e]

                def body(eng, stream=stream):
                    for th in stream:
                        th(eng)
                deco(body)
        self.stack.close()


class T:
    def __init__(self, S, name, shape, dtype=F32, psum=False):
        self.S = S
        S.ntile = getattr(S, "ntile", 0) + 1
        self.t = (S.psum if psum else S.sbuf)(f"t{S.ntile}_" + name, shape, dtype)
        self.b = Buf(name)
        self._sems = {}
        self.name = name
        if not psum:
            S.phase_tiles.append(self)

    def semq(self, q):
        kind = "sw" if q == "pool" else "hw"
        if kind not in self._sems:
            self._sems[kind] = self.S.dmasem(self.name, kind)
        return self._sems[kind]

    @property
    def sem(self):
        return self.semq("sp")


def ld(S, tl, dst, src, q="sp", nowaw=False):
    return S.dma(q, dst, src, tl.semq(q), writes=[tl.b], nowaw=nowaw)


def st(S, tl, dst, src, q="sp", final=False):
    tok = S.dma(q, dst, src, tl.semq(q), reads=[tl.b])
    if final:
        S.out_toks.append(tok)
    return tok


def dscr(nc, name, shape, dt=F32):
    return nc.dram_tensor(name, list(shape), dt, kind="Internal").ap()


def mk_psum(S, n=8):
    return [T(S, f"ps{i}", [128, 512], F32, psum=True) for i in range(n)]


def nxt(S, PS):
    p = PS[S.pi % len(PS)]
    S.pi += 1
    return p


def din(nc, name, shape, dt=F32):
    return nc.dram_tensor(name, list(shape), dt, kind="ExternalInput").ap()


def dout(nc, name, shape, dt=F32):
    return nc.dram_tensor(name, list(shape), dt, kind="ExternalOutput").ap()


SEQ = 8192
NSC = SEQ // 512
NEG_E = -0.6065306597126334


NTOK = 4096
DN_ALPHA = 2.0 ** 0.25
LN_EPS = 1e-5


def layer_norm_tile(S, Z, OUT, LNG, LNB, stat, mv, rstd):
    for hf in range(2):
        S.op("dve", lambda e, hf=hf: e.bn_stats(stat.t[:, hf * 6:(hf + 1) * 6], Z.t[:, hf * 512:(hf + 1) * 512]),
             reads=[Z.b], writes=[stat.b])
    S.op("dve", lambda e: e.bn_aggr(mv.t[:], stat.t[:]), reads=[stat.b], writes=[mv.b])
    S.op("dve", lambda e: e.tensor_scalar(rstd.t[:], mv.t[:, 1:2], LN_EPS, None, ALU.add), reads=[mv.b], writes=[rstd.b])
    S.op("act", lambda e: e.activation(rstd.t[:], rstd.t[:], AF.Sqrt), reads=[rstd.b], writes=[rstd.b])
    S.op("dve", lambda e: e.reciprocal(rstd.t[:], rstd.t[:]), reads=[rstd.b], writes=[rstd.b])
    S.op("dve", lambda e: e.tensor_scalar(Z.t[:], Z.t[:], mv.t[:, 0:1], rstd.t[:, 0:1], ALU.subtract, ALU.mult),
         reads=[Z.b, mv.b, rstd.b], writes=[Z.b])
    S.op("dve", lambda e: e.tensor_tensor(Z.t[:], Z.t[:], LNG.t[:], ALU.mult), reads=[Z.b, LNG.b], writes=[Z.b])
    S.op("dve", lambda e: e.tensor_tensor(OUT.t[:], Z.t[:], LNB.t[:], ALU.add), reads=[Z.b, LNB.b], writes=[OUT.b])


def phase1(S, PS, D):
    xv = D["xT"].rearrange("(kc p) t -> p kc t", p=128)
    xov = D["xTo"].rearrange("(kc p) t -> p kc t", p=128)
    W = T(S, "W", [128, 8, 1796], BF16); WS = [T(S, f"WS{i}", [128, 1796]) for i in range(2)]
    PC = T(S, "PC", [128, 19]); W2 = T(S, "W2", [128, 256]); G2 = T(S, "G2", [128, 256])
    BD = T(S, "BD", [128, 128]); CM = T(S, "CM", [128, 512]); IDN = T(S, "IDN", [128, 128]); MS = T(S, "MS", [128, 2])
    ONE = T(S, "ONE", [4, 512]); CAR = T(S, "CAR", [4, 1])
    XF = [T(S, f"XF{i}", [128, 8, 512]) for i in range(2)]
    X = [T(S, f"X{i}", [128, 8, 512], BF16) for i in range(2)]

    def load_x(i, src):
        xf = XF[i % 2]; xt = X[i % 2]
        ld(S, xf, xf.t[:], src, q="pool")
        S.op("pool", lambda e: e.tensor_copy(xt.t[:], xf.t[:]), reads=[xf.b], writes=[xt.b])
        return xt
    ld(S, BD, BD.t[:], D["bd"]); ld(S, CM, CM.t[:], D["cm"]); ld(S, IDN, IDN.t[:], D["idn"]); ld(S, MS, MS.t[:], D["msel"])
    S.op("dve", lambda e: e.memset(ONE.t[:], 1.0), writes=[ONE.b])
    Pn = ["r0", "r1", "k0", "k1", "v0", "v1", "l", "g"]
    P = {n: T(S, "P" + n, [128, 513]) for n in Pn}
    SH = {n: T(S, "SH" + n, [128, 512]) for n in Pn}
    tmp = T(S, "tmp", [128, 512])
    FO = [T(S, f"FO{i}", [128, 512], BF16) for i in range(2)]
    TME = [T(S, f"TME{i}", [128, 512]) for i in range(2)]
    FL = T(S, "FL", [4, 512]); FC = T(S, "FCt", [4, 512]); FN = T(S, "FNt", [4, 512])
    SP3 = T(S, "SP3", [8, 3, 512], BF16); SPR = T(S, "SPR", [8, 512])
    TH = T(S, "TH", [64, 512]); SGL = T(S, "SGL", [128, 512])
    nm = ["lw", "a", "g", "t1", "sq", "nr", "kk", "t2", "kmod", "cs", "en", "ep", "d2", "epv",
          "alpha", "t3", "beta", "nbeta", "kappa", "rho", "pr", "bo"]
    R = {n: T(S, "R" + n, [128, 512]) for n in nm}
    GC = T(S, "GC", [128, 8])
    pcmap = {"r0": 0, "r1": 1, "k0": 2, "k1": 3, "v0": 4, "v1": 5, "l": 6, "g": 7}
    blkmap = {"r0": 6, "r1": 7, "k0": 8, "k1": 9, "v0": 10, "v1": 11, "l": 12, "g": 13}
    bfc = Buf("fc_scr")
    cnt = {"fo": 0, "tme": 0}

    def mm_block(xt, col0, ncol):
        ps = nxt(S, PS)
        for kc in range(8):
            S.op("pe", lambda e, kc=kc: e.matmul(ps.t[0:ncol, :], W.t[:, kc, col0:col0 + ncol], xt.t[:, kc, :],
                                                 start=(kc == 0), stop=(kc == 7)), reads=[W.b, xt.b], writes=[ps.b])
        return ps

    def split3(src, n, dst_ap):
        S.op("dve", lambda e: e.tensor_copy(SP3.t[0:n, 0, :], src.t[0:n, :]), reads=[src.b], writes=[SP3.b])
        S.op("dve", lambda e: e.tensor_tensor(SPR.t[0:n, :], src.t[0:n, :], SP3.t[0:n, 0, :], ALU.subtract), reads=[src.b, SP3.b], writes=[SPR.b])
        S.op("dve", lambda e: e.tensor_copy(SP3.t[0:n, 1, :], SPR.t[0:n, :]), reads=[SPR.b], writes=[SP3.b])
        S.op("dve", lambda e: e.tensor_tensor(SPR.t[0:n, :], SPR.t[0:n, :], SP3.t[0:n, 1, :], ALU.subtract), reads=[SPR.b, SP3.b], writes=[SPR.b])
        S.op("dve", lambda e: e.tensor_copy(SP3.t[0:n, 2, :], SPR.t[0:n, :]), reads=[SPR.b], writes=[SP3.b])
        st(S, SP3, dst_ap, SP3.t[0:n, :, :])

    def group(g):
        Wv = D["Wg"][g].rearrange("(kc p) c -> p kc c", p=128)
        for kc in range(8):
            def wl(kc=kc):
                ws = WS[kc % 2]
                ld(S, ws, ws.t[:], Wv[:, kc, :], q="pool")
                S.op("pool", lambda e: e.tensor_copy(W.t[:, kc, :], ws.t[:]), reads=[ws.b], writes=[W.b])
            wl()
        ld(S, PC, PC.t[:], D["pcol"][g]); ld(S, W2, W2.t[:], D["w2a2"][g]); ld(S, G2, G2.t[:], D["g2"][g])
        S.op("dve", lambda e: e.memset(CAR.t[:], 0.0), writes=[CAR.b])
        for n in Pn:
            S.op("dve", lambda e, n=n: e.memset(P[n].t[:, 0:1], 0.0), writes=[P[n].b])
        pc = lambda j: PC.t[:, j:j + 1]

        def superchunk(sc):
            tsl = slice(sc * 512, (sc + 1) * 512)
            xt = load_x(sc, xv[:, :, tsl])

            def foxk(blk):
                ps = mm_block(xt, blk * 128, 128)
                fo = FO[cnt["fo"] % 2]; cnt["fo"] += 1
                S.op("act", lambda e: e.activation(fo.t[:], ps.t[:], AF.Copy), reads=[ps.b], writes=[fo.b])
                r0 = g * 256 + (blk % 2) * 128
                st(S, fo, D["fk"][r0:r0 + 128, tsl], fo.t[:])
            foxk(2); foxk(3)

            def foxv(pair):
                ps = nxt(S, PS)
                for j in range(2):
                    tt = pair * 2 + j
                    for kc in range(8):
                        S.op("pe", lambda e, j=j, tt=tt, kc=kc: e.matmul(ps.t[:, j * 256:(j + 1) * 256], xt.t[:, kc, tt * 128:(tt + 1) * 128],
                                                                         W.t[:, kc, 512:768], start=(kc == 0), stop=(kc == 7)),
                             reads=[W.b, xt.b], writes=[ps.b])
                fo = FO[cnt["fo"] % 2]; cnt["fo"] += 1
                S.op("act", lambda e: e.activation(fo.t[:], ps.t[:], AF.Copy), reads=[ps.b], writes=[fo.b])
                r0 = sc * 512 + pair * 256
                st(S, fo, D["fvtm"][r0:r0 + 256, g * 256:(g + 1) * 256].rearrange("(j p) c -> p j c", p=128),
                   fo.t[:].rearrange("p (j c) -> p j c", c=256))
            foxv(0); foxv(1)
            ps = mm_block(xt, 1792, 4)
            S.op("act", lambda e: e.activation(FL.t[:], ps.t[0:4, :], AF.Sigmoid, bias=PC.t[0:4, 18:19]), reads=[ps.b, PC.b], writes=[FL.b])
            S.op("act", lambda e: e.activation(FL.t[:], FL.t[:], AF.Ln), reads=[FL.b], writes=[FL.b])
            S.op("dve", lambda e: e.tensor_tensor_scan(FC.t[:], ONE.t[:], FL.t[:], CAR.t[:, 0:1], ALU.mult, ALU.add),
                 reads=[ONE.b, FL.b, CAR.b], writes=[FC.b])
            S.op("dve", lambda e: e.tensor_copy(CAR.t[:], FC.t[:, 511:512]), reads=[FC.b], writes=[CAR.b])
            S.op("dve", lambda e: e.tensor_scalar(FN.t[:], FC.t[:], -1.0, None, ALU.mult), reads=[FC.b], writes=[FN.b])
            S.dma("sp", D["fc"][g * 4:(g + 1) * 4, tsl], FC.t[:], FC.sem, reads=[FC.b], writes=[bfc], nowaw=True)
            split3(FN, 4, D["fnc3"][g * 4:(g + 1) * 4, :, tsl])

            def proj_shift(n):
                ps = mm_block(xt, blkmap[n] * 128, 128)
                p = P[n]; sh = SH[n]; mu = pc(pcmap[n])
                S.op("act", lambda e: e.activation(p.t[:, 1:513], ps.t[:], AF.Copy), reads=[ps.b], writes=[p.b])
                S.op("dve", lambda e: e.tensor_tensor(tmp.t[:], p.t[:, 0:512], p.t[:, 1:513], ALU.subtract), reads=[p.b], writes=[tmp.b])
                S.op("dve", lambda e: e.scalar_tensor_tensor(sh.t[:], tmp.t[:], mu, p.t[:, 1:513], ALU.mult, ALU.add),
                     reads=[p.b, tmp.b, PC.b], writes=[sh.b])
                S.op("dve", lambda e: e.tensor_copy(p.t[:, 0:1], p.t[:, 512:513]), reads=[p.b], writes=[p.b])
            for n in Pn:
                proj_shift(n)
            S.op("act", lambda e: e.activation(TH.t[:], SH["l"].t[0:64, :], AF.Tanh), reads=[SH["l"].b], writes=[TH.b])
            S.op("act", lambda e: e.activation(SGL.t[:], SH["g"].t[:], AF.Sigmoid), reads=[SH["g"].b], writes=[SGL.b])

            def rwkv_block(b):
                cs_ = slice(b * 128, b * 128 + 128)
                shr, shk, shv = SH[f"r{b}"], SH[f"k{b}"], SH[f"v{b}"]

                def mm1(lhs, rhs, reads):
                    ps = nxt(S, PS)
                    S.op("pe", lambda e: e.matmul(ps.t[:], lhs, rhs, start=True, stop=True), reads=reads, writes=[ps.b])
                    return ps

                def act(dst, src, func, rd, **kw):
                    S.op("act", lambda e: e.activation(dst.t[:], src, func, **kw), reads=rd, writes=[dst.b])

                def tt(dst, a, b_, op):
                    S.op("dve", lambda e: e.tensor_tensor(dst.t[:], a.t[:], b_.t[:], op), reads=[a.b, b_.b], writes=[dst.b])

                def ts(dst, a, s1, s2, op0, op1=None, extra=()):
                    if op1 is None:
                        S.op("dve", lambda e: e.tensor_scalar(dst.t[:], a.t[:], s1, s2, op0), reads=[a.b, *extra], writes=[dst.b])
                    else:
                        S.op("dve", lambda e: e.tensor_scalar(dst.t[:], a.t[:], s1, s2, op0, op1), reads=[a.b, *extra], writes=[dst.b])
                ps = mm1(W2.t[0:64, cs_], TH.t[:], [W2.b, TH.b])
                act(R["lw"], ps.t[:], AF.Sigmoid, [ps.b, PC.b], bias=pc(8 + b))
                ts(R["lw"], R["lw"], NEG_E, None, ALU.mult)
                ps = mm1(W2.t[64:128, cs_], SH["l"].t[64:128, :], [W2.b, SH["l"].b])
                act(R["a"], ps.t[:], AF.Sigmoid, [ps.b, PC.b], bias=pc(10 + b))
                ps = mm1(G2.t[:, cs_], SGL.t[:], [G2.b, SGL.b])
                act(R["g"], ps.t[:], AF.Copy, [ps.b])
                ts(R["t1"], shk, pc(12 + b), None, ALU.mult, extra=[PC.b])
                tt(R["sq"], R["t1"], R["t1"], ALU.mult)
                ps = mm1(BD.t[:], R["sq"].t[:], [BD.b, R["sq"].b])
                act(R["nr"], ps.t[:], AF.Sqrt, [ps.b])
                ts(R["nr"], R["nr"], 1e-12, None, ALU.max)
                S.op("dve", lambda e: e.reciprocal(R["nr"].t[:], R["nr"].t[:]), reads=[R["nr"].b], writes=[R["nr"].b])
                tt(R["kk"], R["t1"], R["nr"], ALU.mult)
                ts(R["t2"], R["a"], -1.0, pc(14 + b), ALU.add, ALU.mult, extra=[PC.b])
                S.op("dve", lambda e: e.scalar_tensor_tensor(R["kmod"].t[:], R["t2"].t[:], 1.0, shk.t[:], ALU.add, ALU.mult),
                     reads=[R["t2"].b, shk.b], writes=[R["kmod"].b])
                S.op("dve", lambda e: e.tensor_tensor_scan(R["cs"].t[:], CM.t[:], R["lw"].t[:], 0.0, ALU.mult, ALU.add),
                     reads=[CM.b, R["lw"].b], writes=[R["cs"].b])
                act(R["en"], R["cs"].t[:], AF.Exp, [R["cs"].b], scale=-1.0)
                act(R["ep"], R["cs"].t[:], AF.Exp, [R["cs"].b])
                tt(R["d2"], R["cs"], R["lw"], ALU.subtract)
                act(R["epv"], R["d2"].t[:], AF.Exp, [R["d2"].b])
                tt(R["alpha"], R["kk"], R["epv"], ALU.mult)
                tt(R["t3"], R["kk"], R["a"], ALU.mult)
                tt(R["beta"], R["t3"], R["en"], ALU.mult)
                ts(R["nbeta"], R["beta"], -1.0, None, ALU.mult)
                tt(R["kappa"], R["kmod"], R["en"], ALU.mult)
                tt(R["rho"], shr, R["ep"], ALU.mult)
                S.op("dve", lambda e: e.tensor_copy(GC.t[:], R["ep"].t[:, 63::64]), reads=[R["ep"].b], writes=[GC.b])
                S.op("dve", lambda e: e.scalar_tensor_tensor(R["pr"].t[:], shr.t[:], pc(16 + b), R["kmod"].t[:], ALU.mult, ALU.mult),
                     reads=[shr.b, PC.b, R["kmod"].b], writes=[R["pr"].b])
                ps = mm1(BD.t[:], R["pr"].t[:], [BD.b, R["pr"].b])
                act(R["bo"], ps.t[:], AF.Copy, [ps.b])
                r0 = g * 256 + b * 128
                for on, tl in (("al", R["alpha"]), ("be", R["beta"]), ("ka", R["kappa"]), ("rh", R["rho"])):
                    st(S, tl, D[on][r0:r0 + 128, tsl], tl.t[:])
                st(S, GC, D["gc"][r0:r0 + 128, sc * 8:(sc + 1) * 8], GC.t[:])

                def tmaj(on, tl):
                    ps = nxt(S, PS)
                    for t4 in range(4):
                        S.op("pe", lambda e, t4=t4: e.matmul(ps.t[:, t4 * 128:(t4 + 1) * 128], tl.t[:, t4 * 128:(t4 + 1) * 128], IDN.t[:],
                                                             start=True, stop=True), reads=[tl.b, IDN.b], writes=[ps.b])
                    te = TME[cnt["tme"] % 2]; cnt["tme"] += 1
                    S.op("act", lambda e: e.activation(te.t[:], ps.t[:], AF.Copy), reads=[ps.b], writes=[te.b])
                    st(S, te, D[on][tsl, r0:r0 + 128].rearrange("(t4 p) c -> p t4 c", p=128), te.t[:].rearrange("p (t4 c) -> p t4 c", c=128))
                for on, tl in (("nbe_tm", R["nbeta"]), ("ka_tm", R["kappa"]), ("rv_tm", shv), ("rg_tm", R["g"]), ("bo_tm", R["bo"])):
                    tmaj(on, tl)
            for b in range(2):
                rwkv_block(b)
        for sc in range(NSC):
            superchunk(sc)

        def ownq(i):
            xt = load_x(i, xov[:, :, i * 512:(i + 1) * 512])
            for blk in range(2):
                def one(blk=blk):
                    ps = mm_block(xt, blk * 128, 128)
                    fo = FO[cnt["fo"] % 2]; cnt["fo"] += 1
                    S.op("act", lambda e: e.activation(fo.t[:], ps.t[:], AF.Copy, scale=0.125), reads=[ps.b], writes=[fo.b])
                    r0 = g * 256 + blk * 128
                    st(S, fo, D["fqo"][r0:r0 + 128, i * 512:(i + 1) * 512], fo.t[:])
                one()
        for i in range(NTOK // 512):
            ownq(i)
    for g in range(2):
        group(g)
    FCA = T(S, "FCA", [8, 512]); FCB = T(S, "FCB", [8, 512])

    def blend(i):
        S.dma("sp", FCA.t[:], D["fc"][:, (2 * i) * 512:(2 * i + 1) * 512], FCA.sem, reads=[bfc], writes=[FCA.b])
        S.dma("sp", FCB.t[:], D["fc"][:, (2 * i + 1) * 512:(2 * i + 2) * 512], FCB.sem, reads=[bfc], writes=[FCB.b])
        S.op("dve", lambda e: e.tensor_scalar(FCA.t[:], FCA.t[:], MS.t[0:8, 0:1], None, ALU.mult), reads=[FCA.b, MS.b], writes=[FCA.b])
        S.op("dve", lambda e: e.scalar_tensor_tensor(FCA.t[:], FCB.t[:], MS.t[0:8, 1:2], FCA.t[:], ALU.mult, ALU.add),
             reads=[FCA.b, FCB.b, MS.b], writes=[FCA.b])
        split3(FCA, 8, D["co3"][:, :, i * 512:(i + 1) * 512])
    for i in range(NTOK // 512):
        blend(i)


def phase2(S, PS, D):
    SCP = PS[0:2]; ACC = PS[2:4]
    QA = T(S, "QA", [70, NTOK], BF16); KA = T(S, "KA", [70, SEQ], BF16); VO = T(S, "VO", [128, 64, 128], BF16)
    NEG = T(S, "NEG", [128, 8, 512])
    PT = [T(S, f"PT{i}", [128, 512], BF16) for i in range(3)]
    TMP = [T(S, f"TMPm{i}", [128, 512]) for i in range(2)]
    RD = T(S, "RD", [128, 512]); Y = T(S, "Y", [64, 512], BF16)
    ld(S, NEG, NEG.t[:], D["neg"].rearrange("j p q -> p j q"))
    S.op("dve", lambda e: e.memset(QA.t[64:70, :], 1.0), writes=[QA.b])
    S.op("dve", lambda e: e.memset(KA.t[64:70, :], 1.0), writes=[KA.b])
    S.op("dve", lambda e: e.memset(VO.t[:, :, 64:128], 1.0), writes=[VO.b])
    cnt = {"pi": 0, "pt": 0, "tm": 0}

    def head(hh):
        r = slice(hh * 64, (hh + 1) * 64)
        for i in range(2):
            sl = slice(i * 2048, (i + 1) * 2048)
            ld(S, QA, QA.t[0:64, sl], D["fqo"][r, sl], nowaw=(i > 0))
        ld(S, QA, QA.t[64:67, :], D["co3"][hh], nowaw=True)
        for i in range(4):
            sl = slice(i * 2048, (i + 1) * 2048)
            ld(S, KA, KA.t[0:64, sl], D["fk"][r, sl], nowaw=(i > 0))
        ld(S, KA, KA.t[67:70, :], D["fnc3"][hh], nowaw=True)
        ld(S, VO, VO.t[:, :, 0:64], D["fvtm"][:, r].rearrange("(kb p) d -> p kb d", p=128), nowaw=True)

        def qchunk(i):
            acc = ACC[i % 2]
            nkb = 8 * i + 8

            def scores(kb):
                j = kb - 8 * i
                ps = SCP[cnt["pi"] % 2]; cnt["pi"] += 1
                pt = PT[cnt["pt"] % 3]; cnt["pt"] += 1
                S.op("pe", lambda e: e.matmul(ps.t[:], KA.t[:, kb * 128:(kb + 1) * 128], QA.t[:, i * 512:(i + 1) * 512], start=True, stop=True),
                     reads=[KA.b, QA.b], writes=[ps.b])
                if j >= 0:
                    tm = TMP[cnt["tm"] % 2]; cnt["tm"] += 1
                    S.op("dve", lambda e: e.tensor_tensor(tm.t[:], ps.t[:], NEG.t[:, j, :], ALU.add), reads=[ps.b, NEG.b], writes=[tm.b])
                    S.op("act", lambda e: e.activation(pt.t[:], tm.t[:], AF.Exp), reads=[tm.b], writes=[pt.b])
                else:
                    S.op("act", lambda e: e.activation(pt.t[:], ps.t[:], AF.Exp), reads=[ps.b], writes=[pt.b])
                return kb, pt

            def pv(kb, pt):
                S.op("pe", lambda e: e.matmul(acc.t[:], VO.t[:, kb, :], pt.t[:], start=(kb == 0), stop=(kb == nkb - 1)),
                     reads=[VO.b, pt.b], writes=[acc.b])
            prev = None
            for kb in range(nkb):
                cur = scores(kb)
                if prev is not None:
                    pv(*prev)
                prev = cur
                yield
            pv(*prev)
            S.op("dve", lambda e: e.reciprocal(RD.t[64:128, :], acc.t[64:128, :]), reads=[acc.b], writes=[RD.b])
            S.op("dve", lambda e: e.tensor_tensor(Y.t[:], acc.t[0:64, :], RD.t[64:128, :], ALU.mult), reads=[acc.b, RD.b], writes=[Y.b])
            st(S, Y, D["yfo"][r, i * 512:(i + 1) * 512], Y.t[:])
        for i in range(NTOK // 512):
            yield from qchunk(i)
    for hh in range(8):
        yield from head(hh)


GN_EPS = 64e-5


def phase3(S, PS, D):
    MK = [T(S, f"MK{i}", [128, 512]) for i in range(5)]
    for i in range(5):
        ld(S, MK[i], MK[i].t[0:64, :], D["mk"][i])
        ld(S, MK[i], MK[i].t[64:128, :], D["mk"][i], nowaw=True)
    MSL, MSU, MIU, NMIU, ID8 = MK
    FM = [[T(S, f"FM{p}_{i}", [128, 512]) for i in range(4)] for p in range(2)]
    TM = [[T(S, f"TM{p}_{i}", [128, 8, 64]) for i in range(5)] for p in range(2)]
    GC = T(S, "GC3", [128, 128]); LG = T(S, "LG", [128, 64]); LB = T(S, "LB", [128, 64])
    mk3 = lambda n: T(S, n, [128, 8, 64])
    A = mk3("A"); AT = mk3("AT"); Wt = [mk3("W0"), mk3("W1")]; Pt = [mk3("P0"), mk3("P1")]; PTt = [mk3("PT0"), mk3("PT1")]
    AakT = mk3("AakT"); nArbT = mk3("nArbT"); ArkT = mk3("ArkT")
    ST = [T(S, "ST0", [128, 64]), T(S, "ST1", [128, 64])]
    STg = T(S, "STg", [128, 64]); RHS = T(S, "RHS", [128, 64]); US = T(S, "US", [128, 64])
    YO = [mk3("YO0"), mk3("YO1")]; YT = [T(S, "YT0", [128, 512], BF16), T(S, "YT1", [128, 512], BF16)]
    stat = T(S, "stat3", [128, 6]); mv = T(S, "mv3", [128, 2]); rstd = T(S, "rstd3", [128, 1]); yn = T(S, "yn", [128, 64]); bt = T(S, "bt", [128, 64])
    cnt = {"st": 0, "cast": 0}
    fmn = ("al", "be", "ka", "rh"); tmn = ("nbe_tm", "ka_tm", "rv_tm", "rg_tm", "bo_tm")
    HS = (slice(0, 64), slice(64, 128))
    CF = [T(S, f"CF{i}", [128, 4096]) for i in range(2)]; CB = [T(S, f"CB{i}", [128, 4096], BF16) for i in range(2)]
    NCH = 16384 * 1024 // 128 // 4096

    def cast_step():
        k = cnt["cast"]
        if k >= 2 * NCH:
            return
        cnt["cast"] += 1
        src = D["u" if k < NCH else "v"].rearrange("(p r) d -> p (r d)", p=128)
        dst = D["uvb"].rearrange("(p r) (two d) -> p r two d", p=128, two=2)
        c = k % NCH
        cf = CF[k % 2]; cb = CB[k % 2]
        ld(S, cf, cf.t[:], src[:, c * 4096:(c + 1) * 4096], q="pool")
        S.op("pool", lambda e: e.tensor_copy(cb.t[:], cf.t[:]), reads=[cf.b], writes=[cb.b])
        st(S, cb, dst[:, c * 4:(c + 1) * 4, 0 if k < NCH else 1, :], cb.t[:].rearrange("p (r d) -> p r d", d=1024), q="pool")

    def mm2(ps, col, lhs, rhs, reads, start=True, stop=True):
        for hs in HS:
            S.op("pe", lambda e, hs=hs: e.matmul(ps.t[hs, col * 64:(col + 1) * 64], lhs(hs), rhs(hs), start=start, stop=stop),
                 reads=reads, writes=[ps.b])

    def batch_mm(lhs_of, rhs_of, reads):
        ps = nxt(S, PS)
        for j in range(8):
            mm2(ps, j, (lambda hs, j=j: lhs_of(j, hs)), (lambda hs, j=j: rhs_of(j, hs)), reads)
        return ps

    def pair(hp):
        hh = 2 * hp
        r = slice(hh * 64, (hh + 2) * 64)
        ld(S, GC, GC.t[:], D["gc"][r, :])
        ld(S, LG, LG.t[:], D["lg3"][hh:hh + 2].rearrange("h p c -> (h p) c"))
        ld(S, LB, LB.t[:], D["lb3"][hh:hh + 2].rearrange("h p c -> (h p) c"))
        s0 = ST[cnt["st"] % 2]
        S.op("dve", lambda e: e.memset(s0.t[:], 0.0), writes=[s0.b])

        def superchunk(sc):
            fm = FM[sc % 2]; tm = TM[sc % 2]
            tsl = slice(sc * 512, (sc + 1) * 512)
            for i in range(4):
                ld(S, fm[i], fm[i].t[:], D[fmn[i]][r, tsl])
            for i in range(5):
                for k_, hs in enumerate(HS):
                    rr = slice((hh + k_) * 64, (hh + k_ + 1) * 64)
                    ld(S, tm[i], tm[i].t[hs, :, :], D[tmn[i]][tsl, rr].rearrange("(c t) k -> t c k", t=64), nowaw=(k_ > 0))
            alT, beT, kaT, rhT = fm
            nbe, ka, vm, gt, bo = tm
            fsl = lambda t_, j, hs: t_.t[hs, j * 64:(j + 1) * 64]
            f3 = lambda t_, j, hs: t_.t[hs, j, :]
            flat = lambda t_: t_.t[:].rearrange("p a b -> p (a b)")

            def evac_mask(dst, ps, mk):
                S.op("dve", lambda e: e.tensor_tensor(flat(dst), ps.t[:], mk.t[:], ALU.mult), reads=[ps.b, mk.b], writes=[dst.b])

            def evac_copy(dst, ps):
                S.op("act", lambda e: e.activation(flat(dst), ps.t[:], AF.Copy), reads=[ps.b], writes=[dst.b])

            def evac_add(dst, ps, src):
                S.op("dve", lambda e: e.tensor_tensor(flat(dst), ps.t[:], flat(src), ALU.add), reads=[ps.b, src.b], writes=[dst.b])

            def bmm3(l, r_):
                return batch_mm(lambda j, hs: f3(l, j, hs), lambda j, hs: f3(r_, j, hs), [l.b, r_.b])

            def bmmf(l, r_):
                return batch_mm(lambda j, hs: fsl(l, j, hs), lambda j, hs: fsl(r_, j, hs), [l.b, r_.b])

            evac_mask(A, bmmf(alT, beT), MSL)
            evac_mask(AT, bmmf(beT, alT), MSU)
            W0 = Wt[0]
            S.op("dve", lambda e: e.tensor_tensor(flat(W0), ID8.t[:], flat(AT), ALU.subtract), reads=[ID8.b, AT.b], writes=[W0.b])
            evac_copy(Pt[0], bmm3(AT, A))
            evac_copy(PTt[0], bmm3(A, AT))
            for i in range(5):
                Wc, Pc, PTc = Wt[i % 2], Pt[i % 2], PTt[i % 2]
                Wn, Pn_, PTn = Wt[(i + 1) % 2], Pt[(i + 1) % 2], PTt[(i + 1) % 2]
                evac_add(Wn, bmm3(Pc, Wc), Wc)
                if i < 4:
                    evac_copy(Pn_, bmm3(PTc, Pc))
                    evac_copy(PTn, bmm3(Pc, PTc))
            Wf = Wt[5 % 2]
            evac_mask(AakT, bmmf(kaT, alT), MSU)
            evac_mask(nArbT, bmmf(beT, rhT), NMIU)
            evac_mask(ArkT, bmmf(kaT, rhT), MIU)
            yo = YO[sc % 2]; yt = YT[sc % 2]
            yield

            def chunk(j):
                c = sc * 8 + j
                st0 = ST[cnt["st"] % 2]; st1 = ST[(cnt["st"] + 1) % 2]; cnt["st"] += 1
                gcol = GC.t[:, c:c + 1]
                sT = lambda t_: (lambda hs: t_.t[hs, :])
                psr = nxt(S, PS)
                mm2(psr, 0, lambda hs: fsl(alT, j, hs), sT(st0), [alT.b, st0.b], True, False)
                mm2(psr, 0, lambda hs: f3(AakT, j, hs), lambda hs: f3(vm, j, hs), [AakT.b, vm.b], False, True)
                S.op("act", lambda e: e.activation(RHS.t[:], psr.t[:, 0:64], AF.Copy), reads=[psr.b], writes=[RHS.b])
                S.op("dve", lambda e: e.tensor_scalar(STg.t[:], st0.t[:], gcol, None, ALU.mult), reads=[st0.b, GC.b], writes=[STg.b])
                psu = nxt(S, PS)
                mm2(psu, 0, lambda hs: f3(Wf, j, hs), sT(RHS), [Wf.b, RHS.b])
                S.op("act", lambda e: e.activation(US.t[:], psu.t[:, 0:64], AF.Copy), reads=[psu.b], writes=[US.b])
                psy = nxt(S, PS)
                mm2(psy, 0, lambda hs: fsl(rhT, j, hs), sT(st0), [rhT.b, st0.b], True, False)
                mm2(psy, 0, lambda hs: f3(nArbT, j, hs), sT(US), [nArbT.b, US.b], False, False)
                mm2(psy, 0, lambda hs: f3(ArkT, j, hs), lambda hs: f3(vm, j, hs), [ArkT.b, vm.b], False, True)
                psd = nxt(S, PS)
                mm2(psd, 0, lambda hs: f3(nbe, j, hs), sT(US), [nbe.b, US.b], True, False)
                mm2(psd, 0, lambda hs: f3(ka, j, hs), lambda hs: f3(vm, j, hs), [ka.b, vm.b], False, True)
                S.op("dve", lambda e: e.scalar_tensor_tensor(st1.t[:], psd.t[:, 0:64], gcol, STg.t[:], ALU.mult, ALU.add),
                     reads=[psd.b, GC.b, STg.b], writes=[st1.b])
                py = psy.t[:, 0:64]
                S.op("dve", lambda e: e.bn_stats(stat.t[:], py), reads=[psy.b], writes=[stat.b])
                S.op("dve", lambda e: e.bn_aggr(mv.t[:], stat.t[:]), reads=[stat.b], writes=[mv.b])
                S.op("dve", lambda e: e.tensor_scalar(rstd.t[:], mv.t[:, 1:2], GN_EPS, None, ALU.add), reads=[mv.b], writes=[rstd.b])
                S.op("act", lambda e: e.activation(rstd.t[:], rstd.t[:], AF.Sqrt), reads=[rstd.b], writes=[rstd.b])
                S.op("dve", lambda e: e.reciprocal(rstd.t[:], rstd.t[:]), reads=[rstd.b], writes=[rstd.b])
                S.op("dve", lambda e: e.tensor_scalar(yn.t[:], py, mv.t[:, 0:1], rstd.t[:, 0:1], ALU.subtract, ALU.mult),
                     reads=[psy.b, mv.b, rstd.b], writes=[yn.b])
                S.op("dve", lambda e: e.tensor_tensor(yn.t[:], yn.t[:], LG.t[:], ALU.mult), reads=[yn.b, LG.b], writes=[yn.b])
                S.op("dve", lambda e: e.tensor_tensor(yn.t[:], yn.t[:], LB.t[:], ALU.add), reads=[yn.b, LB.b], writes=[yn.b])
                S.op("dve", lambda e: e.tensor_tensor(bt.t[:], bo.t[:, j, :], vm.t[:, j, :], ALU.mult), reads=[bo.b, vm.b], writes=[bt.b])
                S.op("dve", lambda e: e.tensor_tensor(yn.t[:], yn.t[:], bt.t[:], ALU.add), reads=[yn.b, bt.b], writes=[yn.b])
                S.op("dve", lambda e: e.tensor_tensor(yo.t[:, j, :], yn.t[:], gt.t[:, j, :], ALU.mult), reads=[yn.b, gt.b], writes=[yo.b])
            for j in range(8):
                chunk(j)
                yield
            ps = batch_mm(lambda j, hs: f3(yo, j, hs), lambda j, hs: ID8.t[hs, 0:64], [yo.b, ID8.b])
            S.op("act", lambda e: e.activation(yt.t[:], ps.t[:], AF.Copy), reads=[ps.b], writes=[yt.b])
            st(S, yt, D["yr"][r, tsl], yt.t[:])
            cast_step(); cast_step()
        for sc in range(NSC):
            yield from superchunk(sc)
    for hp in range(4):
        yield from pair(hp)
    while cnt["cast"] < 2 * NCH:
        cast_step()


def phase4(S, PS, D):
    WG = T(S, "WG", [128, 8, 1024], BF16); PA = T(S, "PA", [128, 4, 1024], BF16); PB = T(S, "PB", [128, 4, 1024], BF16)
    WO = T(S, "WO", [128, 8, 1024], BF16); WST = [T(S, f"WST{i}", [128, 1024]) for i in range(2)]
    LNG = T(S, "LNG", [128, 1024]); LNB = T(S, "LNB", [128, 1024]); MS = T(S, "MS4", [128, 2]); IDN = T(S, "IDN4", [128, 128])
    wgv = D["wg"].rearrange("(k p) c -> p k c", p=128)
    cw = {"n": 0}

    def ldw(dst, dslice, src):
        ws = WST[cw["n"] % 2]; cw["n"] += 1
        ld(S, ws, ws.t[:], src, q="pool")
        S.op("pool", lambda e: e.tensor_copy(dslice, ws.t[:]), reads=[ws.b], writes=[dst.b])
    for kc in range(8):
        ldw(WO, WO.t[:, kc, :], D["wo"].rearrange("(k p) c -> p k c", p=128)[:, kc, :])
    for kc in range(4):
        ldw(PA, PA.t[:, kc, :], D["pa"].rearrange("(k p) c -> p k c", p=128)[:, kc, :])
        ldw(PB, PB.t[:, kc, :], D["pb"].rearrange("(k p) c -> p k c", p=128)[:, kc, :])
    ld(S, LNG, LNG.t[:], D["lg1"]); ld(S, LNB, LNB.t[:], D["lb1"]); ld(S, MS, MS.t[:], D["msel"]); ld(S, IDN, IDN.t[:], D["idn"])
    XTF = T(S, "XTF", [128, 8, 512]); XT = T(S, "XT", [128, 8, 512], BF16)
    YF = T(S, "YF", [128, 4, 512], BF16); YR = T(S, "YR", [128, 4, 512], BF16); YRb = T(S, "YRb", [128, 4, 512], BF16)
    MT = T(S, "MT", [128, 8, 512], BF16); SG = T(S, "SG", [128, 512])
    XR = T(S, "XR", [128, 1024]); Z = T(S, "Z", [128, 1024]); X1 = T(S, "X14", [128, 1024]); TP = T(S, "TP", [128, 512])
    stat = T(S, "stat4", [128, 12]); mv = T(S, "mv4", [128, 2]); rstd = T(S, "rstd4", [128, 1])
    yrv = D["yr"].rearrange("(k p) t -> p k t", p=128)

    def superchunk(i):
        tsl = slice(i * 512, (i + 1) * 512)
        ld(S, XTF, XTF.t[:], D["xTo"].rearrange("(k p) t -> p k t", p=128)[:, :, tsl], q="pool")
        S.op("pool", lambda e: e.tensor_copy(XT.t[:], XTF.t[:]), reads=[XTF.b], writes=[XT.b])
        ld(S, YF, YF.t[:], D["yfo"].rearrange("(k p) t -> p k t", p=128)[:, :, tsl])
        ld(S, YR, YR.t[:], yrv[:, :, (2 * i) * 512:(2 * i + 1) * 512])
        ld(S, YRb, YRb.t[:], yrv[:, :, (2 * i + 1) * 512:(2 * i + 2) * 512])
        fl = lambda t_: t_.t[:].rearrange("p a b -> p (a b)")
        S.op("dve", lambda e: e.tensor_scalar(fl(YR), fl(YR), MS.t[:, 0:1], None, ALU.mult), reads=[YR.b, MS.b], writes=[YR.b])
        S.op("dve", lambda e: e.scalar_tensor_tensor(fl(YR), fl(YRb), MS.t[:, 1:2], fl(YR), ALU.mult, ALU.add),
             reads=[YR.b, YRb.b, MS.b], writes=[YR.b])

        def branch(goff, PW, Y, first):
            for kc in range(8):
                ldw(WG, WG.t[:, kc, :], wgv[:, kc, goff:goff + 1024])

            def nblock(nb):
                psg = nxt(S, PS)
                for kc in range(8):
                    S.op("pe", lambda e, kc=kc: e.matmul(psg.t[:], WG.t[:, kc, nb * 128:nb * 128 + 128], XT.t[:, kc, :],
                                                         start=(kc == 0), stop=(kc == 7)), reads=[WG.b, XT.b], writes=[psg.b])
                S.op("act", lambda e: e.activation(SG.t[:], psg.t[:], AF.Sigmoid), reads=[psg.b], writes=[SG.b])
                psz = nxt(S, PS)
                for kc in range(4):
                    S.op("pe", lambda e, kc=kc: e.matmul(psz.t[:], PW.t[:, kc, nb * 128:nb * 128 + 128], Y.t[:, kc, :],
                                                         start=(kc == 0), stop=(kc == 3)), reads=[PW.b, Y.b], writes=[psz.b])
                if first:
                    S.op("dve", lambda e: e.tensor_tensor(MT.t[:, nb, :], psz.t[:], SG.t[:], ALU.mult), reads=[psz.b, SG.b], writes=[MT.b])
                else:
                    S.op("dve", lambda e: e.tensor_tensor(SG.t[:], psz.t[:], SG.t[:], ALU.mult), reads=[psz.b, SG.b], writes=[SG.b])
                    S.op("dve", lambda e: e.tensor_tensor(MT.t[:, nb, :], MT.t[:, nb, :], SG.t[:], ALU.add), reads=[MT.b, SG.b], writes=[MT.b])
            for nb in range(8):
                nblock(nb)
        branch(0, PA, YF, True)
        branch(1024, PB, YR, False)

        def ttile(tt):
            r0 = i * 512 + tt * 128
            ld(S, XR, XR.t[:], D["xo"][r0:r0 + 128, :])

            def half(hf):
                ps = nxt(S, PS)
                for nb in range(8):
                    S.op("pe", lambda e, nb=nb: e.matmul(ps.t[:], MT.t[:, nb, tt * 128:(tt + 1) * 128], WO.t[:, nb, hf * 512:(hf + 1) * 512],
                                                         start=(nb == 0), stop=(nb == 7)), reads=[MT.b, WO.b], writes=[ps.b])
                S.op("dve", lambda e: e.scalar_tensor_tensor(Z.t[:, hf * 512:(hf + 1) * 512], XR.t[:, hf * 512:(hf + 1) * 512], DN_ALPHA,
                                                             ps.t[:], ALU.mult, ALU.add), reads=[XR.b, ps.b], writes=[Z.b])
            half(0); half(1)
            layer_norm_tile(S, Z, X1, LNG, LNB, stat, mv, rstd)
            st(S, X1, D["x1"][r0:r0 + 128, :], X1.t[:])

            def tgroup(gq):
                ps = nxt(S, PS)
                for bi in range(4):
                    kc = gq * 4 + bi
                    S.op("pe", lambda e, bi=bi, kc=kc: e.matmul(ps.t[:, bi * 128:(bi + 1) * 128], X1.t[:, kc * 128:(kc + 1) * 128], IDN.t[:],
                                                                start=True, stop=True), reads=[X1.b, IDN.b], writes=[ps.b])
                S.op("act", lambda e: e.activation(TP.t[:], ps.t[:], AF.Copy), reads=[ps.b], writes=[TP.b])
                st(S, TP, D["x1T"][gq * 512:(gq + 1) * 512, r0:r0 + 128].rearrange("(bi p) t -> p bi t", p=128),
                   TP.t[:].rearrange("p (bi t) -> p bi t", t=128))
            tgroup(0); tgroup(1)
        for tt in range(4):
            ttile(tt)
    for i in range(NTOK // 512):
        superchunk(i)


def phase5(S, PS, D, ntile=NTOK // 128):
    x1d = D["x1"]; x1Td = D["x1T"]; wqd = D["wq"]; skd = D["sk"]
    lgd = D["lg2"]; lbd = D["lb2"]; iod = D["iota"]; od = D["out"]
    WQ = T(S, "WQ", [128, 8, 2048]); SK = T(S, "SK", [128, 16, 128])
    LNG = T(S, "LNG", [128, 1024]); LNB = T(S, "LNB", [128, 1024])
    for kc in range(8):
        ld(S, WQ, WQ.t[:, kc, :], wqd.rearrange("(k p) c -> p k c", p=128)[:, kc, :], nowaw=True)
    ld(S, SK, SK.t[:], skd.rearrange("p (b k) -> p b k", k=128))
    ld(S, LNG, LNG.t[:], lgd); ld(S, LNB, LNB.t[:], lbd)
    IOT = T(S, "IOT", [128, 256]); ld(S, IOT, IOT.t[:], iod)
    BPU = T(S, "BPU", [128, 16], U32); BPF = T(S, "BPF", [128, 16])
    X1 = [T(S, f"X1_{i}", [128, 1024]) for i in range(2)]; X1T = T(S, "X1T", [128, 8, 128])
    QT = T(S, "QT", [128, 16, 128]); SC = T(S, "SC", [128, 16, 128]); SC2 = T(S, "SC2", [128, 128])
    TS = T(S, "TS", [128, 16, 16]); TI = T(S, "TI", [128, 16, 16], U32); TIF = T(S, "TIF", [128, 16, 16]); TI128 = T(S, "TI128", [128, 16, 16])
    CS = T(S, "CS", [128, 256]); CI = T(S, "CI", [128, 256]); CS2 = T(S, "CS2", [128, 256]); JKs = [T(S, f"JK{i}", [128, 256]) for i in range(2)]
    BS = T(S, "BS", [128, 8, 16]); BP = T(S, "BP", [128, 8], U32); IDF = T(S, "IDF", [128, 128]); IDX = [T(S, f"IDX{i}", [128, 128], U32) for i in range(2)]
    NM = T(S, "NM", [128, 8]); EX = T(S, "EX", [128, 8, 16]); SM = T(S, "SM", [128, 8]); GW = [T(S, f"GW{i}", [128, 128]) for i in range(2)]
    UV = [[T(S, f"UV{p}_{i}", [128, 2048], BF16) for i in range(8)] for p in range(2)]
    DG = [T(S, f"DG{i}", [128, 128], BF16) for i in range(4)]; IDB = T(S, "IDB", [128, 128], BF16); IDF32 = T(S, "IDF32", [128, 128])
    ld(S, IDF32, IDF32.t[:], D["idn"])
    S.op("dve", lambda e: e.tensor_copy(IDB.t[:], IDF32.t[:]), reads=[IDF32.b], writes=[IDB.b])
    X1B = [T(S, f"X1B{i}", [128, 1024], BF16) for i in range(2)]
    JK2s = [T(S, f"JK2_{i}", [128, 1024], BF16) for i in range(2)]; H = T(S, "H", [128, 128]); HG = T(S, "HG", [128, 128])
    Z = T(S, "Z", [128, 1024]); OUT = T(S, "OUT", [128, 1024])
    ACCP = PS[6:8]; PS = PS[0:6]; uvd = D["uvb"]
    stat = T(S, "stat5", [128, 12]); mv = T(S, "mv5", [128, 2]); rstd = T(S, "rstd5", [128, 1])
    cnt = {"ub": 0, "dg": 0, "jk": 0, "jk2": 0}

    def front(ti):
        r0 = ti * 128
        X1c, X1Bc, IDXc, GWc = X1[ti % 2], X1B[ti % 2], IDX[ti % 2], GW[ti % 2]
        ld(S, X1c, X1c.t[:], x1d[r0:r0 + 128, :])
        S.op("dve", lambda e: e.tensor_copy(X1Bc.t[:], X1c.t[:]), reads=[X1c.b], writes=[X1Bc.b])
        ld(S, X1T, X1T.t[:], x1Td.rearrange("(k p) t -> p k t", p=128)[:, :, r0:r0 + 128])

        def qgroup(gq):
            ps = nxt(S, PS)
            for bi in range(4):
                blk = gq * 4 + bi
                for kc in range(8):
                    S.op("pe", lambda e, bi=bi, blk=blk, kc=kc: e.matmul(ps.t[:, bi * 128:(bi + 1) * 128], WQ.t[:, kc, blk * 128:(blk + 1) * 128],
                                                                         X1T.t[:, kc, :], start=(kc == 0), stop=(kc == 7)),
                         reads=[WQ.b, X1T.b], writes=[ps.b])
            S.op("act", lambda e: e.activation(QT.t[:, gq * 4:(gq + 1) * 4, :].rearrange("p a b -> p (a b)"), ps.t[:], AF.Copy),
                 reads=[ps.b], writes=[QT.b])
        for gq in range(4):
            qgroup(gq)
            yield

        def sgroup(gq):
            ps = nxt(S, PS)
            for bi in range(4):
                blk = gq * 4 + bi
                S.op("pe", lambda e, bi=bi, blk=blk: e.matmul(ps.t[:, bi * 128:(bi + 1) * 128], QT.t[:, blk, :], SK.t[:, blk, :],
                                                              start=True, stop=True), reads=[QT.b, SK.b], writes=[ps.b])
            S.op("act", lambda e: e.activation(SC.t[:, gq * 4:(gq + 1) * 4, :].rearrange("p a b -> p (a b)"), ps.t[:], AF.Copy),
                 reads=[ps.b], writes=[SC.b])
        for gq in range(4):
            sgroup(gq)
            yield

        def top16(blk):
            S.op("dve", lambda e: e.max(TS.t[:, blk, 0:8], SC.t[:, blk, :]), reads=[SC.b], writes=[TS.b])
            S.op("dve", lambda e: e.max_index(TI.t[:, blk, 0:8], TS.t[:, blk, 0:8], SC.t[:, blk, :]), reads=[SC.b, TS.b], writes=[TI.b])
            S.op("dve", lambda e: e.match_replace(SC2.t[:], TS.t[:, blk, 0:8], SC.t[:, blk, :], -1e30), reads=[SC.b, TS.b], writes=[SC2.b])
            S.op("dve", lambda e: e.max(TS.t[:, blk, 8:16], SC2.t[:]), reads=[SC2.b], writes=[TS.b])
            S.op("dve", lambda e: e.max_index(TI.t[:, blk, 8:16], TS.t[:, blk, 8:16], SC2.t[:]), reads=[SC2.b, TS.b], writes=[TI.b])
        for blk in range(16):
            top16(blk)
            yield
        S.op("dve", lambda e: e.tensor_copy(TIF.t[:], TI.t[:]), reads=[TI.b], writes=[TIF.b])
        S.op("dve", lambda e: e.tensor_scalar(TI128.t[:], TIF.t[:], 128.0, None, ALU.mult), reads=[TIF.b], writes=[TI128.b])

        def head(h):
            for a in range(16):
                S.op("dve", lambda e, a=a: e.tensor_scalar(CS.t[:, a * 16:(a + 1) * 16], TS.t[:, 2 * h + 1, :], TS.t[:, 2 * h, a:a + 1], None, ALU.add),
                     reads=[TS.b], writes=[CS.b], soft=((CS.b,) if a > 0 else ()))
                S.op("dve", lambda e, a=a: e.tensor_scalar(CI.t[:, a * 16:(a + 1) * 16], TIF.t[:, 2 * h + 1, :], TI128.t[:, 2 * h, a:a + 1], None, ALU.add),
                     reads=[TIF.b, TI128.b], writes=[CI.b], soft=((CI.b,) if a > 0 else ()))
            S.op("dve", lambda e: e.max(BS.t[:, h, 0:8], CS.t[:]), reads=[CS.b], writes=[BS.b])
            S.op("dve", lambda e: e.max_index(BPU.t[:, 0:8], BS.t[:, h, 0:8], CS.t[:]), reads=[CS.b, BS.b], writes=[BPU.b])
            S.op("dve", lambda e: e.match_replace(CS2.t[:], BS.t[:, h, 0:8], CS.t[:], -1e30), reads=[CS.b, BS.b], writes=[CS2.b])
            S.op("dve", lambda e: e.max(BS.t[:, h, 8:16], CS2.t[:]), reads=[CS2.b], writes=[BS.b])
            S.op("dve", lambda e: e.max_index(BPU.t[:, 8:16], BS.t[:, h, 8:16], CS2.t[:]), reads=[CS2.b, BS.b], writes=[BPU.b])
            S.op("dve", lambda e: e.tensor_copy(BPF.t[:], BPU.t[:]), reads=[BPU.b], writes=[BPF.b])
            for k in range(16):
                def pick(k=k):
                    jk = JKs[cnt["jk"] % 2]; cnt["jk"] += 1
                    S.op("dve", lambda e: e.scalar_tensor_tensor(jk.t[:], IOT.t[:], BPF.t[:, k:k + 1], CI.t[:], ALU.is_equal, ALU.mult,
                                                                 accum_out=IDF.t[:, h * 16 + k:h * 16 + k + 1]),
                         reads=[IOT.b, BPF.b, CI.b], writes=[jk.b, IDF.b], soft=((IDF.b,) if (h > 0 or k > 0) else ()))
                pick()
            S.op("dve", lambda e: e.tensor_scalar(NM.t[:, h:h + 1], BS.t[:, h, 0:1], -1.0, None, ALU.mult), reads=[BS.b], writes=[NM.b])
            S.op("act", lambda e: e.activation(EX.t[:, h, :], BS.t[:, h, :], AF.Exp, bias=NM.t[:, h:h + 1], accum_out=SM.t[:, h:h + 1]),
                 reads=[BS.b, NM.b], writes=[EX.b, SM.b])
        for h in range(8):
            head(h)
            yield
        S.op("dve", lambda e: e.tensor_copy(IDXc.t[:], IDF.t[:]), reads=[IDF.b], writes=[IDXc.b])
        S.op("dve", lambda e: e.reciprocal(SM.t[:], SM.t[:]), reads=[SM.b], writes=[SM.b])
        for h in range(8):
            S.op("dve", lambda e, h=h: e.tensor_scalar(GWc.t[:, h * 16:(h + 1) * 16], EX.t[:, h, :], SM.t[:, h:h + 1], None, ALU.mult),
                 reads=[EX.b, SM.b], writes=[GWc.b])

    def back(ti, fg):
        r0 = ti * 128
        X1c, X1Bc, IDXc, GWc = X1[ti % 2], X1B[ti % 2], IDX[ti % 2], GW[ti % 2]

        def group(gi):
            bufs = UV[gi % 2]
            for k in range(8):
                def g1(k=k):
                    sl_ = gi * 8 + k
                    ub = bufs[k]
                    S.dma("pool", None, None, ub.semq("pool"), reads=[IDXc.b], writes=[ub.b],
                          fn=lambda e: e.indirect_dma_start(out=ub.t[:], out_offset=None, in_=uvd,
                                                            in_offset=bass.IndirectOffsetOnAxis(ap=IDXc.t[:, sl_:sl_ + 1], axis=0)))
                g1()
            for k in range(8):
                def d1(k=k):
                    sl_ = gi * 8 + k
                    ub = bufs[k]
                    jk2 = JK2s[cnt["jk2"] % 2]; cnt["jk2"] += 1
                    S.op("dve", lambda e: e.scalar_tensor_tensor(jk2.t[:], ub.t[:, 0:1024], 1.0, X1Bc.t[:], ALU.mult, ALU.mult,
                                                                 accum_out=H.t[:, sl_:sl_ + 1]), reads=[ub.b, X1Bc.b], writes=[jk2.b, H.b],
                         soft=((H.b,) if k > 0 else ()))
                d1()
            gs = slice(gi * 8, gi * 8 + 8)
            S.op("act", lambda e: e.activation(HG.t[:, gs], H.t[:, gs], AF.Gelu), reads=[H.b], writes=[HG.b])
            S.op("dve", lambda e: e.tensor_tensor(HG.t[:, gs], HG.t[:, gs], GWc.t[:, gs], ALU.mult), reads=[HG.b, GWc.b], writes=[HG.b])
            for k in range(8):
                def v1(k=k):
                    sl_ = gi * 8 + k
                    ub = bufs[k]
                    dg = DG[cnt["dg"] % 4]; cnt["dg"] += 1
                    S.op("act", lambda e: e.activation(dg.t[:], IDB.t[:], AF.Copy, scale=HG.t[:, sl_:sl_ + 1]), reads=[IDB.b, HG.b], writes=[dg.b])
                    for hf in range(2):
                        S.op("pe", lambda e, hf=hf: e.matmul(ACCP[hf].t[:], dg.t[:], ub.t[:, 1024 + hf * 512:1024 + (hf + 1) * 512],
                                                             start=(sl_ == 0), stop=(sl_ == 127)), reads=[dg.b, ub.b], writes=[ACCP[hf].b])
                v1()
        for gi in range(16):
            group(gi)
            next(fg, None); next(fg, None)
        for hf in range(2):
            S.op("dve", lambda e, hf=hf: e.scalar_tensor_tensor(Z.t[:, hf * 512:(hf + 1) * 512], X1c.t[:, hf * 512:(hf + 1) * 512], DN_ALPHA,
                                                                ACCP[hf].t[:], ALU.mult, ALU.add), reads=[X1c.b, ACCP[hf].b], writes=[Z.b])
        layer_norm_tile(S, Z, OUT, LNG, LNB, stat, mv, rstd)
        st(S, OUT, od[r0:r0 + 128, :], OUT.t[:], final=True)
    fg = front(0)
    for _ in fg:
        pass
    for ti in range(ntile):
        nfg = front(ti + 1) if ti + 1 < ntile else iter(())
        back(ti, nfg)
        for _ in nfg:
            pass


def phase23(S, PS, D):
    g2 = phase2(S, PS[0:4], D)
    g3 = phase3(S, PS[4:8], D)
    import os
    mode = os.environ.get("MK_MODE", "seq")
    if mode == "seq":
        for _ in g2:
            pass
        for _ in g3:
            pass
        return
    done2 = done3 = False
    while not (done2 and done3):
        for _ in range(2):
            if not done2:
                try:
                    next(g2)
                except StopIteration:
                    done2 = True
        if not done3:
            try:
                next(g3)
            except StopIteration:
                done3 = True


def build_fused(nph=4):
    nc = bass.Bass("TRN2", target_bir_lowering=False)
    D = {}
    for n, shp in (("xT", [1024, SEQ]), ("xTo", [1024, NTOK]), ("xo", [NTOK, 1024]), ("Wg", [2, 1024, 1796]), ("pcol", [2, 128, 19]),
                   ("w2a2", [2, 128, 256]), ("g2", [2, 128, 256]), ("bd", [128, 128]), ("cm", [128, 512]), ("idn", [128, 128]),
                   ("msel", [128, 2]), ("neg", [8, 128, 512]), ("mk", [5, 64, 512]), ("lg3", [8, 64, 64]), ("lb3", [8, 64, 64]),
                   ("wg", [1024, 2048]), ("pa", [512, 1024]), ("pb", [512, 1024]), ("wo", [1024, 1024]), ("lg1", [128, 1024]),
                   ("lb1", [128, 1024]), ("wq", [1024, 2048]), ("sk", [128, 2048]), ("u", [16384, 1024]), ("v", [16384, 1024]),
                   ("lg2", [128, 1024]), ("lb2", [128, 1024]), ("iota", [128, 256])):
        D[n] = din(nc, n, shp)
    D["out"] = dout(nc, "out", [NTOK, 1024])
    for n, shp in (("fc", [8, SEQ]),
                   ("al", [512, SEQ]), ("be", [512, SEQ]), ("ka", [512, SEQ]), ("rh", [512, SEQ]), ("gc", [512, SEQ // 64]),
                   ("nbe_tm", [SEQ, 512]), ("ka_tm", [SEQ, 512]), ("rv_tm", [SEQ, 512]), ("rg_tm", [SEQ, 512]), ("bo_tm", [SEQ, 512]),
                   ("x1", [NTOK, 1024]), ("x1T", [1024, NTOK])):
        D[n] = dscr(nc, "s_" + n, shp)
    for n, shp in (("fqo", [512, NTOK]), ("fk", [512, SEQ]), ("fvtm", [SEQ, 512]), ("fnc3", [8, 3, SEQ]), ("co3", [8, 3, NTOK]), ("yfo", [512, NTOK]), ("yr", [512, SEQ]),
                   ("uvb", [16384, 2048])):
        D[n] = dscr(nc, "s_" + n, shp, BF16)
    S = Sched(nc)
    PS = mk_psum(S)
    phases = (phase1, phase23, phase4, phase5)[:nph]
    for i, ph in enumerate(phases):
        S.phase_begin()
        ph(S, PS, D)
        S.phase_end(final=(i == len(phases) - 1))
    S.stack.close()
    return nc, S


def core_inputs(x, P, b, t):
    c_ = np.ascontiguousarray
    xb = x[b]
    own = xb.reshape(16, 512, 1024)[t::2].reshape(NTOK, 1024)
    bc = lambda v: c_(np.broadcast_to(v[None, :], (128, v.shape[0])))
    RB = 1544
    mu = P["rwkv_mu"]
    Wg = []; pcols = []; w2a2 = []; g2 = []
    two = lambda v: v.reshape(2, 128).T
    for g in range(2):
        ch = slice(256 * g, 256 * g + 256)
        cols = np.concatenate([
            np.arange(256 * g, 256 * g + 256), 512 + np.arange(256 * g, 256 * g + 256), 1024 + np.arange(256 * g, 256 * g + 256),
            RB + np.arange(256 * g, 256 * g + 256), RB + 512 + np.arange(256 * g, 256 * g + 256),
            RB + 1024 + np.arange(256 * g, 256 * g + 256), RB + np.arange(1536, 1792), 1536 + np.arange(4 * g, 4 * g + 4)])
        Wg.append(P["w_in"][:, cols])
        pc = np.zeros((128, 19), np.float32)
        pc[:, 0:2] = two(mu[0:512][ch]); pc[:, 2:4] = two(mu[512:1024][ch]); pc[:, 4:6] = two(mu[1024:1536][ch])
        pc[:, 6] = mu[1536:1664]; pc[:, 7] = mu[1664:1792]
        pc[:, 8:10] = two(P["rwkv_w0"][ch]); pc[:, 10:12] = two(P["rwkv_a0"][ch])
        pc[:, 12:14] = two(P["rwkv_k_k"][ch]); pc[:, 14:16] = two(P["rwkv_k_a"][ch]); pc[:, 16:18] = two(P["rwkv_r_k"][ch])
        pc[0:4, 18] = P["fox_f_bias"][4 * g:4 * g + 4]
        pcols.append(pc)
        w2a2.append(np.concatenate([P["rwkv_w2"][:, ch], P["rwkv_a2"][:, ch]], 0))
        g2.append(P["rwkv_g2"][:, ch])
    bd = np.kron(np.eye(2, dtype=np.float32), np.ones((64, 64), np.float32))
    cm = np.ones((128, 512), np.float32); cm[:, ::64] = 0.0
    msel = np.zeros((128, 2), np.float32); msel[:, t] = 1.0
    kpos = (128 * np.arange(8)[:, None, None] + np.arange(128)[None, :, None])
    qpos = 512 * t + np.arange(512)[None, None, :]
    neg = np.where(kpos <= qpos, 0.0, -30000.0).astype(np.float32)
    one = np.ones((64, 64), np.float32)
    rep = lambda m: np.tile(m, (1, 8))
    mk = np.stack([rep(np.tril(one, -1)), rep(np.triu(one, 1)), rep(np.triu(one)), rep(-np.triu(one)), rep(np.eye(64, dtype=np.float32))])
    lg3 = np.stack([np.broadcast_to(P["rwkv_ln_g"][h * 64:(h + 1) * 64][None, :], (64, 64)) for h in range(8)])
    lb3 = np.stack([np.broadcast_to(P["rwkv_ln_b"][h * 64:(h + 1) * 64][None, :], (64, 64)) for h in range(8)])
    sk = P["peer_sub_keys"].reshape(16, 128, 128).transpose(2, 0, 1).reshape(128, 16 * 128)
    iota = np.broadcast_to(np.arange(256, dtype=np.float32)[None, :], (128, 256))
    m = {"xT": xb.T, "xTo": own.T, "xo": own, "Wg": np.stack(Wg), "pcol": np.stack(pcols), "w2a2": np.stack(w2a2), "g2": np.stack(g2),
         "bd": bd, "cm": cm, "idn": np.eye(128, dtype=np.float32), "msel": msel, "neg": neg, "mk": mk, "lg3": lg3, "lb3": lb3,
         "wg": P["w_in"][:, 3336:], "pa": P["p_fox"], "pb": P["p_rwkv"], "wo": P["w_o"], "lg1": bc(P["ln1_g"]), "lb1": bc(P["ln1_b"]),
         "wq": P["peer_w_q"], "sk": sk, "u": P["peer_u"], "v": P["peer_v"], "lg2": bc(P["ln2_g"]), "lb2": bc(P["ln2_b"]), "iota": iota}
    return {k: c_(np.asarray(v, np.float32)) for k, v in m.items()}


def kernel(**inputs):
    x = np.asarray(inputs["x"], np.float32)
    P = {k: np.asarray(v, np.float32)[0] for k, v in inputs.items() if k != "x"}
    B = x.shape[0]
    nc, _ = build_fused()
    in_maps = [core_inputs(x, P, b, t) for b in range(B) for t in range(2)]
    res = run_bass_kernel_spmd(nc, in_maps, core_ids=list(range(2 * B)))
    out = np.empty_like(x)
    for b in range(B):
        ob = out[b].reshape(16, 512, 1024)
        for t in range(2):
            ob[t::2] = res.results[2 * b + t]["out"].reshape(8, 512, 1024)
    return out
```

```python
import numpy as np
import concourse.bass as bass
import concourse.mybir as mybir
from concourse.bass_utils import run_bass_kernel_spmd
from contextlib import ExitStack

F32 = mybir.dt.float32
BF16 = mybir.dt.bfloat16
U32 = mybir.dt.uint32
AF = mybir.ActivationFunctionType
ALU = mybir.AluOpType
AX = mybir.AxisListType

SYNC_SAME_ENGINE = True
EPOCH = 10 ** 9
DMA_MAX = 10 ** 6


class Buf:
    __slots__ = ("name", "w", "r")

    def __init__(self, name):
        self.name = name
        self.w = None
        self.r = {}


class DmaSem:
    __slots__ = ("handle", "count", "uid")
    _n = 0

    def __init__(self, handle):
        self.handle = handle
        self.count = 0
        DmaSem._n += 1
        self.uid = DmaSem._n


class Sched:
    ENG = ("sp", "act", "dve", "pool", "pe")

    def __init__(self, nc):
        self.nc = nc
        self.streams = {e: [] for e in self.ENG}
        self.seq = {e: 0 for e in self.ENG}
        self.known = {e: {} for e in self.ENG}
        self.esem = {}
        self.stack = ExitStack()
        self.nsem = 0
        self.ninstr = 0
        self.out_toks = []
        self.pi = 0
        self.pstack = None
        self.all_dmasems = []
        self.free_dmasems = {"hw": [], "sw": []}
        self.phase_tiles = []

    def sbuf(self, name, shape, dtype):
        stk = self.pstack if self.pstack is not None else self.stack
        return stk.enter_context(self.nc.sbuf_tensor(name, list(shape), dtype))

    def phase_begin(self):
        self.pstack = ExitStack()

    def phase_end(self, final=False):
        if final:
            for tok in self.out_toks:
                self._wait("sp", tok)
        self.barrier()
        self.emit_block()
        self.pstack.close()
        self.pstack = None
        for tl in self.phase_tiles:
            for kind, sm in tl._sems.items():
                self.free_dmasems[kind].append(sm)
            tl._sems = {}
        self.phase_tiles = []

    def barrier(self):
        for e in self.ENG:
            for f in self.ENG:
                if f != e and self.seq[f] > 0:
                    self._wait(e, ("eng", f, self.seq[f]))
            for sem in self.all_dmasems:
                if sem.count > 0:
                    self._wait(e, ("dma", sem, 16 * sem.count))

    def emit_block(self):
        nc = self.nc
        with nc.Block() as block:
            for e, deco in (("sp", block.sync), ("act", block.scalar), ("dve", block.vector),
                            ("pool", block.gpsimd), ("pe", block.tensor)):
                stream = self.streams[e]

                def body(eng, stream=stream):
                    for th in stream:
                        th(eng)
                deco(body)
        self.streams = {e: [] for e in self.ENG}

    def psum(self, name, shape, dtype):
        return self.stack.enter_context(self.nc.psum_tensor(name, list(shape), dtype))

    def newsem(self, name):
        self.nsem += 1
        return self.nc.alloc_semaphore(name=f"{name}_{self.nsem}")

    def dmasem(self, name="d", kind="hw"):
        if self.free_dmasems[kind]:
            return self.free_dmasems[kind].pop()
        d = DmaSem(self.newsem(name + kind))
        self.all_dmasems.append(d)
        return d

    def _esem(self, e, epoch):
        k = (e, epoch)
        if k not in self.esem:
            self.esem[k] = self.newsem(f"e_{e}{epoch}")
        return self.esem[k]

    def _wait(self, e, tok):
        if tok is None:
            return
        kind, ident, val = tok
        key = ident if kind == "eng" else ("dma", ident.uid)
        if self.known[e].get(key, 0) >= val:
            return
        self.known[e][key] = val
        if kind == "eng":
            sem = self._esem(ident, (val - 1) // EPOCH)
            v = (val - 1) % EPOCH + 1
        else:
            sem = ident.handle
            v = val
        self.streams[e].append(lambda eng, sem=sem, v=v: eng.wait_ge(sem, v))
        self.ninstr += 1

    def _deps(self, e, reads, writes, nowaw=False, soft=()):
        strict = SYNC_SAME_ENGINE and e != "pe"
        for b in reads:
            if b.w is not None:
                if not (e == "pe" and b.w[0] == "eng" and b.w[1] == "pe"):
                    self._wait(e, b.w)
        for b in writes:
            if b.w is not None and not nowaw:
                same = (b.w[0] == "eng" and b.w[1] == e)
                if (strict and not (same and b in soft)) or not same:
                    self._wait(e, b.w)
            for tok in b.r.values():
                if strict or not (tok[0] == "eng" and tok[1] == e):
                    self._wait(e, tok)

    def _record(self, tok, reads, writes):
        for b in writes:
            b.w = tok
            b.r = {}
        key = tok[1] if tok[0] == "eng" else ("dma", tok[1].uid)
        for b in reads:
            b.r[key] = tok

    def op(self, e, fn, reads=(), writes=(), soft=()):
        self._deps(e, reads, writes, soft=soft)
        self.seq[e] += 1
        s = self.seq[e]
        sem = self._esem(e, (s - 1) // EPOCH)
        self.streams[e].append(lambda eng, fn=fn, sem=sem: fn(eng).then_inc(sem, 1))
        self.ninstr += 1
        tok = ("eng", e, s)
        self._record(tok, reads, writes)
        return tok

    def dma(self, q, out, in_, sem, reads=(), writes=(), nowaw=False, fn=None, **kw):
        self._deps(q, reads, writes, nowaw=nowaw)
        sem.count += 1
        assert sem.count < DMA_MAX, "dma sem overflow"
        h = sem.handle
        if fn is None:
            self.streams[q].append(
                lambda eng, out=out, in_=in_, h=h, kw=kw: eng.dma_start(out=out, in_=in_, **kw).then_inc(h, 16))
        else:
            self.streams[q].append(lambda eng, fn=fn, h=h: fn(eng).then_inc(h, 16))
        self.ninstr += 1
        tok = ("dma", sem, 16 * sem.count)
        self._record(tok, reads, writes)
        return tok

    def emit(self):
        for tok in self.out_toks:
            self._wait("sp", tok)
        nc = self.nc
        with nc.Block() as block:
            for e, deco in (("sp", block.sync), ("act", block.scalar), ("dve", block.vector),
                            ("pool", block.gpsimd), ("pe", block.tensor)):
                stream = self.streams[e]

                def body(eng, stream=stream):
                    for th in stream:
                        th(eng)
                deco(body)
        self.stack.close()


class T:
    def __init__(self, S, name, shape, dtype=F32, psum=False):
        self.S = S
        S.ntile = getattr(S, "ntile", 0) + 1
        self.t = (S.psum if psum else S.sbuf)(f"t{S.ntile}_" + name, shape, dtype)
        self.b = Buf(name)
        self._sems = {}
        self.name = name
        if not psum:
            S.phase_tiles.append(self)

    def semq(self, q):
        kind = "sw" if q == "pool" else "hw"
        if kind not in self._sems:
            self._sems[kind] = self.S.dmasem(self.name, kind)
        return self._sems[kind]

    @property
    def sem(self):
        return self.semq("sp")


def ld(S, tl, dst, src, q="sp", nowaw=False):
    return S.dma(q, dst, src, tl.semq(q), writes=[tl.b], nowaw=nowaw)


def st(S, tl, dst, src, q="sp", final=False):
    tok = S.dma(q, dst, src, tl.semq(q), reads=[tl.b])
    if final:
        S.out_toks.append(tok)
    return tok


def dscr(nc, name, shape, dt=F32):
    return nc.dram_tensor(name, list(shape), dt, kind="Internal").ap()


def mk_psum(S, n=8):
    return [T(S, f"ps{i}", [128, 512], F32, psum=True) for i in range(n)]


def nxt(S, PS):
    p = PS[S.pi % len(PS)]
    S.pi += 1
    return p


def din(nc, name, shape, dt=F32):
    return nc.dram_tensor(name, list(shape), dt, kind="ExternalInput").ap()


def dout(nc, name, shape, dt=F32):
    return nc.dram_tensor(name, list(shape), dt, kind="ExternalOutput").ap()


SEQ = 8192
NSC = SEQ // 512
NEG_E = -0.6065306597126334


NTOK = 4096
DN_ALPHA = 2.0 ** 0.25
LN_EPS = 1e-5


def layer_norm_tile(S, Z, OUT, LNG, LNB, stat, mv, rstd):
    for hf in range(2):
        S.op("dve", lambda e, hf=hf: e.bn_stats(stat.t[:, hf * 6:(hf + 1) * 6], Z.t[:, hf * 512:(hf + 1) * 512]),
             reads=[Z.b], writes=[stat.b])
    S.op("dve", lambda e: e.bn_aggr(mv.t[:], stat.t[:]), reads=[stat.b], writes=[mv.b])
    S.op("dve", lambda e: e.tensor_scalar(rstd.t[:], mv.t[:, 1:2], LN_EPS, None, ALU.add), reads=[mv.b], writes=[rstd.b])
    S.op("act", lambda e: e.activation(rstd.t[:], rstd.t[:], AF.Sqrt), reads=[rstd.b], writes=[rstd.b])
    S.op("dve", lambda e: e.reciprocal(rstd.t[:], rstd.t[:]), reads=[rstd.b], writes=[rstd.b])
    S.op("dve", lambda e: e.tensor_scalar(Z.t[:], Z.t[:], mv.t[:, 0:1], rstd.t[:, 0:1], ALU.subtract, ALU.mult),
         reads=[Z.b, mv.b, rstd.b], writes=[Z.b])
    S.op("dve", lambda e: e.tensor_tensor(Z.t[:], Z.t[:], LNG.t[:], ALU.mult), reads=[Z.b, LNG.b], writes=[Z.b])
    S.op("dve", lambda e: e.tensor_tensor(OUT.t[:], Z.t[:], LNB.t[:], ALU.add), reads=[Z.b, LNB.b], writes=[OUT.b])


def phase1(S, PS, D):
    xv = D["xT"].rearrange("(kc p) t -> p kc t", p=128)
    xov = D["xTo"].rearrange("(kc p) t -> p kc t", p=128)
    W = T(S, "W", [128, 8, 1796], BF16); WS = [T(S, f"WS{i}", [128, 1796]) for i in range(2)]
    PC = T(S, "PC", [128, 19]); W2 = T(S, "W2", [128, 256]); G2 = T(S, "G2", [128, 256])
    BD = T(S, "BD", [128, 128]); CM = T(S, "CM", [128, 512]); IDN = T(S, "IDN", [128, 128]); MS = T(S, "MS", [128, 2])
    ONE = T(S, "ONE", [4, 512]); CAR = T(S, "CAR", [4, 1])
    XF = [T(S, f"XF{i}", [128, 8, 512]) for i in range(2)]
    X = [T(S, f"X{i}", [128, 8, 512], BF16) for i in range(2)]

    def load_x(i, src):
        xf = XF[i % 2]; xt = X[i % 2]
        ld(S, xf, xf.t[:], src, q="pool")
        S.op("pool", lambda e: e.tensor_copy(xt.t[:], xf.t[:]), reads=[xf.b], writes=[xt.b])
        return xt
    ld(S, BD, BD.t[:], D["bd"]); ld(S, CM, CM.t[:], D["cm"]); ld(S, IDN, IDN.t[:], D["idn"]); ld(S, MS, MS.t[:], D["msel"])
    S.op("dve", lambda e: e.memset(ONE.t[:], 1.0), writes=[ONE.b])
    Pn = ["r0", "r1", "k0", "k1", "v0", "v1", "l", "g"]
    P = {n: T(S, "P" + n, [128, 513]) for n in Pn}
    SH = {n: T(S, "SH" + n, [128, 512]) for n in Pn}
    tmp = T(S, "tmp", [128, 512])
    FO = [T(S, f"FO{i}", [128, 512], BF16) for i in range(2)]
    TME = [T(S, f"TME{i}", [128, 512]) for i in range(2)]
    FL = T(S, "FL", [4, 512]); FC = T(S, "FCt", [4, 512]); FN = T(S, "FNt", [4, 512])
    SP3 = T(S, "SP3", [8, 3, 512], BF16); SPR = T(S, "SPR", [8, 512])
    TH = T(S, "TH", [64, 512]); SGL = T(S, "SGL", [128, 512])
    nm = ["lw", "a", "g", "t1", "sq", "nr", "kk", "t2", "kmod", "cs", "en", "ep", "d2", "epv",
          "alpha", "t3", "beta", "nbeta", "kappa", "rho", "pr", "bo"]
    R = {n: T(S, "R" + n, [128, 512]) for n in nm}
    GC = T(S, "GC", [128, 8])
    pcmap = {"r0": 0, "r1": 1, "k0": 2, "k1": 3, "v0": 4, "v1": 5, "l": 6, "g": 7}
    blkmap = {"r0": 6, "r1": 7, "k0": 8, "k1": 9, "v0": 10, "v1": 11, "l": 12, "g": 13}
    bfc = Buf("fc_scr")
    cnt = {"fo": 0, "tme": 0}

    def mm_block(xt, col0, ncol):
        ps = nxt(S, PS)
        for kc in range(8):
            S.op("pe", lambda e, kc=kc: e.matmul(ps.t[0:ncol, :], W.t[:, kc, col0:col0 + ncol], xt.t[:, kc, :],
                                                 start=(kc == 0), stop=(kc == 7)), reads=[W.b, xt.b], writes=[ps.b])
        return ps

    def split3(src, n, dst_ap):
        S.op("dve", lambda e: e.tensor_copy(SP3.t[0:n, 0, :], src.t[0:n, :]), reads=[src.b], writes=[SP3.b])
        S.op("dve", lambda e: e.tensor_tensor(SPR.t[0:n, :], src.t[0:n, :], SP3.t[0:n, 0, :], ALU.subtract), reads=[src.b, SP3.b], writes=[SPR.b])
        S.op("dve", lambda e: e.tensor_copy(SP3.t[0:n, 1, :], SPR.t[0:n, :]), reads=[SPR.b], writes=[SP3.b])
        S.op("dve", lambda e: e.tensor_tensor(SPR.t[0:n, :], SPR.t[0:n, :], SP3.t[0:n, 1, :], ALU.subtract), reads=[SPR.b, SP3.b], writes=[SPR.b])
        S.op("dve", lambda e: e.tensor_copy(SP3.t[0:n, 2, :], SPR.t[0:n, :]), reads=[SPR.b], writes=[SP3.b])
        st(S, SP3, dst_ap, SP3.t[0:n, :, :])

    def group(g):
        Wv = D["Wg"][g].rearrange("(kc p) c -> p kc c", p=128)
        for kc in range(8):
            def wl(kc=kc):
                ws = WS[kc % 2]
                ld(S, ws, ws.t[:], Wv[:, kc, :], q="pool")
                S.op("pool", lambda e: e.tensor_copy(W.t[:, kc, :], ws.t[:]), reads=[ws.b], writes=[W.b])
            wl()
        ld(S, PC, PC.t[:], D["pcol"][g]); ld(S, W2, W2.t[:], D["w2a2"][g]); ld(S, G2, G2.t[:], D["g2"][g])
        S.op("dve", lambda e: e.memset(CAR.t[:], 0.0), writes=[CAR.b])
        for n in Pn:
            S.op("dve", lambda e, n=n: e.memset(P[n].t[:, 0:1], 0.0), writes=[P[n].b])
        pc = lambda j: PC.t[:, j:j + 1]

        def superchunk(sc):
            tsl = slice(sc * 512, (sc + 1) * 512)
            xt = load_x(sc, xv[:, :, tsl])

            def foxk(blk):
                ps = mm_block(xt, blk * 128, 128)
                fo = FO[cnt["fo"] % 2]; cnt["fo"] += 1
                S.op("act", lambda e: e.activation(fo.t[:], ps.t[:], AF.Copy), reads=[ps.b], writes=[fo.b])
                r0 = g * 256 + (blk % 2) * 128
                st(S, fo, D["fk"][r0:r0 + 128, tsl], fo.t[:])
            foxk(2); foxk(3)

            def foxv(pair):
                ps = nxt(S, PS)
                for j in range(2):
                    tt = pair * 2 + j
                    for kc in range(8):
                        S.op("pe", lambda e, j=j, tt=tt, kc=kc: e.matmul(ps.t[:, j * 256:(j + 1) * 256], xt.t[:, kc, tt * 128:(tt + 1) * 128],
                                                                         W.t[:, kc, 512:768], start=(kc == 0), stop=(kc == 7)),
                             reads=[W.b, xt.b], writes=[ps.b])
                fo = FO[cnt["fo"] % 2]; cnt["fo"] += 1
                S.op("act", lambda e: e.activation(fo.t[:], ps.t[:], AF.Copy), reads=[ps.b], writes=[fo.b])
                r0 = sc * 512 + pair * 256
                st(S, fo, D["fvtm"][r0:r0 + 256, g * 256:(g + 1) * 256].rearrange("(j p) c -> p j c", p=128),
                   fo.t[:].rearrange("p (j c) -> p j c", c=256))
            foxv(0); foxv(1)
            ps = mm_block(xt, 1792, 4)
            S.op("act", lambda e: e.activation(FL.t[:], ps.t[0:4, :], AF.Sigmoid, bias=PC.t[0:4, 18:19]), reads=[ps.b, PC.b], writes=[FL.b])
            S.op("act", lambda e: e.activation(FL.t[:], FL.t[:], AF.Ln), reads=[FL.b], writes=[FL.b])
            S.op("dve", lambda e: e.tensor_tensor_scan(FC.t[:], ONE.t[:], FL.t[:], CAR.t[:, 0:1], ALU.mult, ALU.add),
                 reads=[ONE.b, FL.b, CAR.b], writes=[FC.b])
            S.op("dve", lambda e: e.tensor_copy(CAR.t[:], FC.t[:, 511:512]), reads=[FC.b], writes=[CAR.b])
            S.op("dve", lambda e: e.tensor_scalar(FN.t[:], FC.t[:], -1.0, None, ALU.mult), reads=[FC.b], writes=[FN.b])
            S.dma("sp", D["fc"][g * 4:(g + 1) * 4, tsl], FC.t[:], FC.sem, reads=[FC.b], writes=[bfc], nowaw=True)
            split3(FN, 4, D["fnc3"][g * 4:(g + 1) * 4, :, tsl])

            def proj_shift(n):
                ps = mm_block(xt, blkmap[n] * 128, 128)
                p = P[n]; sh = SH[n]; mu = pc(pcmap[n])
                S.op("act", lambda e: e.activation(p.t[:, 1:513], ps.t[:], AF.Copy), reads=[ps.b], writes=[p.b])
                S.op("dve", lambda e: e.tensor_tensor(tmp.t[:], p.t[:, 0:512], p.t[:, 1:513], ALU.subtract), reads=[p.b], writes=[tmp.b])
                S.op("dve", lambda e: e.scalar_tensor_tensor(sh.t[:], tmp.t[:], mu, p.t[:, 1:513], ALU.mult, ALU.add),
                     reads=[p.b, tmp.b, PC.b], writes=[sh.b])
                S.op("dve", lambda e: e.tensor_copy(p.t[:, 0:1], p.t[:, 512:513]), reads=[p.b], writes=[p.b])
            for n in Pn:
                proj_shift(n)
            S.op("act", lambda e: e.activation(TH.t[:], SH["l"].t[0:64, :], AF.Tanh), reads=[SH["l"].b], writes=[TH.b])
            S.op("act", lambda e: e.activation(SGL.t[:], SH["g"].t[:], AF.Sigmoid), reads=[SH["g"].b], writes=[SGL.b])

            def rwkv_block(b):
                cs_ = slice(b * 128, b * 128 + 128)
                shr, shk, shv = SH[f"r{b}"], SH[f"k{b}"], SH[f"v{b}"]

                def mm1(lhs, rhs, reads):
                    ps = nxt(S, PS)
                    S.op("pe", lambda e: e.matmul(ps.t[:], lhs, rhs, start=True, stop=True), reads=reads, writes=[ps.b])
                    return ps

                def act(dst, src, func, rd, **kw):
                    S.op("act", lambda e: e.activation(dst.t[:], src, func, **kw), reads=rd, writes=[dst.b])

                def tt(dst, a, b_, op):
                    S.op("dve", lambda e: e.tensor_tensor(dst.t[:], a.t[:], b_.t[:], op), reads=[a.b, b_.b], writes=[dst.b])

                def ts(dst, a, s1, s2, op0, op1=None, extra=()):
                    if op1 is None:
                        S.op("dve", lambda e: e.tensor_scalar(dst.t[:], a.t[:], s1, s2, op0), reads=[a.b, *extra], writes=[dst.b])
                    else:
                        S.op("dve", lambda e: e.tensor_scalar(dst.t[:], a.t[:], s1, s2, op0, op1), reads=[a.b, *extra], writes=[dst.b])
                ps = mm1(W2.t[0:64, cs_], TH.t[:], [W2.b, TH.b])
                act(R["lw"], ps.t[:], AF.Sigmoid, [ps.b, PC.b], bias=pc(8 + b))
                ts(R["lw"], R["lw"], NEG_E, None, ALU.mult)
                ps = mm1(W2.t[64:128, cs_], SH["l"].t[64:128, :], [W2.b, SH["l"].b])
                act(R["a"], ps.t[:], AF.Sigmoid, [ps.b, PC.b], bias=pc(10 + b))
                ps = mm1(G2.t[:, cs_], SGL.t[:], [G2.b, SGL.b])
                act(R["g"], ps.t[:], AF.Copy, [ps.b])
                ts(R["t1"], shk, pc(12 + b), None, ALU.mult, extra=[PC.b])
                tt(R["sq"], R["t1"], R["t1"], ALU.mult)
                ps = mm1(BD.t[:], R["sq"].t[:], [BD.b, R["sq"].b])
                act(R["nr"], ps.t[:], AF.Sqrt, [ps.b])
                ts(R["nr"], R["nr"], 1e-12, None, ALU.max)
                S.op("dve", lambda e: e.reciprocal(R["nr"].t[:], R["nr"].t[:]), reads=[R["nr"].b], writes=[R["nr"].b])
                tt(R["kk"], R["t1"], R["nr"], ALU.mult)
                ts(R["t2"], R["a"], -1.0, pc(14 + b), ALU.add, ALU.mult, extra=[PC.b])
                S.op("dve", lambda e: e.scalar_tensor_tensor(R["kmod"].t[:], R["t2"].t[:], 1.0, shk.t[:], ALU.add, ALU.mult),
                     reads=[R["t2"].b, shk.b], writes=[R["kmod"].b])
                S.op("dve", lambda e: e.tensor_tensor_scan(R["cs"].t[:], CM.t[:], R["lw"].t[:], 0.0, ALU.mult, ALU.add),
                     reads=[CM.b, R["lw"].b], writes=[R["cs"].b])
                act(R["en"], R["cs"].t[:], AF.Exp, [R["cs"].b], scale=-1.0)
                act(R["ep"], R["cs"].t[:], AF.Exp, [R["cs"].b])
                tt(R["d2"], R["cs"], R["lw"], ALU.subtract)
                act(R["epv"], R["d2"].t[:], AF.Exp, [R["d2"].b])
                tt(R["alpha"], R["kk"], R["epv"], ALU.mult)
                tt(R["t3"], R["kk"], R["a"], ALU.mult)
                tt(R["beta"], R["t3"], R["en"], ALU.mult)
                ts(R["nbeta"], R["beta"], -1.0, None, ALU.mult)
                tt(R["kappa"], R["kmod"], R["en"], ALU.mult)
                tt(R["rho"], shr, R["ep"], ALU.mult)
                S.op("dve", lambda e: e.tensor_copy(GC.t[:], R["ep"].t[:, 63::64]), reads=[R["ep"].b], writes=[GC.b])
                S.op("dve", lambda e: e.scalar_tensor_tensor(R["pr"].t[:], shr.t[:], pc(16 + b), R["kmod"].t[:], ALU.mult, ALU.mult),
                     reads=[shr.b, PC.b, R["kmod"].b], writes=[R["pr"].b])
                ps = mm1(BD.t[:], R["pr"].t[:], [BD.b, R["pr"].b])
                act(R["bo"], ps.t[:], AF.Copy, [ps.b])
                r0 = g * 256 + b * 128
                for on, tl in (("al", R["alpha"]), ("be", R["beta"]), ("ka", R["kappa"]), ("rh", R["rho"])):
                    st(S, tl, D[on][r0:r0 + 128, tsl], tl.t[:])
                st(S, GC, D["gc"][r0:r0 + 128, sc * 8:(sc + 1) * 8], GC.t[:])

                def tmaj(on, tl):
                    ps = nxt(S, PS)
                    for t4 in range(4):
                        S.op("pe", lambda e, t4=t4: e.matmul(ps.t[:, t4 * 128:(t4 + 1) * 128], tl.t[:, t4 * 128:(t4 + 1) * 128], IDN.t[:],
                                                             start=True, stop=True), reads=[tl.b, IDN.b], writes=[ps.b])
                    te = TME[cnt["tme"] % 2]; cnt["tme"] += 1
                    S.op("act", lambda e: e.activation(te.t[:], ps.t[:], AF.Copy), reads=[ps.b], writes=[te.b])
                    st(S, te, D[on][tsl, r0:r0 + 128].rearrange("(t4 p) c -> p t4 c", p=128), te.t[:].rearrange("p (t4 c) -> p t4 c", c=128))
                for on, tl in (("nbe_tm", R["nbeta"]), ("ka_tm", R["kappa"]), ("rv_tm", shv), ("rg_tm", R["g"]), ("bo_tm", R["bo"])):
                    tmaj(on, tl)
            for b in range(2):
                rwkv_block(b)
        for sc in range(NSC):
            superchunk(sc)

        def ownq(i):
            xt = load_x(i, xov[:, :, i * 512:(i + 1) * 512])
            for blk in range(2):
                def one(blk=blk):
                    ps = mm_block(xt, blk * 128, 128)
                    fo = FO[cnt["fo"] % 2]; cnt["fo"] += 1
                    S.op("act", lambda e: e.activation(fo.t[:], ps.t[:], AF.Copy, scale=0.125), reads=[ps.b], writes=[fo.b])
                    r0 = g * 256 + blk * 128
                    st(S, fo, D["fqo"][r0:r0 + 128, i * 512:(i + 1) * 512], fo.t[:])
                one()
        for i in range(NTOK // 512):
            ownq(i)
    for g in range(2):
        group(g)
    FCA = T(S, "FCA", [8, 512]); FCB = T(S, "FCB", [8, 512])

    def blend(i):
        S.dma("sp", FCA.t[:], D["fc"][:, (2 * i) * 512:(2 * i + 1) * 512], FCA.sem, reads=[bfc], writes=[FCA.b])
        S.dma("sp", FCB.t[:], D["fc"][:, (2 * i + 1) * 512:(2 * i + 2) * 512], FCB.sem, reads=[bfc], writes=[FCB.b])
        S.op("dve", lambda e: e.tensor_scalar(FCA.t[:], FCA.t[:], MS.t[0:8, 0:1], None, ALU.mult), reads=[FCA.b, MS.b], writes=[FCA.b])
        S.op("dve", lambda e: e.scalar_tensor_tensor(FCA.t[:], FCB.t[:], MS.t[0:8, 1:2], FCA.t[:], ALU.mult, ALU.add),
             reads=[FCA.b, FCB.b, MS.b], writes=[FCA.b])
        split3(FCA, 8, D["co3"][:, :, i * 512:(i + 1) * 512])
    for i in range(NTOK // 512):
        blend(i)


def phase2(S, PS, D):
    SCP = PS[0:2]; ACC = PS[2:4]
    QA = T(S, "QA", [70, NTOK], BF16); KA = T(S, "KA", [70, SEQ], BF16); VO = T(S, "VO", [128, 64, 128], BF16)
    NEG = T(S, "NEG", [128, 8, 512])
    PT = [T(S, f"PT{i}", [128, 512], BF16) for i in range(3)]
    TMP = [T(S, f"TMPm{i}", [128, 512]) for i in range(2)]
    RD = T(S, "RD", [128, 512]); Y = T(S, "Y", [64, 512], BF16)
    ld(S, NEG, NEG.t[:], D["neg"].rearrange("j p q -> p j q"))
    S.op("dve", lambda e: e.memset(QA.t[64:70, :], 1.0), writes=[QA.b])
    S.op("dve", lambda e: e.memset(KA.t[64:70, :], 1.0), writes=[KA.b])
    S.op("dve", lambda e: e.memset(VO.t[:, :, 64:128], 1.0), writes=[VO.b])
    cnt = {"pi": 0, "pt": 0, "tm": 0}

    def head(hh):
        r = slice(hh * 64, (hh + 1) * 64)
        for i in range(2):
            sl = slice(i * 2048, (i + 1) * 2048)
            ld(S, QA, QA.t[0:64, sl], D["fqo"][r, sl], nowaw=(i > 0))
        ld(S, QA, QA.t[64:67, :], D["co3"][hh], nowaw=True)
        for i in range(4):
            sl = slice(i * 2048, (i + 1) * 2048)
            ld(S, KA, KA.t[0:64, sl], D["fk"][r, sl], nowaw=(i > 0))
        ld(S, KA, KA.t[67:70, :], D["fnc3"][hh], nowaw=True)
        ld(S, VO, VO.t[:, :, 0:64], D["fvtm"][:, r].rearrange("(kb p) d -> p kb d", p=128), nowaw=True)

        def qchunk(i):
            acc = ACC[i % 2]
            nkb = 8 * i + 8

            def scores(kb):
                j = kb - 8 * i
                ps = SCP[cnt["pi"] % 2]; cnt["pi"] += 1
                pt = PT[cnt["pt"] % 3]; cnt["pt"] += 1
                S.op("pe", lambda e: e.matmul(ps.t[:], KA.t[:, kb * 128:(kb + 1) * 128], QA.t[:, i * 512:(i + 1) * 512], start=True, stop=True),
                     reads=[KA.b, QA.b], writes=[ps.b])
                if j >= 0:
                    tm = TMP[cnt["tm"] % 2]; cnt["tm"] += 1
                    S.op("dve", lambda e: e.tensor_tensor(tm.t[:], ps.t[:], NEG.t[:, j, :], ALU.add), reads=[ps.b, NEG.b], writes=[tm.b])
                    S.op("act", lambda e: e.activation(pt.t[:], tm.t[:], AF.Exp), reads=[tm.b], writes=[pt.b])
                else:
                    S.op("act", lambda e: e.activation(pt.t[:], ps.t[:], AF.Exp), reads=[ps.b], writes=[pt.b])
                return kb, pt

            def pv(kb, pt):
                S.op("pe", lambda e: e.matmul(acc.t[:], VO.t[:, kb, :], pt.t[:], start=(kb == 0), stop=(kb == nkb - 1)),
                     reads=[VO.b, pt.b], writes=[acc.b])
            prev = None
            for kb in range(nkb):
                cur = scores(kb)
                if prev is not None:
                    pv(*prev)
                prev = cur
                yield
            pv(*prev)
            S.op("dve", lambda e: e.reciprocal(RD.t[64:128, :], acc.t[64:128, :]), reads=[acc.b], writes=[RD.b])
            S.op("dve", lambda e: e.tensor_tensor(Y.t[:], acc.t[0:64, :], RD.t[64:128, :], ALU.mult), reads=[acc.b, RD.b], writes=[Y.b])
            st(S, Y, D["yfo"][r, i * 512:(i + 1) * 512], Y.t[:])
        for i in range(NTOK // 512):
            yield from qchunk(i)
    for hh in range(8):
        yield from head(hh)


GN_EPS = 64e-5


def phase3(S, PS, D):
    MK = [T(S, f"MK{i}", [128, 512]) for i in range(5)]
    for i in range(5):
        ld(S, MK[i], MK[i].t[0:64, :], D["mk"][i])
        ld(S, MK[i], MK[i].t[64:128, :], D["mk"][i], nowaw=True)
    MSL, MSU, MIU, NMIU, ID8 = MK
    FM = [[T(S, f"FM{p}_{i}", [128, 512]) for i in range(4)] for p in range(2)]
    TM = [[T(S, f"TM{p}_{i}", [128, 8, 64]) for i in range(5)] for p in range(2)]
    GC = T(S, "GC3", [128, 128]); LG = T(S, "LG", [128, 64]); LB = T(S, "LB", [128, 64])
    mk3 = lambda n: T(S, n, [128, 8, 64])
    A = mk3("A"); AT = mk3("AT"); Wt = [mk3("W0"), mk3("W1")]; Pt = [mk3("P0"), mk3("P1")]; PTt = [mk3("PT0"), mk3("PT1")]
    AakT = mk3("AakT"); nArbT = mk3("nArbT"); ArkT = mk3("ArkT")
    ST = [T(S, "ST0", [128, 64]), T(S, "ST1", [128, 64])]
    STg = T(S, "STg", [128, 64]); RHS = T(S, "RHS", [128, 64]); US = T(S, "US", [128, 64])
    YO = [mk3("YO0"), mk3("YO1")]; YT = [T(S, "YT0", [128, 512], BF16), T(S, "YT1", [128, 512], BF16)]
    stat = T(S, "stat3", [128, 6]); mv = T(S, "mv3", [128, 2]); rstd = T(S, "rstd3", [128, 1]); yn = T(S, "yn", [128, 64]); bt = T(S, "bt", [128, 64])
    cnt = {"st": 0, "cast": 0}
    fmn = ("al", "be", "ka", "rh"); tmn = ("nbe_tm", "ka_tm", "rv_tm", "rg_tm", "bo_tm")
    HS = (slice(0, 64), slice(64, 128))
    CF = [T(S, f"CF{i}", [128, 4096]) for i in range(2)]; CB = [T(S, f"CB{i}", [128, 4096], BF16) for i in range(2)]
    NCH = 16384 * 1024 // 128 // 4096

    def cast_step():
        k = cnt["cast"]
        if k >= 2 * NCH:
            return
        cnt["cast"] += 1
        src = D["u" if k < NCH else "v"].rearrange("(p r) d -> p (r d)", p=128)
        dst = D["uvb"].rearrange("(p r) (two d) -> p r two d", p=128, two=2)
        c = k % NCH
        cf = CF[k % 2]; cb = CB[k % 2]
        ld(S, cf, cf.t[:], src[:, c * 4096:(c + 1) * 4096], q="pool")
        S.op("pool", lambda e: e.tensor_copy(cb.t[:], cf.t[:]), reads=[cf.b], writes=[cb.b])
        st(S, cb, dst[:, c * 4:(c + 1) * 4, 0 if k < NCH else 1, :], cb.t[:].rearrange("p (r d) -> p r d", d=1024), q="pool")

    def mm2(ps, col, lhs, rhs, reads, start=True, stop=True):
        for hs in HS:
            S.op("pe", lambda e, hs=hs: e.matmul(ps.t[hs, col * 64:(col + 1) * 64], lhs(hs), rhs(hs), start=start, stop=stop),
                 reads=reads, writes=[ps.b])

    def batch_mm(lhs_of, rhs_of, reads):
        ps = nxt(S, PS)
        for j in range(8):
            mm2(ps, j, (lambda hs, j=j: lhs_of(j, hs)), (lambda hs, j=j: rhs_of(j, hs)), reads)
        return ps

    def pair(hp):
        hh = 2 * hp
        r = slice(hh * 64, (hh + 2) * 64)
        ld(S, GC, GC.t[:], D["gc"][r, :])
        ld(S, LG, LG.t[:], D["lg3"][hh:hh + 2].rearrange("h p c -> (h p) c"))
        ld(S, LB, LB.t[:], D["lb3"][hh:hh + 2].rearrange("h p c -> (h p) c"))
        s0 = ST[cnt["st"] % 2]
        S.op("dve", lambda e: e.memset(s0.t[:], 0.0), writes=[s0.b])

        def superchunk(sc):
            fm = FM[sc % 2]; tm = TM[sc % 2]
            tsl = slice(sc * 512, (sc + 1) * 512)
            for i in range(4):
                ld(S, fm[i], fm[i].t[:], D[fmn[i]][r, tsl])
            for i in range(5):
                for k_, hs in enumerate(HS):
                    rr = slice((hh + k_) * 64, (hh + k_ + 1) * 64)
                    ld(S, tm[i], tm[i].t[hs, :, :], D[tmn[i]][tsl, rr].rearrange("(c t) k -> t c k", t=64), nowaw=(k_ > 0))
            alT, beT, kaT, rhT = fm
            nbe, ka, vm, gt, bo = tm
            fsl = lambda t_, j, hs: t_.t[hs, j * 64:(j + 1) * 64]
            f3 = lambda t_, j, hs: t_.t[hs, j, :]
            flat = lambda t_: t_.t[:].rearrange("p a b -> p (a b)")

            def evac_mask(dst, ps, mk):
                S.op("dve", lambda e: e.tensor_tensor(flat(dst), ps.t[:], mk.t[:], ALU.mult), reads=[ps.b, mk.b], writes=[dst.b])

            def evac_copy(dst, ps):
                S.op("act", lambda e: e.activation(flat(dst), ps.t[:], AF.Copy), reads=[ps.b], writes=[dst.b])

            def evac_add(dst, ps, src):
                S.op("dve", lambda e: e.tensor_tensor(flat(dst), ps.t[:], flat(src), ALU.add), reads=[ps.b, src.b], writes=[dst.b])

            def bmm3(l, r_):
                return batch_mm(lambda j, hs: f3(l, j, hs), lambda j, hs: f3(r_, j, hs), [l.b, r_.b])

            def bmmf(l, r_):
                return batch_mm(lambda j, hs: fsl(l, j, hs), lambda j, hs: fsl(r_, j, hs), [l.b, r_.b])

            evac_mask(A, bmmf(alT, beT), MSL)
            evac_mask(AT, bmmf(beT, alT), MSU)
            W0 = Wt[0]
            S.op("dve", lambda e: e.tensor_tensor(flat(W0), ID8.t[:], flat(AT), ALU.subtract), reads=[ID8.b, AT.b], writes=[W0.b])
            evac_copy(Pt[0], bmm3(AT, A))
            evac_copy(PTt[0], bmm3(A, AT))
            for i in range(5):
                Wc, Pc, PTc = Wt[i % 2], Pt[i % 2], PTt[i % 2]
                Wn, Pn_, PTn = Wt[(i + 1) % 2], Pt[(i + 1) % 2], PTt[(i + 1) % 2]
                evac_add(Wn, bmm3(Pc, Wc), Wc)
                if i < 4:
                    evac_copy(Pn_, bmm3(PTc, Pc))
                    evac_copy(PTn, bmm3(Pc, PTc))
            Wf = Wt[5 % 2]
            evac_mask(AakT, bmmf(kaT, alT), MSU)
            evac_mask(nArbT, bmmf(beT, rhT), NMIU)
            evac_mask(ArkT, bmmf(kaT, rhT), MIU)
            yo = YO[sc % 2]; yt = YT[sc % 2]
            yield

            def chunk(j):
                c = sc * 8 + j
                st0 = ST[cnt["st"] % 2]; st1 = ST[(cnt["st"] + 1) % 2]; cnt["st"] += 1
                gcol = GC.t[:, c:c + 1]
                sT = lambda t_: (lambda hs: t_.t[hs, :])
                psr = nxt(S, PS)
                mm2(psr, 0, lambda hs: fsl(alT, j, hs), sT(st0), [alT.b, st0.b], True, False)
                mm2(psr, 0, lambda hs: f3(AakT, j, hs), lambda hs: f3(vm, j, hs), [AakT.b, vm.b], False, True)
                S.op("act", lambda e: e.activation(RHS.t[:], psr.t[:, 0:64], AF.Copy), reads=[psr.b], writes=[RHS.b])
                S.op("dve", lambda e: e.tensor_scalar(STg.t[:], st0.t[:], gcol, None, ALU.mult), reads=[st0.b, GC.b], writes=[STg.b])
                psu = nxt(S, PS)
                mm2(psu, 0, lambda hs: f3(Wf, j, hs), sT(RHS), [Wf.b, RHS.b])
                S.op("act", lambda e: e.activation(US.t[:], psu.t[:, 0:64], AF.Copy), reads=[psu.b], writes=[US.b])
                psy = nxt(S, PS)
                mm2(psy, 0, lambda hs: fsl(rhT, j, hs), sT(st0), [rhT.b, st0.b], True, False)
                mm2(psy, 0, lambda hs: f3(nArbT, j, hs), sT(US), [nArbT.b, US.b], False, False)
                mm2(psy, 0, lambda hs: f3(ArkT, j, hs), lambda hs: f3(vm, j, hs), [ArkT.b, vm.b], False, True)
                psd = nxt(S, PS)
                mm2(psd, 0, lambda hs: f3(nbe, j, hs), sT(US), [nbe.b, US.b], True, False)
                mm2(psd, 0, lambda hs: f3(ka, j, hs), lambda hs: f3(vm, j, hs), [ka.b, vm.b], False, True)
                S.op("dve", lambda e: e.scalar_tensor_tensor(st1.t[:], psd.t[:, 0:64], gcol, STg.t[:], ALU.mult, ALU.add),
                     reads=[psd.b, GC.b, STg.b], writes=[st1.b])
                py = psy.t[:, 0:64]
                S.op("dve", lambda e: e.bn_stats(stat.t[:], py), reads=[psy.b], writes=[stat.b])
                S.op("dve", lambda e: e.bn_aggr(mv.t[:], stat.t[:]), reads=[stat.b], writes=[mv.b])
                S.op("dve", lambda e: e.tensor_scalar(rstd.t[:], mv.t[:, 1:2], GN_EPS, None, ALU.add), reads=[mv.b], writes=[rstd.b])
                S.op("act", lambda e: e.activation(rstd.t[:], rstd.t[:], AF.Sqrt), reads=[rstd.b], writes=[rstd.b])
                S.op("dve", lambda e: e.reciprocal(rstd.t[:], rstd.t[:]), reads=[rstd.b], writes=[rstd.b])
                S.op("dve", lambda e: e.tensor_scalar(yn.t[:], py, mv.t[:, 0:1], rstd.t[:, 0:1], ALU.subtract, ALU.mult),
                     reads=[psy.b, mv.b, rstd.b], writes=[yn.b])
                S.op("dve", lambda e: e.tensor_tensor(yn.t[:], yn.t[:], LG.t[:], ALU.mult), reads=[yn.b, LG.b], writes=[yn.b])
                S.op("dve", lambda e: e.tensor_tensor(yn.t[:], yn.t[:], LB.t[:], ALU.add), reads=[yn.b, LB.b], writes=[yn.b])
                S.op("dve", lambda e: e.tensor_tensor(bt.t[:], bo.t[:, j, :], vm.t[:, j, :], ALU.mult), reads=[bo.b, vm.b], writes=[bt.b])
                S.op("dve", lambda e: e.tensor_tensor(yn.t[:], yn.t[:], bt.t[:], ALU.add), reads=[yn.b, bt.b], writes=[yn.b])
                S.op("dve", lambda e: e.tensor_tensor(yo.t[:, j, :], yn.t[:], gt.t[:, j, :], ALU.mult), reads=[yn.b, gt.b], writes=[yo.b])
            for j in range(8):
                chunk(j)
                yield
            ps = batch_mm(lambda j, hs: f3(yo, j, hs), lambda j, hs: ID8.t[hs, 0:64], [yo.b, ID8.b])
            S.op("act", lambda e: e.activation(yt.t[:], ps.t[:], AF.Copy), reads=[ps.b], writes=[yt.b])
            st(S, yt, D["yr"][r, tsl], yt.t[:], q="act")
            cast_step(); cast_step()
        for sc in range(NSC):
            yield from superchunk(sc)
    for hp in range(4):
        yield from pair(hp)
    while cnt["cast"] < 2 * NCH:
        cast_step()


def phase4(S, PS, D):
    WG = T(S, "WG", [128, 8, 1024], BF16); PA = T(S, "PA", [128, 4, 1024], BF16); PB = T(S, "PB", [128, 4, 1024], BF16)
    WO = T(S, "WO", [128, 8, 1024], BF16); WST = [T(S, f"WST{i}", [128, 1024]) for i in range(2)]
    LNG = T(S, "LNG", [128, 1024]); LNB = T(S, "LNB", [128, 1024]); MS = T(S, "MS4", [128, 2]); IDN = T(S, "IDN4", [128, 128])
    wgv = D["wg"].rearrange("(k p) c -> p k c", p=128)
    cw = {"n": 0}

    def ldw(dst, dslice, src):
        ws = WST[cw["n"] % 2]; cw["n"] += 1
        ld(S, ws, ws.t[:], src, q="pool")
        S.op("pool", lambda e: e.tensor_copy(dslice, ws.t[:]), reads=[ws.b], writes=[dst.b])
    for kc in range(8):
        ldw(WO, WO.t[:, kc, :], D["wo"].rearrange("(k p) c -> p k c", p=128)[:, kc, :])
    for kc in range(4):
        ldw(PA, PA.t[:, kc, :], D["pa"].rearrange("(k p) c -> p k c", p=128)[:, kc, :])
        ldw(PB, PB.t[:, kc, :], D["pb"].rearrange("(k p) c -> p k c", p=128)[:, kc, :])
    ld(S, LNG, LNG.t[:], D["lg1"]); ld(S, LNB, LNB.t[:], D["lb1"]); ld(S, MS, MS.t[:], D["msel"]); ld(S, IDN, IDN.t[:], D["idn"])
    XTF = T(S, "XTF", [128, 8, 512]); XT = T(S, "XT", [128, 8, 512], BF16)
    YF = T(S, "YF", [128, 4, 512], BF16); YR = T(S, "YR", [128, 4, 512], BF16); YRb = T(S, "YRb", [128, 4, 512], BF16)
    MT = T(S, "MT", [128, 8, 512], BF16); SG = T(S, "SG", [128, 512])
    XR = T(S, "XR", [128, 1024]); Z = T(S, "Z", [128, 1024]); X1 = T(S, "X14", [128, 1024]); TP = T(S, "TP", [128, 512])
    stat = T(S, "stat4", [128, 12]); mv = T(S, "mv4", [128, 2]); rstd = T(S, "rstd4", [128, 1])
    yrv = D["yr"].rearrange("(k p) t -> p k t", p=128)

    def superchunk(i):
        tsl = slice(i * 512, (i + 1) * 512)
        ld(S, XTF, XTF.t[:], D["xTo"].rearrange("(k p) t -> p k t", p=128)[:, :, tsl], q="pool")
        S.op("pool", lambda e: e.tensor_copy(XT.t[:], XTF.t[:]), reads=[XTF.b], writes=[XT.b])
        ld(S, YF, YF.t[:], D["yfo"].rearrange("(k p) t -> p k t", p=128)[:, :, tsl])
        ld(S, YR, YR.t[:], yrv[:, :, (2 * i) * 512:(2 * i + 1) * 512])
        ld(S, YRb, YRb.t[:], yrv[:, :, (2 * i + 1) * 512:(2 * i + 2) * 512])
        fl = lambda t_: t_.t[:].rearrange("p a b -> p (a b)")
        S.op("dve", lambda e: e.tensor_scalar(fl(YR), fl(YR), MS.t[:, 0:1], None, ALU.mult), reads=[YR.b, MS.b], writes=[YR.b])
        S.op("dve", lambda e: e.scalar_tensor_tensor(fl(YR), fl(YRb), MS.t[:, 1:2], fl(YR), ALU.mult, ALU.add),
             reads=[YR.b, YRb.b, MS.b], writes=[YR.b])

        def branch(goff, PW, Y, first):
            for kc in range(8):
                ldw(WG, WG.t[:, kc, :], wgv[:, kc, goff:goff + 1024])

            def nblock(nb):
                psg = nxt(S, PS)
                for kc in range(8):
                    S.op("pe", lambda e, kc=kc: e.matmul(psg.t[:], WG.t[:, kc, nb * 128:nb * 128 + 128], XT.t[:, kc, :],
                                                         start=(kc == 0), stop=(kc == 7)), reads=[WG.b, XT.b], writes=[psg.b])
                S.op("act", lambda e: e.activation(SG.t[:], psg.t[:], AF.Sigmoid), reads=[psg.b], writes=[SG.b])
                psz = nxt(S, PS)
                for kc in range(4):
                    S.op("pe", lambda e, kc=kc: e.matmul(psz.t[:], PW.t[:, kc, nb * 128:nb * 128 + 128], Y.t[:, kc, :],
                                                         start=(kc == 0), stop=(kc == 3)), reads=[PW.b, Y.b], writes=[psz.b])
                if first:
                    S.op("dve", lambda e: e.tensor_tensor(MT.t[:, nb, :], psz.t[:], SG.t[:], ALU.mult), reads=[psz.b, SG.b], writes=[MT.b])
                else:
                    S.op("dve", lambda e: e.tensor_tensor(SG.t[:], psz.t[:], SG.t[:], ALU.mult), reads=[psz.b, SG.b], writes=[SG.b])
                    S.op("dve", lambda e: e.tensor_tensor(MT.t[:, nb, :], MT.t[:, nb, :], SG.t[:], ALU.add), reads=[MT.b, SG.b], writes=[MT.b])
            for nb in range(8):
                nblock(nb)
        branch(0, PA, YF, True)
        branch(1024, PB, YR, False)

        def ttile(tt):
            r0 = i * 512 + tt * 128
            ld(S, XR, XR.t[:], D["xo"][r0:r0 + 128, :])

            def half(hf):
                ps = nxt(S, PS)
                for nb in range(8):
                    S.op("pe", lambda e, nb=nb: e.matmul(ps.t[:], MT.t[:, nb, tt * 128:(tt + 1) * 128], WO.t[:, nb, hf * 512:(hf + 1) * 512],
                                                         start=(nb == 0), stop=(nb == 7)), reads=[MT.b, WO.b], writes=[ps.b])
                S.op("dve", lambda e: e.scalar_tensor_tensor(Z.t[:, hf * 512:(hf + 1) * 512], XR.t[:, hf * 512:(hf + 1) * 512], DN_ALPHA,
                                                             ps.t[:], ALU.mult, ALU.add), reads=[XR.b, ps.b], writes=[Z.b])
            half(0); half(1)
            layer_norm_tile(S, Z, X1, LNG, LNB, stat, mv, rstd)
            st(S, X1, D["x1"][r0:r0 + 128, :], X1.t[:])

            def tgroup(gq):
                ps = nxt(S, PS)
                for bi in range(4):
                    kc = gq * 4 + bi
                    S.op("pe", lambda e, bi=bi, kc=kc: e.matmul(ps.t[:, bi * 128:(bi + 1) * 128], X1.t[:, kc * 128:(kc + 1) * 128], IDN.t[:],
                                                                start=True, stop=True), reads=[X1.b, IDN.b], writes=[ps.b])
                S.op("act", lambda e: e.activation(TP.t[:], ps.t[:], AF.Copy), reads=[ps.b], writes=[TP.b])
                st(S, TP, D["x1T"][gq * 512:(gq + 1) * 512, r0:r0 + 128].rearrange("(bi p) t -> p bi t", p=128),
                   TP.t[:].rearrange("p (bi t) -> p bi t", t=128))
            tgroup(0); tgroup(1)
        for tt in range(4):
            ttile(tt)
    for i in range(NTOK // 512):
        superchunk(i)


def phase5(S, PS, D, ntile=NTOK // 128):
    x1d = D["x1"]; x1Td = D["x1T"]; wqd = D["wq"]; skd = D["sk"]
    lgd = D["lg2"]; lbd = D["lb2"]; iod = D["iota"]; od = D["out"]
    WQ = T(S, "WQ", [128, 8, 2048]); SK = T(S, "SK", [128, 16, 128])
    LNG = T(S, "LNG", [128, 1024]); LNB = T(S, "LNB", [128, 1024])
    for kc in range(8):
        ld(S, WQ, WQ.t[:, kc, :], wqd.rearrange("(k p) c -> p k c", p=128)[:, kc, :], nowaw=True)
    ld(S, SK, SK.t[:], skd.rearrange("p (b k) -> p b k", k=128))
    ld(S, LNG, LNG.t[:], lgd); ld(S, LNB, LNB.t[:], lbd)
    IOT = T(S, "IOT", [128, 256]); ld(S, IOT, IOT.t[:], iod)
    BPU = T(S, "BPU", [128, 16], U32); BPF = T(S, "BPF", [128, 16])
    X1 = [T(S, f"X1_{i}", [128, 1024]) for i in range(2)]; X1T = T(S, "X1T", [128, 8, 128])
    QT = T(S, "QT", [128, 16, 128]); SC = T(S, "SC", [128, 16, 128]); SC2 = T(S, "SC2", [128, 128])
    TS = T(S, "TS", [128, 16, 16]); TI = T(S, "TI", [128, 16, 16], U32); TIF = T(S, "TIF", [128, 16, 16]); TI128 = T(S, "TI128", [128, 16, 16])
    CS = T(S, "CS", [128, 256]); CI = T(S, "CI", [128, 256]); CS2 = T(S, "CS2", [128, 256]); JKs = [T(S, f"JK{i}", [128, 256]) for i in range(2)]
    BS = T(S, "BS", [128, 8, 16]); BP = T(S, "BP", [128, 8], U32); IDF = T(S, "IDF", [128, 128]); IDX = [T(S, f"IDX{i}", [128, 128], U32) for i in range(2)]
    NM = T(S, "NM", [128, 8]); EX = T(S, "EX", [128, 8, 16]); SM = T(S, "SM", [128, 8]); GW = [T(S, f"GW{i}", [128, 128]) for i in range(2)]
    UV = [[T(S, f"UV{p}_{i}", [128, 2048], BF16) for i in range(8)] for p in range(2)]
    DG = [T(S, f"DG{i}", [128, 128], BF16) for i in range(4)]; IDB = T(S, "IDB", [128, 128], BF16); IDF32 = T(S, "IDF32", [128, 128])
    ld(S, IDF32, IDF32.t[:], D["idn"])
    S.op("dve", lambda e: e.tensor_copy(IDB.t[:], IDF32.t[:]), reads=[IDF32.b], writes=[IDB.b])
    X1B = [T(S, f"X1B{i}", [128, 1024], BF16) for i in range(2)]
    JK2s = [T(S, f"JK2_{i}", [128, 1024], BF16) for i in range(2)]; H = T(S, "H", [128, 128]); HG = T(S, "HG", [128, 128])
    Z = T(S, "Z", [128, 1024]); OUT = T(S, "OUT", [128, 1024])
    ACCP = PS[6:8]; PS = PS[0:6]; uvd = D["uvb"]
    stat = T(S, "stat5", [128, 12]); mv = T(S, "mv5", [128, 2]); rstd = T(S, "rstd5", [128, 1])
    cnt = {"ub": 0, "dg": 0, "jk": 0, "jk2": 0}

    def front(ti):
        r0 = ti * 128
        X1c, X1Bc, IDXc, GWc = X1[ti % 2], X1B[ti % 2], IDX[ti % 2], GW[ti % 2]
        ld(S, X1c, X1c.t[:], x1d[r0:r0 + 128, :])
        S.op("dve", lambda e: e.tensor_copy(X1Bc.t[:], X1c.t[:]), reads=[X1c.b], writes=[X1Bc.b])
        ld(S, X1T, X1T.t[:], x1Td.rearrange("(k p) t -> p k t", p=128)[:, :, r0:r0 + 128])

        def qgroup(gq):
            ps = nxt(S, PS)
            for bi in range(4):
                blk = gq * 4 + bi
                for kc in range(8):
                    S.op("pe", lambda e, bi=bi, blk=blk, kc=kc: e.matmul(ps.t[:, bi * 128:(bi + 1) * 128], WQ.t[:, kc, blk * 128:(blk + 1) * 128],
                                                                         X1T.t[:, kc, :], start=(kc == 0), stop=(kc == 7)),
                         reads=[WQ.b, X1T.b], writes=[ps.b])
            S.op("act", lambda e: e.activation(QT.t[:, gq * 4:(gq + 1) * 4, :].rearrange("p a b -> p (a b)"), ps.t[:], AF.Copy),
                 reads=[ps.b], writes=[QT.b])
        for gq in range(4):
            qgroup(gq)
            yield

        def sgroup(gq):
            ps = nxt(S, PS)
            for bi in range(4):
                blk = gq * 4 + bi
                S.op("pe", lambda e, bi=bi, blk=blk: e.matmul(ps.t[:, bi * 128:(bi + 1) * 128], QT.t[:, blk, :], SK.t[:, blk, :],
                                                              start=True, stop=True), reads=[QT.b, SK.b], writes=[ps.b])
            S.op("act", lambda e: e.activation(SC.t[:, gq * 4:(gq + 1) * 4, :].rearrange("p a b -> p (a b)"), ps.t[:], AF.Copy),
                 reads=[ps.b], writes=[SC.b])
        for gq in range(4):
            sgroup(gq)
            yield

        def top16(blk):
            S.op("dve", lambda e: e.max(TS.t[:, blk, 0:8], SC.t[:, blk, :]), reads=[SC.b], writes=[TS.b])
            S.op("dve", lambda e: e.max_index(TI.t[:, blk, 0:8], TS.t[:, blk, 0:8], SC.t[:, blk, :]), reads=[SC.b, TS.b], writes=[TI.b])
            S.op("dve", lambda e: e.match_replace(SC2.t[:], TS.t[:, blk, 0:8], SC.t[:, blk, :], -1e30), reads=[SC.b, TS.b], writes=[SC2.b])
            S.op("dve", lambda e: e.max(TS.t[:, blk, 8:16], SC2.t[:]), reads=[SC2.b], writes=[TS.b])
            S.op("dve", lambda e: e.max_index(TI.t[:, blk, 8:16], TS.t[:, blk, 8:16], SC2.t[:]), reads=[SC2.b, TS.b], writes=[TI.b])
        for blk in range(16):
            top16(blk)
            yield
        S.op("dve", lambda e: e.tensor_copy(TIF.t[:], TI.t[:]), reads=[TI.b], writes=[TIF.b])
        S.op("dve", lambda e: e.tensor_scalar(TI128.t[:], TIF.t[:], 128.0, None, ALU.mult), reads=[TIF.b], writes=[TI128.b])

        def head(h):
            for a in range(16):
                S.op("dve", lambda e, a=a: e.tensor_scalar(CS.t[:, a * 16:(a + 1) * 16], TS.t[:, 2 * h + 1, :], TS.t[:, 2 * h, a:a + 1], None, ALU.add),
                     reads=[TS.b], writes=[CS.b], soft=((CS.b,) if a > 0 else ()))
                S.op("dve", lambda e, a=a: e.tensor_scalar(CI.t[:, a * 16:(a + 1) * 16], TIF.t[:, 2 * h + 1, :], TI128.t[:, 2 * h, a:a + 1], None, ALU.add),
                     reads=[TIF.b, TI128.b], writes=[CI.b], soft=((CI.b,) if a > 0 else ()))
            S.op("dve", lambda e: e.max(BS.t[:, h, 0:8], CS.t[:]), reads=[CS.b], writes=[BS.b])
            S.op("dve", lambda e: e.max_index(BPU.t[:, 0:8], BS.t[:, h, 0:8], CS.t[:]), reads=[CS.b, BS.b], writes=[BPU.b])
            S.op("dve", lambda e: e.match_replace(CS2.t[:], BS.t[:, h, 0:8], CS.t[:], -1e30), reads=[CS.b, BS.b], writes=[CS2.b])
            S.op("dve", lambda e: e.max(BS.t[:, h, 8:16], CS2.t[:]), reads=[CS2.b], writes=[BS.b])
            S.op("dve", lambda e: e.max_index(BPU.t[:, 8:16], BS.t[:, h, 8:16], CS2.t[:]), reads=[CS2.b, BS.b], writes=[BPU.b])
            S.op("dve", lambda e: e.tensor_copy(BPF.t[:], BPU.t[:]), reads=[BPU.b], writes=[BPF.b])
            for k in range(16):
                def pick(k=k):
                    jk = JKs[cnt["jk"] % 2]; cnt["jk"] += 1
                    S.op("dve", lambda e: e.scalar_tensor_tensor(jk.t[:], IOT.t[:], BPF.t[:, k:k + 1], CI.t[:], ALU.is_equal, ALU.mult,
                                                                 accum_out=IDF.t[:, h * 16 + k:h * 16 + k + 1]),
                         reads=[IOT.b, BPF.b, CI.b], writes=[jk.b, IDF.b], soft=((IDF.b,) if (h > 0 or k > 0) else ()))
                pick()
            S.op("dve", lambda e: e.tensor_scalar(NM.t[:, h:h + 1], BS.t[:, h, 0:1], -1.0, None, ALU.mult), reads=[BS.b], writes=[NM.b])
            S.op("act", lambda e: e.activation(EX.t[:, h, :], BS.t[:, h, :], AF.Exp, bias=NM.t[:, h:h + 1], accum_out=SM.t[:, h:h + 1]),
                 reads=[BS.b, NM.b], writes=[EX.b, SM.b])
        for h in range(8):
            head(h)
            yield
        S.op("dve", lambda e: e.tensor_copy(IDXc.t[:], IDF.t[:]), reads=[IDF.b], writes=[IDXc.b])
        S.op("dve", lambda e: e.reciprocal(SM.t[:], SM.t[:]), reads=[SM.b], writes=[SM.b])
        for h in range(8):
            S.op("dve", lambda e, h=h: e.tensor_scalar(GWc.t[:, h * 16:(h + 1) * 16], EX.t[:, h, :], SM.t[:, h:h + 1], None, ALU.mult),
                 reads=[EX.b, SM.b], writes=[GWc.b])

    def back(ti, fg):
        r0 = ti * 128
        X1c, X1Bc, IDXc, GWc = X1[ti % 2], X1B[ti % 2], IDX[ti % 2], GW[ti % 2]

        def group(gi):
            bufs = UV[gi % 2]
            for k in range(8):
                def g1(k=k):
                    sl_ = gi * 8 + k
                    ub = bufs[k]
                    S.dma("pool", None, None, ub.semq("pool"), reads=[IDXc.b], writes=[ub.b],
                          fn=lambda e: e.indirect_dma_start(out=ub.t[:], out_offset=None, in_=uvd,
                                                            in_offset=bass.IndirectOffsetOnAxis(ap=IDXc.t[:, sl_:sl_ + 1], axis=0)))
                g1()
            for k in range(8):
                def d1(k=k):
                    sl_ = gi * 8 + k
                    ub = bufs[k]
                    jk2 = JK2s[cnt["jk2"] % 2]; cnt["jk2"] += 1
                    S.op("dve", lambda e: e.scalar_tensor_tensor(jk2.t[:], ub.t[:, 0:1024], 1.0, X1Bc.t[:], ALU.mult, ALU.mult,
                                                                 accum_out=H.t[:, sl_:sl_ + 1]), reads=[ub.b, X1Bc.b], writes=[jk2.b, H.b],
                         soft=((H.b,) if k > 0 else ()))
                d1()
            gs = slice(gi * 8, gi * 8 + 8)
            S.op("act", lambda e: e.activation(HG.t[:, gs], H.t[:, gs], AF.Gelu), reads=[H.b], writes=[HG.b])
            S.op("dve", lambda e: e.tensor_tensor(HG.t[:, gs], HG.t[:, gs], GWc.t[:, gs], ALU.mult), reads=[HG.b, GWc.b], writes=[HG.b])
            for k in range(8):
                def v1(k=k):
                    sl_ = gi * 8 + k
                    ub = bufs[k]
                    dg = DG[cnt["dg"] % 4]; cnt["dg"] += 1
                    S.op("act", lambda e: e.activation(dg.t[:], IDB.t[:], AF.Copy, scale=HG.t[:, sl_:sl_ + 1]), reads=[IDB.b, HG.b], writes=[dg.b])
                    for hf in range(2):
                        S.op("pe", lambda e, hf=hf: e.matmul(ACCP[hf].t[:], dg.t[:], ub.t[:, 1024 + hf * 512:1024 + (hf + 1) * 512],
                                                             start=(sl_ == 0), stop=(sl_ == 127)), reads=[dg.b, ub.b], writes=[ACCP[hf].b])
                v1()
        for gi in range(16):
            group(gi)
            next(fg, None); next(fg, None)
        for hf in range(2):
            S.op("dve", lambda e, hf=hf: e.scalar_tensor_tensor(Z.t[:, hf * 512:(hf + 1) * 512], X1c.t[:, hf * 512:(hf + 1) * 512], DN_ALPHA,
                                                                ACCP[hf].t[:], ALU.mult, ALU.add), reads=[X1c.b, ACCP[hf].b], writes=[Z.b])
        layer_norm_tile(S, Z, OUT, LNG, LNB, stat, mv, rstd)
        st(S, OUT, od[r0:r0 + 128, :], OUT.t[:], final=True)
    fg = front(0)
    for _ in fg:
        pass
    for ti in range(ntile):
        nfg = front(ti + 1) if ti + 1 < ntile else iter(())
        back(ti, nfg)
        for _ in nfg:
            pass


def phase23(S, PS, D):
    g2 = phase2(S, PS[0:4], D)
    g3 = phase3(S, PS[4:8], D)
    import os
    mode = os.environ.get("MK_MODE", "seq")
    if mode == "seq":
        for _ in g2:
            pass
        for _ in g3:
            pass
        return
    done2 = done3 = False
    while not (done2 and done3):
        for _ in range(2):
            if not done2:
                try:
                    next(g2)
                except StopIteration:
                    done2 = True
        if not done3:
            try:
                next(g3)
            except StopIteration:
                done3 = True


def build_fused(nph=4):
    nc = bass.Bass("TRN2", target_bir_lowering=False)
    D = {}
    for n, shp in (("xT", [1024, SEQ]), ("xTo", [1024, NTOK]), ("xo", [NTOK, 1024]), ("Wg", [2, 1024, 1796]), ("pcol", [2, 128, 19]),
                   ("w2a2", [2, 128, 256]), ("g2", [2, 128, 256]), ("bd", [128, 128]), ("cm", [128, 512]), ("idn", [128, 128]),
                   ("msel", [128, 2]), ("neg", [8, 128, 512]), ("mk", [5, 64, 512]), ("lg3", [8, 64, 64]), ("lb3", [8, 64, 64]),
                   ("wg", [1024, 2048]), ("pa", [512, 1024]), ("pb", [512, 1024]), ("wo", [1024, 1024]), ("lg1", [128, 1024]),
                   ("lb1", [128, 1024]), ("wq", [1024, 2048]), ("sk", [128, 2048]), ("u", [16384, 1024]), ("v", [16384, 1024]),
                   ("lg2", [128, 1024]), ("lb2", [128, 1024]), ("iota", [128, 256])):
        D[n] = din(nc, n, shp)
    D["out"] = dout(nc, "out", [NTOK, 1024])
    for n, shp in (("fc", [8, SEQ]),
                   ("al", [512, SEQ]), ("be", [512, SEQ]), ("ka", [512, SEQ]), ("rh", [512, SEQ]), ("gc", [512, SEQ // 64]),
                   ("nbe_tm", [SEQ, 512]), ("ka_tm", [SEQ, 512]), ("rv_tm", [SEQ, 512]), ("rg_tm", [SEQ, 512]), ("bo_tm", [SEQ, 512]),
                   ("x1", [NTOK, 1024]), ("x1T", [1024, NTOK])):
        D[n] = dscr(nc, "s_" + n, shp)
    for n, shp in (("fqo", [512, NTOK]), ("fk", [512, SEQ]), ("fvtm", [SEQ, 512]), ("fnc3", [8, 3, SEQ]), ("co3", [8, 3, NTOK]), ("yfo", [512, NTOK]), ("yr", [512, SEQ]),
                   ("uvb", [16384, 2048])):
        D[n] = dscr(nc, "s_" + n, shp, BF16)
    S = Sched(nc)
    PS = mk_psum(S)
    phases = (phase1, phase23, phase4, phase5)[:nph]
    for i, ph in enumerate(phases):
        S.phase_begin()
        ph(S, PS, D)
        S.phase_end(final=(i == len(phases) - 1))
    S.stack.close()
    return nc, S


def core_inputs(x, P, b, t):
    c_ = np.ascontiguousarray
    xb = x[b]
    own = xb.reshape(16, 512, 1024)[t::2].reshape(NTOK, 1024)
    bc = lambda v: c_(np.broadcast_to(v[None, :], (128, v.shape[0])))
    RB = 1544
    mu = P["rwkv_mu"]
    Wg = []; pcols = []; w2a2 = []; g2 = []
    two = lambda v: v.reshape(2, 128).T
    for g in range(2):
        ch = slice(256 * g, 256 * g + 256)
        cols = np.concatenate([
            np.arange(256 * g, 256 * g + 256), 512 + np.arange(256 * g, 256 * g + 256), 1024 + np.arange(256 * g, 256 * g + 256),
            RB + np.arange(256 * g, 256 * g + 256), RB + 512 + np.arange(256 * g, 256 * g + 256),
            RB + 1024 + np.arange(256 * g, 256 * g + 256), RB + np.arange(1536, 1792), 1536 + np.arange(4 * g, 4 * g + 4)])
        Wg.append(P["w_in"][:, cols])
        pc = np.zeros((128, 19), np.float32)
        pc[:, 0:2] = two(mu[0:512][ch]); pc[:, 2:4] = two(mu[512:1024][ch]); pc[:, 4:6] = two(mu[1024:1536][ch])
        pc[:, 6] = mu[1536:1664]; pc[:, 7] = mu[1664:1792]
        pc[:, 8:10] = two(P["rwkv_w0"][ch]); pc[:, 10:12] = two(P["rwkv_a0"][ch])
        pc[:, 12:14] = two(P["rwkv_k_k"][ch]); pc[:, 14:16] = two(P["rwkv_k_a"][ch]); pc[:, 16:18] = two(P["rwkv_r_k"][ch])
        pc[0:4, 18] = P["fox_f_bias"][4 * g:4 * g + 4]
        pcols.append(pc)
        w2a2.append(np.concatenate([P["rwkv_w2"][:, ch], P["rwkv_a2"][:, ch]], 0))
        g2.append(P["rwkv_g2"][:, ch])
    bd = np.kron(np.eye(2, dtype=np.float32), np.ones((64, 64), np.float32))
    cm = np.ones((128, 512), np.float32); cm[:, ::64] = 0.0
    msel = np.zeros((128, 2), np.float32); msel[:, t] = 1.0
    kpos = (128 * np.arange(8)[:, None, None] + np.arange(128)[None, :, None])
    qpos = 512 * t + np.arange(512)[None, None, :]
    neg = np.where(kpos <= qpos, 0.0, -30000.0).astype(np.float32)
    one = np.ones((64, 64), np.float32)
    rep = lambda m: np.tile(m, (1, 8))
    mk = np.stack([rep(np.tril(one, -1)), rep(np.triu(one, 1)), rep(np.triu(one)), rep(-np.triu(one)), rep(np.eye(64, dtype=np.float32))])
    lg3 = np.stack([np.broadcast_to(P["rwkv_ln_g"][h * 64:(h + 1) * 64][None, :], (64, 64)) for h in range(8)])
    lb3 = np.stack([np.broadcast_to(P["rwkv_ln_b"][h * 64:(h + 1) * 64][None, :], (64, 64)) for h in range(8)])
    sk = P["peer_sub_keys"].reshape(16, 128, 128).transpose(2, 0, 1).reshape(128, 16 * 128)
    iota = np.broadcast_to(np.arange(256, dtype=np.float32)[None, :], (128, 256))
    m = {"xT": xb.T, "xTo": own.T, "xo": own, "Wg": np.stack(Wg), "pcol": np.stack(pcols), "w2a2": np.stack(w2a2), "g2": np.stack(g2),
         "bd": bd, "cm": cm, "idn": np.eye(128, dtype=np.float32), "msel": msel, "neg": neg, "mk": mk, "lg3": lg3, "lb3": lb3,
         "wg": P["w_in"][:, 3336:], "pa": P["p_fox"], "pb": P["p_rwkv"], "wo": P["w_o"], "lg1": bc(P["ln1_g"]), "lb1": bc(P["ln1_b"]),
         "wq": P["peer_w_q"], "sk": sk, "u": P["peer_u"], "v": P["peer_v"], "lg2": bc(P["ln2_g"]), "lb2": bc(P["ln2_b"]), "iota": iota}
    return {k: c_(np.asarray(v, np.float32)) for k, v in m.items()}


def kernel(**inputs):
    x = np.asarray(inputs["x"], np.float32)
    P = {k: np.asarray(v, np.float32)[0] for k, v in inputs.items() if k != "x"}
    B = x.shape[0]
    nc, _ = build_fused()
    in_maps = [core_inputs(x, P, b, t) for b in range(B) for t in range(2)]
    res = run_bass_kernel_spmd(nc, in_maps, core_ids=list(range(2 * B)))
    out = np.empty_like(x)
    for b in range(B):
        ob = out[b].reshape(16, 512, 1024)
        for t in range(2):
            ob[t::2] = res.results[2 * b + t]["out"].reshape(8, 512, 1024)
    return out
```
